# Optimizing a Trainium2 kernel written in Bass

```python
import math
import jax, jax.numpy as jnp
from jax import lax
import numpy as np

D_MODEL = 2048
BATCH = 2
SEQ = 4096
DEPTH = 1

GRID_W = 64
HEAD_DIM = 128
N_Q_HEADS = 8
N_KV_HEADS = 2
Q_PER_KV = N_Q_HEADS // N_KV_HEADS
ATTN_W = N_Q_HEADS * HEAD_DIM
KV_W = N_KV_HEADS * HEAD_DIM
ROPE_THETA = 10000.0
Q_BLOCK = 128
HYENA_W = D_MODEL - ATTN_W
HYENA_GROUPS = 8
HYENA_ORDER = 2
SHORT_TAPS = 3
FILTER_EMB = 33
FILTER_HIDDEN = 64
DECAY_TARGET = 1e-2
FAST_DECAY_PCT = 0.3
SLOW_DECAY_PCT = 1.5
IN_W = (HYENA_ORDER + 1) * HYENA_W + ATTN_W + 2 * KV_W + 2 * D_MODEL
D_FF = -(-8 * D_MODEL // (3 * 256)) * 256
EPS = 1e-6

kernel_name = "hybrid_hyena_gqa_gated_encoder"


def rms_norm(x, g):
    xf = x.astype(jnp.float32)
    y = xf * lax.rsqrt(jnp.mean(xf * xf, axis=-1, keepdims=True) + EPS)
    return (y * g.astype(jnp.float32)).astype(x.dtype)


def centred_short_conv(u, w, b):
    L = u.shape[1]
    pad = SHORT_TAPS // 2
    up = jnp.pad(u, ((0, 0), (pad, pad), (0, 0)))
    out = b
    for j in range(SHORT_TAPS):
        out = out + up[:, j:j + L] * w[j]
    return out


def implicit_filter(L, w1, b1, w2, b2, w3, b3, w4, freq):
    f32 = jnp.float32
    pos = jnp.arange(L, dtype=f32)
    t = pos / max(L - 1, 1)
    bands = (FILTER_EMB - 1) // 2
    fb = jnp.linspace(1e-4, bands - 1, bands, dtype=f32)
    ang = (2.0 * math.pi * pos / L)[:, None] * fb[None, :]
    z = jnp.concatenate([t[:, None], jnp.cos(ang), -jnp.sin(ang)], axis=-1)
    fr = freq.astype(f32)
    h = jnp.sin(fr * (z @ w1.astype(f32) + b1.astype(f32)))
    h = jnp.sin(fr * (h @ w2.astype(f32) + b2.astype(f32)))
    h = jnp.sin(fr * (h @ w3.astype(f32) + b3.astype(f32)))
    h = h @ w4.astype(f32)
    max_decay = math.log(DECAY_TARGET) / FAST_DECAY_PCT
    min_decay = math.log(DECAY_TARGET) / SLOW_DECAY_PCT
    deltas = jnp.abs(jnp.linspace(min_decay, max_decay, HYENA_W, dtype=f32))
    decay = jnp.exp(-t[:, None] * deltas[None, :])
    h_fwd = h[:, :HYENA_W] * decay
    h_bwd = h[:, HYENA_W:] * decay
    k = jnp.concatenate([h_fwd, jnp.zeros((1, HYENA_W), f32), h_bwd[:0:-1]], axis=0)
    return k / jnp.sum(jnp.abs(k), axis=0, keepdims=True)


def bidir_fft_conv(u, k):
    L = u.shape[1]
    uf = jnp.fft.rfft(u.astype(jnp.float32), n=2 * L, axis=1)
    kf = jnp.fft.rfft(k, n=2 * L, axis=0)
    return jnp.fft.irfft(uf * kf[None], n=2 * L, axis=1)[:, :L]


def axial_rope(x, row, col):
    f32 = jnp.float32
    half = HEAD_DIM // 2
    inv = ROPE_THETA ** (-jnp.arange(0, half, 2, dtype=f32) / half)

    def rot(xs, p):
        ang = p.astype(f32)[:, None] * inv[None, :]
        c = jnp.cos(ang)[None, :, None, :]
        s = jnp.sin(ang)[None, :, None, :]
        a, b = xs[..., :half // 2], xs[..., half // 2:]
        return jnp.concatenate([a * c - b * s, a * s + b * c], axis=-1)

    xf = x.astype(f32)
    return jnp.concatenate([rot(xf[..., :half], row), rot(xf[..., half:], col)], axis=-1).astype(x.dtype)


def block_gqa_attention(q, k, v):
    B, S = q.shape[0], q.shape[1]
    nb = S // Q_BLOCK
    qb = q.reshape(B, nb, Q_BLOCK, N_KV_HEADS, Q_PER_KV, HEAD_DIM).transpose(1, 0, 2, 3, 4, 5)
    scale = HEAD_DIM ** -0.5

    def one_block(qblk):
        s = jnp.einsum('bqkgd,bskd->bkgqs', qblk, k).astype(jnp.float32) * scale
        p = jax.nn.softmax(s, axis=-1).astype(v.dtype)
        return jnp.einsum('bkgqs,bskd->bqkgd', p, v)

    o = lax.map(one_block, qb)
    return o.transpose(1, 0, 2, 3, 4, 5).reshape(B, S, ATTN_W)


def hybrid_layer(x, mix_norm_g, w_in, b_gate, hy_conv_w, hy_conv_b,
                 flt_w1, flt_b1, flt_w2, flt_b2, flt_w3, flt_b3, flt_w4, flt_freq, hy_bias,
                 q_norm_g, k_norm_g, w_br_hyena, w_br_attn, w_out,
                 ffn_norm_g, w_ffn_gate, w_ffn_up, w_ffn_down):
    B, S, _ = x.shape
    rows = S // GRID_W
    h = rms_norm(x, mix_norm_g)
    proj = h @ w_in
    c0 = (HYENA_ORDER + 1) * HYENA_W
    c1 = c0 + ATTN_W
    c2 = c1 + KV_W
    c3 = c2 + KV_W
    u_h, q, k, v, g = jnp.split(proj, [c0, c1, c2, c3], axis=-1)

    u_h = centred_short_conv(u_h, hy_conv_w, hy_conv_b)
    x0, x1, hv = jnp.split(u_h, HYENA_ORDER + 1, axis=-1)
    filt = implicit_filter(S, flt_w1, flt_b1, flt_w2, flt_b2, flt_w3, flt_b3, flt_w4, flt_freq)
    z = hv * x1
    zf = z.astype(jnp.float32)
    z = (bidir_fft_conv(z, filt) + hy_bias.astype(jnp.float32) * zf).astype(x.dtype)
    y_h = z * x0

    q = rms_norm(q.reshape(B, S, N_Q_HEADS, HEAD_DIM), q_norm_g)
    k = rms_norm(k.reshape(B, S, N_KV_HEADS, HEAD_DIM), k_norm_g)
    v = v.reshape(B, S, N_KV_HEADS, HEAD_DIM)
    row = jnp.repeat(jnp.arange(rows, dtype=jnp.int32), GRID_W)
    col = jnp.tile(jnp.arange(GRID_W, dtype=jnp.int32), rows)
    q = axial_rope(q, row, col)
    k = axial_rope(k, row, col)
    y_a = block_gqa_attention(q, k, v)

    g_h, g_a = jnp.split(jax.nn.sigmoid(g + b_gate), 2, axis=-1)
    merged = g_h * (y_h @ w_br_hyena) + g_a * (y_a @ w_br_attn)
    x = x + merged @ w_out

    h2 = rms_norm(x, ffn_norm_g)
    return x + (jax.nn.silu(h2 @ w_ffn_gate) * (h2 @ w_ffn_up)) @ w_ffn_down


def setup_inputs(seed: int = 0) -> dict:
    key = jax.random.key(seed)
    ks = jax.random.split(key, 32)
    f32 = jnp.float32

    def dense(k, shape, fan_in):
        return jax.random.normal(k, (DEPTH,) + shape, f32) * (fan_in ** -0.5)

    def gain(k, n):
        return 1.0 + 0.05 * jax.random.normal(k, (DEPTH, n), f32)

    def small(k, shape):
        return 0.01 * jax.random.normal(k, (DEPTH,) + shape, f32)

    return {
        "x": jax.random.normal(ks[0], (BATCH, SEQ, D_MODEL), f32),
        "mix_norm_g": gain(ks[1], D_MODEL),
        "w_in": dense(ks[2], (D_MODEL, IN_W), D_MODEL),
        "b_gate": small(ks[3], (2 * D_MODEL,)),
        "hy_conv_w": dense(ks[4], (SHORT_TAPS, (HYENA_ORDER + 1) * HYENA_W), SHORT_TAPS),
        "hy_conv_b": small(ks[5], ((HYENA_ORDER + 1) * HYENA_W,)),
        "flt_w1": dense(ks[6], (FILTER_EMB, FILTER_HIDDEN), FILTER_EMB),
        "flt_b1": small(ks[7], (FILTER_HIDDEN,)),
        "flt_w2": dense(ks[8], (FILTER_HIDDEN, FILTER_HIDDEN), FILTER_HIDDEN),
        "flt_b2": small(ks[9], (FILTER_HIDDEN,)),
        "flt_w3": dense(ks[10], (FILTER_HIDDEN, FILTER_HIDDEN), FILTER_HIDDEN),
        "flt_b3": small(ks[11], (FILTER_HIDDEN,)),
        "flt_w4": dense(ks[12], (FILTER_HIDDEN, 2 * HYENA_W), FILTER_HIDDEN),
        "flt_freq": gain(ks[13], FILTER_HIDDEN),
        "hy_bias": jax.random.normal(ks[14], (DEPTH, HYENA_W), f32),
        "q_norm_g": gain(ks[15], HEAD_DIM),
        "k_norm_g": gain(ks[16], HEAD_DIM),
        "w_br_hyena": dense(ks[17], (HYENA_W, D_MODEL), HYENA_W),
        "w_br_attn": dense(ks[18], (ATTN_W, D_MODEL), ATTN_W),
        "w_out": dense(ks[19], (D_MODEL, D_MODEL), D_MODEL),
        "ffn_norm_g": gain(ks[20], D_MODEL),
        "w_ffn_gate": dense(ks[21], (D_MODEL, D_FF), D_MODEL),
        "w_ffn_up": dense(ks[22], (D_MODEL, D_FF), D_MODEL),
        "w_ffn_down": dense(ks[23], (D_FF, D_MODEL), D_FF),
    }


def reference(x, mix_norm_g, w_in, b_gate, hy_conv_w, hy_conv_b,
              flt_w1, flt_b1, flt_w2, flt_b2, flt_w3, flt_b3, flt_w4, flt_freq, hy_bias,
              q_norm_g, k_norm_g, w_br_hyena, w_br_attn, w_out,
              ffn_norm_g, w_ffn_gate, w_ffn_up, w_ffn_down):
    for l in range(DEPTH):
        x = hybrid_layer(x, mix_norm_g[l], w_in[l], b_gate[l], hy_conv_w[l], hy_conv_b[l],
                         flt_w1[l], flt_b1[l], flt_w2[l], flt_b2[l], flt_w3[l], flt_b3[l],
                         flt_w4[l], flt_freq[l], hy_bias[l],
                         q_norm_g[l], k_norm_g[l], w_br_hyena[l], w_br_attn[l], w_out[l],
                         ffn_norm_g[l], w_ffn_gate[l], w_ffn_up[l], w_ffn_down[l])
    return x
```

```python
import math
import numpy as np
import ml_dtypes
import concourse.bass as bass
import concourse.mybir as mybir
from concourse.bass_utils import run_bass_kernel_spmd

F32 = mybir.dt.float32
BF16 = mybir.dt.bfloat16
AF = mybir.ActivationFunctionType
ALU = mybir.AluOpType

D = 2048
S = 4096
DFF = 5632
EPS = 1e-6
NFF = DFF // 128
ENGS = ["pe", "act", "dve", "pool", "sp"]
_DEBUG = {"stop": None}
_RID = {}
_SHAPES = {}


class _Op:
    __slots__ = ("eng", "emit", "deps", "dma_deps", "is_dma", "semkey", "signaled", "sig_idx", "inc")


class Sched:
    def __init__(self):
        self.ops = {e: [] for e in ENGS}
        self.lastw = {}
        self.readers = {}
        self.dma_cnt = {}

    def add(self, eng, emit, reads=(), writes=(), dma=None, inc=16):
        op = _Op()
        op.inc = inc
        op.eng = eng
        op.emit = emit
        op.is_dma = dma is not None
        op.semkey = dma
        op.signaled = False
        op.sig_idx = 0
        deps = {}
        same_raw = set()
        for r in reads:
            w = self.lastw.get(r)
            if w is not None:
                deps[id(w)] = w
                if w.eng == eng and eng != "pe" and not w.is_dma:
                    same_raw.add(id(w))
        for wn in writes:
            w = self.lastw.get(wn)
            if w is not None:
                deps[id(w)] = w
            for rd in self.readers.get(wn, ()):
                deps[id(rd)] = rd
        for r in reads:
            self.readers.setdefault(r, []).append(op)
        for wn in writes:
            self.lastw[wn] = op
            self.readers[wn] = []
        op.deps = []
        op.dma_deps = {}
        for d in deps.values():
            if d is op:
                continue
            if d.is_dma:
                op.dma_deps[d.semkey] = self.dma_cnt[d.semkey]
            elif d.eng != eng or op.is_dma or id(d) in same_raw or eng != "pe":
                op.deps.append(d)
                d.signaled = True
        if op.is_dma:
            self.dma_cnt[dma] = self.dma_cnt.get(dma, 0) + inc
        self.ops[eng].append(op)
        return op

    def barrier(self):
        lasts = {}
        for e in ENGS:
            for op in reversed(self.ops[e]):
                if not op.is_dma and op.emit is not None:
                    lasts[e] = op
                    break
        for e in ENGS:
            op = _Op()
            op.inc = 0
            op.eng = e
            op.emit = None
            op.is_dma = False
            op.semkey = None
            op.signaled = False
            op.sig_idx = 0
            op.deps = []
            for e2, l in lasts.items():
                if e2 != e:
                    op.deps.append(l)
                    l.signaled = True
            op.dma_deps = dict(self.dma_cnt)
            self.ops[e].append(op)

    def emit_all(self, block, eng_sems, dma_sems):
        for e in ENGS:
            c = 0
            for op in self.ops[e]:
                if op.signaled and not op.is_dma:
                    c += 1
                    op.sig_idx = c

        def make(engname):
            def fn(e):
                waited = {}
                if engname == "sp":
                    _RID["v"] = e.snap(e.partition_id() % 4, min_val=0, max_val=3)
                for op in self.ops[engname]:
                    for d in op.deps:
                        key = ("e", d.eng)
                        if waited.get(key, 0) < d.sig_idx:
                            e.wait_ge(eng_sems[d.eng], d.sig_idx)
                            waited[key] = d.sig_idx
                    for k, v in op.dma_deps.items():
                        key = ("d", k)
                        if waited.get(key, 0) < v:
                            e.wait_ge(dma_sems[k], v)
                            waited[key] = v
                    if op.emit is None:
                        continue
                    ins = op.emit(e)
                    if op.is_dma:
                        ins.then_inc(dma_sems[op.semkey], op.inc)
                    elif op.signaled:
                        ins.then_inc(eng_sems[engname], 1)
            return fn

        block.tensor(make("pe"))
        block.scalar(make("act"))
        block.vector(make("dve"))
        block.gpsimd(make("pool"))
        block.sync(make("sp"))


class Arena:
    def __init__(self, nc, nbytes):
        self.h32 = nc.alloc_sbuf_tensor("arena", [128, nbytes // 4], F32)
        self.h16 = self.h32.bitcast(BF16)
        self.nbytes = nbytes
        self.off = 0

    def alloc(self, nbytes):
        nbytes = (nbytes + 63) // 64 * 64
        o = self.off
        self.off += nbytes
        assert self.off <= self.nbytes, ("SBUF arena overflow", self.off, self.nbytes)
        return o

    def f32(self, n):
        o = self.alloc(4 * n)
        return self.h32[:, o // 4:o // 4 + n]

    def bf16(self, n):
        o = self.alloc(2 * n)
        return self.h16[:, o // 2:o // 2 + n]


def _ap(t, extra_off, dims):
    return bass.AP(t.tensor, t.offset + extra_off, dims)


def build_program():
    nc = bass.Bass("TRN2", target_bir_lowering=False)
    s = Sched()
    dbg = _DEBUG["stop"]
    ORDER = ["p1", "p3", "p2a", "p2f", "p2", "p4b", "p4", "p5"]

    def upto(name):
        return dbg is None or ORDER.index(name) <= ORDER.index(dbg)

    def din(name, shape, dt=F32):
        need = {"wgate": "p4b", "wbh": "p4b", "wba": "p4b", "wout": "p4", "wfg": "p5", "wfu": "p5", "wfd": "p5",
                "dftF": "p2f", "dftI": "p2"}
        if name in need and not upto(need[name]):
            shape = [32, 128, 128] if len(shape) == 3 else [128, 128]
        _SHAPES[name] = tuple(shape)
        return nc.dram_tensor(name, list(shape), dt, kind="ExternalInput").ap()

    x_d = din("x", [S, D])
    xo_d = din("xo", [1024, D])
    wmix_d = din("wmix", [D, 1280])
    wgate_d = din("wgate", [D, 4096])
    bgate_d = din("bgate", [128, 32])
    g1_d = din("g1", [D])
    convw_d = din("convw", [128, 18])
    convb_d = din("convb", [128, 6])
    fw1_d = din("fw1", [33, 64])
    fw2_d = din("fw2", [64, 64])
    fw3_d = din("fw3", [64, 64])
    fw4_d = din("fw4", [64, 512])
    fvec_d = din("fvec", [64, 4])
    hyb_d = din("hybias", [128, 2])
    gqk_d = din("gqk", [128, 2])
    wbh_d = din("wbh", [1024, D])
    wba_d = din("wba", [1024, D])
    wout_d = din("wout", [D, D])
    g2_d = din("g2", [D])
    wfg_d = din("wfg", [D, DFF])
    wfu_d = din("wfu", [D, DFF])
    wfd_d = din("wfd", [DFF, D])
    identb_d = din("identb", [128, 128], BF16)
    identf_d = din("identf", [128, 128])
    onesdiv_d = din("onesdiv", [128, 128], BF16)
    perm_d = din("perm", [128, 128], BF16)
    ropec_d = din("ropec", [128, 64])
    ropes_d = din("ropes", [128, 64])
    zemb_d = din("zemb", [33, S])
    decay_d = din("decay", [S, 256])
    dftF_d = din("dftF", [32, 128, 2 * 32 * 128], BF16)
    dftI_d = din("dftI", [32, 128, 2 * 32 * 128], BF16)

    out_d = nc.dram_tensor("out", [1024, D], F32, kind="ExternalOutput").ap()
    yloc_d = nc.dram_tensor("yloc", [512, S], BF16, kind="Internal").ap()
    cloc_d = nc.dram_tensor("cloc", [128, S], BF16, kind="Internal").ap()
    cgat_d = nc.dram_tensor("cgat", [4 * 128, S], BF16, kind="Internal").ap()
    ystage_d = nc.dram_tensor("ystage", [4, 128, 4 * 1024], BF16, kind="Internal").ap()
    x1_d = nc.dram_tensor("x1s", [1024, D], F32, kind="Internal").ap()
    dbg_d = None
    if dbg is not None:
        dbg_d = nc.dram_tensor("dbg", [128, 8192], F32, kind="ExternalOutput").ap()

    A = Arena(nc, 207 * 1024)
    PS = [nc.alloc_psum_tensor("psb%d" % i, [128, 512], F32) for i in range(8)]
    PSB = [p.bitcast(BF16) for p in PS]

    identb = A.bf16(128)
    identf = A.f32(128)
    onesdiv = A.bf16(128)
    perm = A.bf16(128)
    ropec = A.f32(64)
    ropes = A.f32(64)
    gqk = A.f32(2)
    convw = A.f32(18)
    convb = A.f32(6)
    hyb = A.f32(2)
    bgate = A.f32(32)
    fvec = A.f32(4)
    stats = A.f32(64)
    negpi = A.f32(1)
    onescol = A.f32(1)
    epst = A.f32(1)
    small_loads = [(identb, identb_d), (identf, identf_d), (onesdiv, onesdiv_d), (perm, perm_d),
                   (ropec, ropec_d), (ropes, ropes_d), (gqk, gqk_d), (convw, convw_d), (convb, convb_d),
                   (hyb, hyb_d), (bgate, bgate_d)]
    for i, (dst, src) in enumerate(small_loads):
        s.add("sp", lambda e, dst=dst, src=src: e.dma_start(out=dst, in_=src), writes=["const%d" % i], dma="const")
    s.add("sp", lambda e: e.dma_start(out=fvec[0:64, :], in_=fvec_d), writes=["fvec"], dma="const")
    s.add("dve", lambda e: e.memset(negpi, -math.pi), writes=["negpi"])
    s.add("dve", lambda e: e.memset(onescol, 1.0), writes=["onescol"])
    s.add("dve", lambda e: e.memset(epst, EPS), writes=["epst"])
    base_mark = A.off

    def dump(ap_sb, ncols, name):
        tmp = A.f32(ncols)
        s.add("dve", lambda e: e.tensor_copy(tmp, ap_sb), reads=[name], writes=["dbgtmp"])
        s.add("sp", lambda e: e.dma_start(out=dbg_d[:, 0:ncols], in_=tmp), reads=["dbgtmp"], writes=["dbgout"], dma="dbg")

    def mm_group(bank_ap, lhs_fn, rhs_fn, nk, reads, bankname):
        def f(e):
            ins = None
            for k in range(nk):
                ins = e.matmul(bank_ap, lhs_fn(k), rhs_fn(k), start=(k == 0), stop=(k == nk - 1))
            return ins
        s.add("pe", f, reads=reads, writes=[bankname])

    cgv = cgat_d.rearrange("(q p) t -> p q t", q=4)

    def xchg_start(g):
        s.add("sp", lambda e: e.dma_start(out=cloc_d, in_=yloc_d[g * 128:(g + 1) * 128, :]),
              reads=["yloc"], writes=["cloc"], dma="cloc")
        s.add("pool", lambda e: e.collective_compute("AllGather", ALU.bypass, replica_groups=[[0, 1, 2, 3], [4, 5, 6, 7]],
                                                     ins=[cloc_d], outs=[cgat_d]),
              reads=["cloc"], writes=["cgat"], dma="cc", inc=1)

    def xchg_finish(g):
        s.add("sp", lambda e: e.dma_start(out=ystage_d[g].rearrange("p (q t) -> p q t", q=4),
                                          in_=cgv[:, :, bass.ds(_RID["v"] * 1024, 1024)]),
              reads=["cgat"], writes=["ystage%d" % g], dma="ystage")

    def transposes_to(src_bf, srcname, hdst3, col0, hname):
        for g4 in range(4):
            bank = g4 % 2
            pst = PSB[bank][:, 0:512].rearrange("p (a t) -> p a t", a=4)

            def tr(e, g4=g4, pst=pst):
                ins = None
                for a in range(4):
                    dk = g4 * 4 + a
                    ins = e.transpose(out=pst[:, a, :], in_=src_bf[:, dk * 128:(dk + 1) * 128], identity=identb)
                return ins
            s.add("pe", tr, reads=[srcname, "const0"], writes=["ps%d" % bank])
            dst = hdst3[:, g4 * 4:(g4 + 1) * 4, col0:col0 + 128]
            if g4 % 2 == 0:
                s.add("act", lambda e, dst=dst, pst=pst: e.activation(out=dst, in_=pst, func=AF.Copy),
                      reads=["ps%d" % bank], writes=[hname + "_a"])
            else:
                s.add("dve", lambda e, dst=dst, pst=pst: e.tensor_copy(dst, pst),
                      reads=["ps%d" % bank], writes=[hname + "_d"])

    def rms_scale(src_f32, srcname, dst_bf, dstname, gtile, gname, statcol, junkbuf):
        sc = stats[:, statcol:statcol + 1]
        sn = "stat%d" % statcol
        s.add("act", lambda e: e.activation(out=junkbuf, in_=src_f32, func=AF.Square, accum_out=sc),
              reads=[srcname], writes=["junk", sn])
        s.add("act", lambda e: e.activation(out=sc, in_=sc, func=AF.Sqrt, bias=epst, scale=1.0 / D), reads=[sn, "epst"], writes=[sn])
        s.add("dve", lambda e: e.reciprocal(sc, sc), reads=[sn], writes=[sn])
        s.add("dve", lambda e: e.scalar_tensor_tensor(out=dst_bf, in0=src_f32, scalar=sc, in1=gtile, op0=ALU.mult, op1=ALU.mult),
              reads=[srcname, sn, gname], writes=[dstname])

    RAW = A.bf16(6 * 4098)
    RAW3 = RAW.rearrange("p (c t) -> p c t", c=6)
    raw_end = A.off
    QT = A.bf16(2 * S)
    KT = A.bf16(S)
    V = A.bf16(32 * 128)
    QT3 = QT.rearrange("p (h t) -> p h t", h=2)
    V3 = V.rearrange("p (c d) -> p c d", c=32)
    p1_mark = A.off

    wm = A.bf16(16 * 1280)
    wm3 = wm.rearrange("p (k n) -> p k n", k=16)
    xt = [A.f32(D) for _ in range(2)]
    xs = [A.bf16(D) for _ in range(2)]
    junk = A.bf16(D)
    hT = [A.bf16(16 * 512) for _ in range(2)]
    hT3 = [h.rearrange("p (k t) -> p k t", k=16) for h in hT]
    gbc = A.f32(D)
    sqs = [A.bf16(512) for _ in range(1)]
    rstdqs = [A.f32(512)] * 3
    qns = [A.bf16(512) for _ in range(3)]
    rawq = [A.f32(512) for _ in range(3)]
    t1 = A.f32(512)
    t2 = A.f32(512)
    print("P1 arena end", A.off, "of", A.nbytes)

    wmix_v = wmix_d.rearrange("(k p) n -> p k n", p=128)
    for k0 in (0, 8):
        s.add("pool", lambda e, k0=k0: e.dma_start(out=wm3[:, k0:k0 + 8, :], in_=wmix_v[:, k0:k0 + 8, :]), writes=["wm"], dma="wm")
    s.add("sp", lambda e: e.dma_start(out=gbc, in_=g1_d.partition_broadcast(128)), writes=["gbc"], dma="gbc")
    s.add("dve", lambda e: e.memset(RAW3[:, :, 0:1], 0.0), writes=["rawpad0"])
    s.add("dve", lambda e: e.memset(RAW3[:, :, 4097:4098], 0.0), writes=["rawpad1"])
    x_v = x_d.rearrange("(n p) d -> n p d", p=128)
    pstep_c = ropec.ap[0][0]
    pstep_s = ropes.ap[0][0]

    def tile_step(jn, i):
        hs_ = jn % 2
        tile = 4 * jn + i
        sl = i % 2
        s.add("sp", lambda e, sl=sl, tile=tile: e.dma_start(out=xt[sl], in_=x_v[tile]), writes=["xt%d" % sl], dma="xt%d" % sl)
        rms_scale(xt[sl], "xt%d" % sl, xs[sl], "xs%d" % sl, gbc, "gbc", tile, junk)
        transposes_to(xs[sl], "xs%d" % sl, hT3[hs_], i * 128, "hT%d" % hs_)

    def chunk_steps(j):
        hs = j % 2
        hname = "hT%d" % hs

        def qk_proj(u):
            bank = 2 + (u % 2)
            col = 768 + u * 128
            mm_group(PS[bank][:, :], lambda k, col=col: wm3[:, k, col:col + 128],
                     lambda k, hs=hs: hT3[hs][:, k, :], 16, [hname + "_a", hname + "_d", "wm"], "ps%d" % bank)
            s.add("act", lambda e, bank=bank, u=u: e.activation(out=rawq[u], in_=PS[bank][:, :], func=AF.Copy),
                  reads=["ps%d" % bank], writes=["rawq%d" % u])

        def hy_proj(cc):
            bank = 2 + (cc % 2)
            mm_group(PS[bank][:, :], lambda k, cc=cc: wm3[:, k, cc * 128:(cc + 1) * 128],
                     lambda k, hs=hs: hT3[hs][:, k, :], 16, [hname + "_a", hname + "_d", "wm"], "ps%d" % bank)
            dst = RAW3[:, cc, 1 + 512 * j:1 + 512 * (j + 1)]
            if cc % 2 == 0:
                s.add("act", lambda e, dst=dst, bank=bank: e.activation(out=dst, in_=PS[bank][:, :], func=AF.Copy),
                      reads=["ps%d" % bank], writes=["raw_a"])
            else:
                s.add("dve", lambda e, dst=dst, bank=bank: e.tensor_copy(dst, PS[bank][:, :]),
                      reads=["ps%d" % bank], writes=["raw_d"])

        def qk_square(u):
            s.add("act", lambda e, u=u: e.activation(out=sqs[0], in_=rawq[u], func=AF.Square),
                  reads=["rawq%d" % u], writes=["sq"])

        def qk_norm(u):
            s.add("pe", lambda e, u=u: e.matmul(PS[4][:, :], onesdiv, sqs[0], start=True, stop=True),
                  reads=["sq", "const2"], writes=["ps4"])
            s.add("act", lambda e, u=u: e.activation(out=rstdqs[u], in_=PS[4][:, :], func=AF.Sqrt, bias=epst, scale=1.0),
                  reads=["ps4", "epst"], writes=["rstdq"])
            s.add("dve", lambda e, u=u: e.reciprocal(rstdqs[u], rstdqs[u]), reads=["rstdq"], writes=["rstdq"])
            gcol = gqk[:, 0:1] if u < 2 else gqk[:, 1:2]
            s.add("dve", lambda e, u=u, gcol=gcol: e.scalar_tensor_tensor(
                out=qns[u], in0=rawq[u], scalar=gcol, in1=rstdqs[u], op0=ALU.mult, op1=ALU.mult),
                reads=["rawq%d" % u, "rstdq", "const6"], writes=["qn%d" % u])

        def qk_rope(u):
            qn_ = qns[u]
            s.add("pe", lambda e, qn_=qn_: e.matmul(PS[5][:, :], perm, qn_, start=True, stop=True),
                  reads=["qn%d" % u, "const3"], writes=["ps5"])
            for half in range(2):
                p0 = half * 64
                if half == 0:
                    cap = _ap(ropec, p0 * pstep_c + 8 * j, [[pstep_c, 64], [1, 8], [0, 64]])
                    sap = _ap(ropes, p0 * pstep_s + 8 * j, [[pstep_s, 64], [1, 8], [0, 64]])
                else:
                    cap = _ap(ropec, p0 * pstep_c, [[pstep_c, 64], [0, 8], [1, 64]])
                    sap = _ap(ropes, p0 * pstep_s, [[pstep_s, 64], [0, 8], [1, 64]])
                qv = qn_[p0:p0 + 64, :].rearrange("p (a b) -> p a b", a=8)
                t1v = t1[p0:p0 + 64, :].rearrange("p (a b) -> p a b", a=8)
                t2v = t2[p0:p0 + 64, :].rearrange("p (a b) -> p a b", a=8)
                pv = PS[5][p0:p0 + 64, :].rearrange("p (a b) -> p a b", a=8)
                s.add("pool", lambda e, t1v=t1v, qv=qv, cap=cap: e.tensor_tensor(out=t1v, in0=qv, in1=cap, op=ALU.mult),
                      reads=["qn%d" % u, "const4"], writes=["t1_%d" % half])
                s.add("dve", lambda e, t2v=t2v, pv=pv, sap=sap: e.tensor_tensor(out=t2v, in0=pv, in1=sap, op=ALU.mult),
                      reads=["ps5", "const5"], writes=["t2_%d" % half])
            dstq = QT3[:, u, 512 * j:512 * (j + 1)] if u < 2 else KT[:, 512 * j:512 * (j + 1)]
            s.add("pool", lambda e, dstq=dstq: e.tensor_tensor(out=dstq, in0=t1, in1=t2, op=ALU.add),
                  reads=["t1_0", "t1_1", "t2_0", "t2_1"], writes=["qk"])
        def v_step():
            def vmm(e, hs=hs):
                ins = None
                for i in range(4):
                    for k in range(16):
                        ins = e.matmul(PS[6][:, i * 128:(i + 1) * 128], hT3[hs][:, k, i * 128:(i + 1) * 128],
                                       wm3[:, k, 1152:1280], start=(k == 0), stop=(k == 15))
                return ins
            s.add("pe", vmm, reads=[hname + "_a", hname + "_d", "wm"], writes=["ps6"])
            s.add("act", lambda e, j=j: e.activation(out=V3[:, 4 * j:4 * j + 4, :],
                                                     in_=PS[6][:, :].rearrange("p (a d) -> p a d", a=4), func=AF.Copy),
                  reads=["ps6"], writes=["v"])
        steps = [lambda: qk_proj(0), lambda: qk_proj(1), lambda: qk_proj(2)]
        for cc in range(3):
            steps.append(lambda cc=cc: (qk_square(cc), hy_proj(cc), qk_norm(cc)))
        for cc in range(3, 6):
            steps.append(lambda cc=cc: (hy_proj(cc), qk_rope(cc - 3)))
        steps.append(v_step)
        return steps

    for i in range(4):
        tile_step(0, i)
    for j in range(8):
        steps = chunk_steps(j)
        nxt = [(j + 1, i) for i in range(4)] if j + 1 < 8 else []
        for idx, st in enumerate(steps):
            st()
            if idx in (1, 3, 5, 7) and nxt:
                tile_step(*nxt.pop(0))
        while nxt:
            tile_step(*nxt.pop(0))
    s.barrier()
    A.off = p1_mark
    if dbg == "p1":
        dump(QT[:, 0:8192], 8192, "qk")
        s.barrier()

    if upto("p3"):
        PT = [A.bf16(512) for _ in range(3)]
        rec = A.f32(512)
        yab = [A.bf16(512) for _ in range(2)]
        ones128 = A.bf16(128)
        s.add("dve", lambda e: e.memset(ones128, 1.0), writes=["ones128"])
        scale = 128.0 ** -0.5
        unit = 0
        for h in range(2):
            for qc in range(8):
                ob = 4 + 2 * (unit % 2)
                def st_mm(sc_):
                    sb = sc_ % 2
                    s.add("pe", lambda e, sb=sb, sc_=sc_, h=h, qc=qc: e.matmul(
                        PS[sb][:, :], KT[:, 128 * sc_:128 * (sc_ + 1)], QT3[:, h, 512 * qc:512 * (qc + 1)], start=True, stop=True),
                        reads=["qk"], writes=["ps%d" % sb])
                    pt = sc_ % 3
                    s.add("act", lambda e, sb=sb, pt=pt: e.activation(out=PT[pt], in_=PS[sb][:, :], func=AF.Exp, scale=scale),
                          reads=["ps%d" % sb], writes=["PT%d" % pt])

                def pv_mm(sc_):
                    pt = sc_ % 3
                    s.add("pe", lambda e, pt=pt, sc_=sc_, ob=ob: e.matmul(PS[ob][:, :], V3[:, sc_, :], PT[pt], start=(sc_ == 0), stop=(sc_ == 31)),
                          reads=["PT%d" % pt, "v"], writes=["ps%d" % ob])
                    s.add("pe", lambda e, pt=pt, sc_=sc_, ob=ob: e.matmul(PS[ob + 1][:, :], ones128, PT[pt], start=(sc_ == 0), stop=(sc_ == 31)),
                          reads=["PT%d" % pt, "ones128"], writes=["ps%d" % (ob + 1)])
                st_mm(0)
                for sc_ in range(32):
                    if sc_ + 1 < 32:
                        st_mm(sc_ + 1)
                    pv_mm(sc_)
                ys = unit % 2
                s.add("dve", lambda e, ob=ob: e.reciprocal(rec, PS[ob + 1][:, :]), reads=["ps%d" % (ob + 1)], writes=["rec"])
                s.add("dve", lambda e, ob=ob, ys=ys: e.tensor_tensor(out=yab[ys], in0=PS[ob][:, :], in1=rec, op=ALU.mult),
                      reads=["ps%d" % ob, "rec"], writes=["yab%d" % ys])
                s.add("sp", lambda e, h=h, qc=qc, ys=ys: e.dma_start(out=yloc_d[256 + h * 128:256 + (h + 1) * 128, 512 * qc:512 * (qc + 1)], in_=yab[ys]),
                      reads=["yab%d" % ys], writes=["yloc"], dma="yab%d" % ys)
                unit += 1
        s.barrier()
        if upto("p4b"):
            xchg_start(2)
        if dbg == "p3":
            s.add("sp", lambda e: e.dma_start(out=xt[0].bitcast(BF16), in_=yloc_d[256:384, :]), reads=["yloc"], writes=["dbgld"], dma="dbgld")
            dump(xt[0].bitcast(BF16)[:, 0:4096], 4096, "dbgld")
            s.barrier()
    A.off = raw_end

    x0T = A.bf16(2 * S)
    zT = A.bf16(2 * S)
    x0T3 = x0T.rearrange("p (c t) -> p c t", c=2)
    zT3 = zT.rearrange("p (c t) -> p c t", c=2)
    p2_mark = A.off
    if upto("p2a"):
        tbuf = [A.f32(S) for _ in range(4)]

        def conv_chain(cc, tb, tbn, eng, dst, dstn):
            w0 = convw[:, cc * 3 + 0:cc * 3 + 1]
            w1 = convw[:, cc * 3 + 1:cc * 3 + 2]
            w2 = convw[:, cc * 3 + 2:cc * 3 + 3]
            bb = convb[:, cc:cc + 1]
            s.add(eng, lambda e: e.tensor_scalar(tb, RAW3[:, cc, 1:4097], w1, bb, op0=ALU.mult, op1=ALU.add),
                  reads=["raw_a", "raw_d", "const7", "const8"], writes=[tbn])
            s.add(eng, lambda e: e.scalar_tensor_tensor(out=tb, in0=RAW3[:, cc, 0:4096], scalar=w0, in1=tb, op0=ALU.mult, op1=ALU.add),
                  reads=["raw_a", "raw_d", "rawpad0", tbn, "const7"], writes=[tbn])
            s.add(eng, lambda e: e.scalar_tensor_tensor(out=dst, in0=RAW3[:, cc, 2:4098], scalar=w2, in1=tb, op0=ALU.mult, op1=ALU.add),
                  reads=["raw_a", "raw_d", "rawpad1", tbn, "const7"], writes=[dstn])
        conv_chain(2, tbuf[0], "tb0", "dve", tbuf[0], "tb0")
        conv_chain(3, tbuf[1], "tb1", "dve", tbuf[1], "tb1")
        conv_chain(4, tbuf[2], "tb2", "dve", tbuf[2], "tb2")
        conv_chain(5, tbuf[3], "tb3", "dve", tbuf[3], "tb3")
        for c2 in range(2):
            s.add("dve", lambda e, c2=c2: e.tensor_tensor(out=zT3[:, c2, :], in0=tbuf[2 + c2], in1=tbuf[c2], op=ALU.mult),
                  reads=["tb%d" % c2, "tb%d" % (2 + c2)], writes=["zT"])
        conv_chain(0, tbuf[0], "tb0", "dve", x0T3[:, 0, :], "x0T")
        conv_chain(1, tbuf[1], "tb1", "dve", x0T3[:, 1, :], "x0T")
        s.barrier()
        if dbg == "p2a":
            A.off = p2_mark
            dump(zT[:, 0:8192], 8192, "zT")
            s.barrier()
    A.off = p2_mark

    if upto("p2f"):
        ZH = RAW[:, 0:32 * 768]
        ZH3 = ZH.rearrange("p (c n) -> p c n", c=32)
        R = A.bf16(32 * 512)
        R3 = R.rearrange("p (c n) -> p c n", c=32)
        dbuf = [A.bf16(2 * 32 * 128) for _ in range(2)]
        dbuf4 = [b.rearrange("p (a c n) -> p a c n", a=2, c=32) for b in dbuf]
        fw1 = A.f32(64)
        fw2 = A.f32(64)
        fw3 = A.f32(64)
        fw4 = A.f32(512)
        zemb = [A.f32(512) for _ in range(2)]
        hA = A.f32(512)
        hB = A.f32(512)
        uu = A.f32(512)
        val = A.f32(512)
        absv = A.f32(256)
        dec = [A.f32(256) for _ in range(2)]
        bsc = A.f32(4)
        scn = A.f32(2)
        sP = A.f32(256)
        sQ = A.f32(256)
        tt = [A.f32(256) for _ in range(4)]
        ysb = A.f32(256)
        ysb3 = ysb.rearrange("p (a c) -> p a c", a=2)
        ub = A.f32(512)
        yhs = [A.bf16(512) for _ in range(2)]
        dt_ = A.f32(4112) if dbg == "p2f" else None

        if upto("p4b"):
            xchg_finish(2)
            xchg_start(3)
        s.add("sp", lambda e: e.dma_start(out=fw1[0:33, :], in_=fw1_d), writes=["fw1"], dma="fw")
        s.add("sp", lambda e: e.dma_start(out=fw2[0:64, :], in_=fw2_d), writes=["fw2"], dma="fw")
        s.add("sp", lambda e: e.dma_start(out=fw3[0:64, :], in_=fw3_d), writes=["fw3"], dma="fw")
        s.add("sp", lambda e: e.dma_start(out=fw4[0:64, :], in_=fw4_d), writes=["fw4"], dma="fw")
        for cc in range(2):
            for g in range(8):
                bank = g % 2
                pst = PSB[bank][:, 0:512].rearrange("p (a t) -> p a t", a=4)

                def trz(e, cc=cc, g=g, pst=pst):
                    ins = None
                    for a in range(4):
                        ch = g * 4 + a
                        ins = e.transpose(out=pst[:, a, :], in_=zT3[:, cc, ch * 128:(ch + 1) * 128], identity=identb)
                    return ins
                s.add("pe", trz, reads=["zT", "const0"], writes=["ps%d" % bank])
                dst = ZH3[:, g * 4:(g + 1) * 4, 256 + cc * 128:256 + (cc + 1) * 128]
                s.add("dve", lambda e, dst=dst, pst=pst: e.tensor_copy(dst, pst), reads=["ps%d" % bank], writes=["ZHz"])
        for l in range(3):
            s.add("dve", lambda e, l=l: e.tensor_tensor(out=bsc[0:64, l:l + 1], in0=fvec[0:64, 0:1], in1=fvec[0:64, l + 1:l + 2], op=ALU.mult),
                  reads=["fvec"], writes=["bsc"])
        s.add("dve", lambda e: e.tensor_scalar(bsc[0:64, 0:3], bsc[0:64, 0:3], 1.0 / (2.0 * math.pi), 16.5, op0=ALU.mult, op1=ALU.add),
              reads=["bsc"], writes=["bsc"])
        s.add("dve", lambda e: e.tensor_scalar(bsc[0:64, 3:4], fvec[0:64, 0:1], 1.0 / (2.0 * math.pi), None, op0=ALU.mult),
              reads=["fvec", "bsc"], writes=["bsc"])
        frp = bsc[0:64, 3:4]
        ki = A.h32.bitcast(mybir.dt.int32)[:, A.alloc(4 * 512) // 4:][:, 0:512]
        kf = A.f32(512)

        def sin_layer(ps_ap, l, dst, rn, wn):
            s.add("dve", lambda e: e.tensor_scalar(uu[0:64, :], ps_ap, frp, bsc[0:64, l:l + 1], op0=ALU.mult, op1=ALU.add),
                  reads=rn + ["bsc"], writes=["uu"])
            s.add("dve", lambda e: e.tensor_copy(ki[0:64, :], uu[0:64, :]), reads=["uu"], writes=["ki"])
            s.add("dve", lambda e: e.tensor_copy(kf[0:64, :], ki[0:64, :]), reads=["ki"], writes=["kf"])
            s.add("dve", lambda e: e.tensor_tensor(out=uu[0:64, :], in0=uu[0:64, :], in1=kf[0:64, :], op=ALU.subtract),
                  reads=["uu", "kf"], writes=["uu"])
            s.add("dve", lambda e: e.scalar_tensor_tensor(out=uu[0:64, :], in0=uu[0:64, :], scalar=0.0, in1=uu[0:64, :], op0=ALU.is_lt, op1=ALU.add),
                  reads=["uu"], writes=["uu"])
            s.add("act", lambda e: e.activation(out=dst, in_=uu[0:64, :], func=AF.Sin, bias=negpi[0:64, :], scale=2.0 * math.pi),
                  reads=["uu", "negpi"], writes=wn)

        decay_v = decay_d.rearrange("(c p) n -> c p n", p=128)
        pstep_d = dec[0].ap[0][0]
        for j in range(8):
            zs = j % 2
            s.add("sp", lambda e, j=j, zs=zs: e.dma_start(out=zemb[zs][0:33, :], in_=zemb_d[:, 512 * j:512 * (j + 1)]),
                  writes=["zemb%d" % zs], dma="zemb%d" % zs)
            s.add("pe", lambda e, zs=zs: e.matmul(PS[2][0:64, :], fw1[0:33, :], zemb[zs][0:33, :], start=True, stop=True),
                  reads=["fw1", "zemb%d" % zs], writes=["ps2"])
            sin_layer(PS[2][0:64, :], 0, hA[0:64, :], ["ps2"], ["hA"])
            if dbg == "p2f" and j == 0:
                s.add("dve", lambda e: e.tensor_copy(dt_[0:64, 2048:2560], hA[0:64, :]), reads=["hA"], writes=["dbgtmp"])
                s.add("dve", lambda e: e.tensor_copy(dt_[64:128, 2048:2560], PS[2][0:64, :]), reads=["hA", "ps2"], writes=["dbgtmp"])
            s.add("pe", lambda e: e.matmul(PS[3][0:64, :], fw2[0:64, :], hA[0:64, :], start=True, stop=True),
                  reads=["fw2", "hA"], writes=["ps3"])
            sin_layer(PS[3][0:64, :], 1, hB[0:64, :], ["ps3"], ["hB"])
            if dbg == "p2f" and j == 0:
                s.add("dve", lambda e: e.tensor_copy(dt_[0:64, 2560:3072], hB[0:64, :]), reads=["hB"], writes=["dbgtmp"])
            s.add("pe", lambda e: e.matmul(PS[2][0:64, :], fw3[0:64, :], hB[0:64, :], start=True, stop=True),
                  reads=["fw3", "hB"], writes=["ps2"])
            if dbg == "p2f" and j == 0:
                s.add("dve", lambda e: e.tensor_copy(dt_[64:128, 2560:3072], PS[2][0:64, :]), reads=["ps2"], writes=["dbgtmp"])
            sin_layer(PS[2][0:64, :], 2, hA[0:64, :], ["ps2"], ["hA"])
            if dbg == "p2f" and j == 0:
                s.add("dve", lambda e: e.tensor_copy(dt_[64:128, 3072:3584], uu[0:64, :]), reads=["uu"], writes=["dbgtmp"])
                s.add("dve", lambda e: e.tensor_copy(dt_[0:64, 3072:3584], hA[0:64, :]), reads=["hA"], writes=["dbgtmp"])
            for i in range(4):
                ch = 4 * j + i
                ds_ = ch % 2
                s.add("sp", lambda e, ch=ch, ds_=ds_: e.dma_start(out=dec[ds_], in_=decay_v[ch]),
                      writes=["dec%d" % ds_], dma="dec%d" % ds_)
                s.add("pe", lambda e, i=i: e.matmul(PS[3][:, :], hA[0:64, i * 128:(i + 1) * 128], fw4[0:64, :], start=True, stop=True),
                      reads=["hA", "fw4"], writes=["ps3"])
                dbc = _ap(dec[ds_], 0, [[pstep_d, 128], [0, 2], [1, 256]])
                s.add("dve", lambda e, dbc=dbc: e.tensor_tensor(out=val.rearrange("p (a c) -> p a c", a=2),
                                                                 in0=PS[3][:, :].rearrange("p (a c) -> p a c", a=2), in1=dbc, op=ALU.mult),
                      reads=["ps3", "dec%d" % ds_], writes=["val"])
                if ch == 0:
                    s.add("dve", lambda e: e.memset(val[0:1, 256:512], 0.0), reads=["val"], writes=["val"])
                s.add("pool", lambda e, ch=ch: e.tensor_tensor(out=ZH3[:, ch, 0:256], in0=val[:, 0:256], in1=val[:, 256:512], op=ALU.add),
                      reads=["val"], writes=["ZHp"])
                s.add("pool", lambda e, ch=ch: e.tensor_tensor(out=ZH3[:, ch, 512:768], in0=val[:, 0:256], in1=val[:, 256:512], op=ALU.subtract),
                      reads=["val"], writes=["ZHm"])
                s.add("act", lambda e: e.activation(out=val, in_=val, func=AF.Abs), reads=["val", "ZHp", "ZHm"], writes=["val"])
                s.add("dve", lambda e: e.tensor_tensor(out=absv, in0=val[:, 0:256], in1=val[:, 256:512], op=ALU.add),
                      reads=["val"], writes=["absv"])
                for c2 in range(2):
                    s.add("pe", lambda e, c2=c2, ch=ch: e.matmul(PS[4 + c2][:, 0:1], absv[:, c2 * 128:(c2 + 1) * 128], onescol,
                                                                  start=(ch == 0), stop=(ch == 31)),
                          reads=["absv", "onescol"], writes=["ps%d" % (4 + c2)])
        for c2 in range(2):
            s.add("dve", lambda e, c2=c2: e.reciprocal(scn[:, c2:c2 + 1], PS[4 + c2][:, 0:1]), reads=["ps%d" % (4 + c2)], writes=["scn"])
        s.add("dve", lambda e: e.tensor_scalar(scn, scn, 2.0 / 8192.0, None, op0=ALU.mult), reads=["scn"], writes=["scn"])

        ZHALL = ["ZHz", "ZHp", "ZHm"]
        for i in range(32):
            sl = i % 2
            dn = "dbuf%d" % sl
            s.add("sp", lambda e, i=i, sl=sl: e.dma_start(out=dbuf[sl], in_=dftF_d[i]), writes=[dn], dma=dn)
            mm_group(PS[0][:, :], lambda k, sl=sl: dbuf4[sl][:, 0, k, :], lambda k: ZH3[:, k, 0:512], 32, [dn] + ZHALL, "ps0")
            mm_group(PS[1][:, :], lambda k, sl=sl: dbuf4[sl][:, 1, k, :], lambda k: ZH3[:, k, 256:768], 32, [dn] + ZHALL, "ps1")
            s.add("act", lambda e: e.activation(out=sP, in_=PS[0][:, 0:256], func=AF.Copy), reads=["ps0"], writes=["sP"])
            s.add("act", lambda e: e.activation(out=sQ, in_=PS[1][:, 256:512], func=AF.Copy), reads=["ps1"], writes=["sQ"])
            s.add("dve", lambda e: e.tensor_tensor(out=tt[0], in0=PS[0][:, 256:512], in1=sP, op=ALU.mult), reads=["ps0", "sP"], writes=["tt0"])
            s.add("dve", lambda e: e.tensor_tensor(out=tt[1], in0=PS[1][:, 0:256], in1=sQ, op=ALU.mult), reads=["ps1", "sQ"], writes=["tt1"])
            s.add("dve", lambda e: e.tensor_tensor(out=tt[2], in0=PS[0][:, 256:512], in1=sQ, op=ALU.mult), reads=["ps0", "sQ"], writes=["tt2"])
            s.add("dve", lambda e: e.tensor_tensor(out=tt[3], in0=PS[1][:, 0:256], in1=sP, op=ALU.mult), reads=["ps1", "sP"], writes=["tt3"])
            s.add("pool", lambda e, i=i: e.tensor_tensor(out=R3[:, i, 0:256], in0=tt[0], in1=tt[1], op=ALU.subtract), reads=["tt0", "tt1"], writes=["R"])
            s.add("pool", lambda e, i=i: e.tensor_tensor(out=R3[:, i, 256:512], in0=tt[2], in1=tt[3], op=ALU.add), reads=["tt2", "tt3"], writes=["R"])
        if dbg == "p2f":
            s.barrier()
            s.add("dve", lambda e: e.tensor_copy(dt_[:, 0:1024].rearrange("p (c n) -> p c n", c=4), ZH3[:, 0:4, 0:256]), reads=["ZHp"], writes=["dbgtmp"])
            s.add("dve", lambda e: e.tensor_copy(dt_[:, 1024:2048].rearrange("p (c n) -> p c n", c=4), ZH3[:, 0:4, 512:768]), reads=["ZHm"], writes=["dbgtmp"])
            s.add("dve", lambda e: e.tensor_copy(dt_[:, 3584:4096], R3[:, 5, :]), reads=["R"], writes=["dbgtmp"])
            s.add("dve", lambda e: e.tensor_copy(dt_[:, 3072:3584], R3[:, 0, :]), reads=["R"], writes=["dbgtmp"])
            s.add("dve", lambda e: e.tensor_copy(dt_[:, 4096:4098], scn), reads=["scn"], writes=["dbgtmp"])
            s.add("sp", lambda e: e.dma_start(out=dbg_d[:, 0:4112], in_=dt_), reads=["dbgtmp"], writes=["dbgout"], dma="dbg")
            s.barrier()
        if upto("p4b"):
            xchg_finish(3)
        ysbs = [ysb, A.f32(256)]
        ysb3s = [y.rearrange("p (a c) -> p a c", a=2) for y in ysbs]

        def inv_issue(tch):
            sl = tch % 2
            dn = "dbuf%d" % sl
            ib = tch % 2
            s.add("sp", lambda e, tch=tch, sl=sl: e.dma_start(out=dbuf[sl], in_=dftI_d[tch]), writes=[dn], dma=dn)

            def inv(e, sl=sl, ib=ib):
                ins = None
                for k in range(32):
                    ins = e.matmul(PS[ib][:, 0:256], dbuf4[sl][:, 0, k, :], R3[:, k, 0:256], start=(k == 0), stop=False)
                    ins = e.matmul(PS[ib][:, 0:256], dbuf4[sl][:, 1, k, :], R3[:, k, 256:512], start=False, stop=(k == 31))
                return ins
            s.add("pe", inv, reads=[dn, "R"], writes=["ps%d" % ib])
            s.add("act", lambda e, ib=ib: e.activation(out=ysbs[ib], in_=PS[ib][:, 0:256], func=AF.Copy), reads=["ps%d" % ib], writes=["ysb%d" % ib])

        def inv_transposes(tch):
            ib = tch % 2
            a = tch % 4
            tb0 = 2 + 2 * ((tch // 4) % 2)
            for c2 in range(2):
                s.add("pe", lambda e, c2=c2, a=a, ib=ib, tb0=tb0: e.transpose(out=PS[tb0 + c2][:, a * 128:(a + 1) * 128], in_=ysb3s[ib][:, c2, :], identity=identf),
                      reads=["ysb%d" % ib, "const1"], writes=["ps%d" % (tb0 + c2)])

        ubs = [ub, A.f32(512)]

        def epilogue(tg):
            tb0 = 2 + 2 * (tg % 2)
            for c2 in range(2):
                tsl = slice(512 * tg, 512 * (tg + 1))
                yb = (tg * 2 + c2) % 2
                ubx = ubs[c2]
                un = "ub%d" % c2
                s.add("pool", lambda e, c2=c2, tsl=tsl, ubx=ubx: e.tensor_scalar(ubx, zT3[:, c2, tsl], hyb[:, c2:c2 + 1], None, op0=ALU.mult),
                      reads=["zT", "const9"], writes=[un])
                s.add("dve", lambda e, c2=c2, ubx=ubx, tb0=tb0: e.scalar_tensor_tensor(out=ubx, in0=PS[tb0 + c2][:, :], scalar=scn[:, c2:c2 + 1], in1=ubx,
                                                                                  op0=ALU.mult, op1=ALU.add),
                      reads=["ps%d" % (tb0 + c2), "scn", un], writes=[un])
                s.add("dve", lambda e, c2=c2, tsl=tsl, yb=yb, ubx=ubx: e.tensor_tensor(out=yhs[yb], in0=ubx, in1=x0T3[:, c2, tsl], op=ALU.mult),
                      reads=[un, "x0T"], writes=["yhs%d" % yb])
                s.add("sp", lambda e, c2=c2, tsl=tsl, yb=yb: e.dma_start(out=yloc_d[c2 * 128:(c2 + 1) * 128, tsl], in_=yhs[yb]),
                      reads=["yhs%d" % yb], writes=["yloc"], dma="yhs%d" % yb)
        if upto("p2"):
            inv_issue(0)
            for tch in range(32):
                if tch + 1 < 32:
                    inv_issue(tch + 1)
                inv_transposes(tch)
                if tch % 4 == 3:
                    epilogue(tch // 4)
        s.barrier()
        if dbg == "p2":
            s.add("sp", lambda e: e.dma_start(out=dbuf[0][:, 0:4096], in_=yloc_d[0:128, :]), reads=["yloc"], writes=["dbgld2"], dma="dbgld")
            tmpf = dbuf[1].bitcast(F32)
            s.add("dve", lambda e: e.tensor_copy(tmpf, dbuf[0][:, 0:4096]), reads=["dbgld2"], writes=["dbgtmp"])
            s.add("sp", lambda e: e.dma_start(out=dbg_d[:, 0:4096], in_=tmpf), reads=["dbgtmp"], writes=["dbgout"], dma="dbg")
            s.barrier()
    A.off = base_mark

    if upto("p4b"):
        xchg_start(0)
        G = A.bf16(32 * 1024)
        G3 = G.rearrange("p (k t) -> p k t", k=32)
        mT3 = G3[:, 0:16, :]
        mT = G[:, 0:16 * 1024]
        mt_end = A.off - 16 * 1024 * 2
        hTo = A.bf16(16 * 1024)
        hTo3 = hTo.rearrange("p (k t) -> p k t", k=16)
        YT = A.bf16(16 * 1024)
        YT3 = YT.rearrange("p (k t) -> p k t", k=16)
        p4_mark = A.off
        h2T_off = 175 * 1024
        h2T = A.h16[:, h2T_off // 2:h2T_off // 2 + 16 * 1024]
        h2T3 = h2T.rearrange("p (k t) -> p k t", k=16)
        xt2 = [A.f32(D) for _ in range(2)]
        xs2 = [A.bf16(D) for _ in range(2)]
        junk2 = A.bf16(D)
        gbc2 = A.f32(D)
        WBLK = 256
        wg = [A.bf16(2 * 16 * WBLK) for _ in range(2)]
        wg4 = [w.rearrange("p (a k n) -> p a k n", a=2, k=16) for w in wg]
        s.add("sp", lambda e: e.dma_start(out=gbc2, in_=g1_d.partition_broadcast(128)), writes=["gbc2"], dma="gbc")
        xo_v = xo_d.rearrange("(n p) d -> n p d", p=128)
        for t in range(8):
            sl = t % 2
            s.add("sp", lambda e, sl=sl, t=t: e.dma_start(out=xt2[sl], in_=xo_v[t]), writes=["oxt%d" % sl], dma="oxt%d" % sl)
            rms_scale(xt2[sl], "oxt%d" % sl, xs2[sl], "oxs%d" % sl, gbc2, "gbc2", 32 + t, junk2)
            transposes_to(xs2[sl], "oxs%d" % sl, hTo3, t * 128, "ohT")
        wgate_v = wgate_d.rearrange("(k p) n -> p k n", p=128)
        wbh_v = wbh_d.rearrange("(k p) n -> p k n", p=128)
        wba_v = wba_d.rearrange("(k p) n -> p k n", p=128)
        NJB = D // WBLK

        def load_gblock(jb):
            sl = jb % 2
            wn = "wg%d" % sl
            c0 = jb * WBLK
            s.add("pool", lambda e, sl=sl, c0=c0: e.dma_start(out=wg4[sl][:, 0, :, :], in_=wgate_v[:, :, c0:c0 + WBLK]), writes=[wn], dma=wn)
            s.add("pool", lambda e, sl=sl, c0=c0: e.dma_start(out=wg4[sl][:, 1, :, :], in_=wgate_v[:, :, D + c0:D + c0 + WBLK]), writes=[wn], dma=wn)
        load_gblock(0)
        load_gblock(1)
        xchg_finish(0)
        xchg_start(1)
        cntA = 0
        for jb in range(NJB):
            sl = jb % 2
            wn = "wg%d" % sl
            for sub in range(WBLK // 128):
                jc = jb * (WBLK // 128) + sub
                so = sub * 128
                for tc_ in range(2):
                    tsl = slice(512 * tc_, 512 * (tc_ + 1))
                    par = cntA % 3
                    cntA += 1
                    b0, b1 = 2 + 2 * par, 3 + 2 * par
                    mm_group(PS[b0][:, :], lambda k, sl=sl, so=so: wg4[sl][:, 0, k, so:so + 128], lambda k, tsl=tsl: hTo3[:, k, tsl], 16, [wn, "ohT_a", "ohT_d"], "ps%d" % b0)
                    mm_group(PS[b1][:, :], lambda k, sl=sl, so=so: wg4[sl][:, 1, k, so:so + 128], lambda k, tsl=tsl: hTo3[:, k, tsl], 16, [wn, "ohT_a", "ohT_d"], "ps%d" % b1)
                    s.add("act", lambda e, jc=jc, b0=b0, tsl=tsl: e.activation(out=G3[:, jc, tsl], in_=PS[b0][:, :], func=AF.Sigmoid, bias=bgate[:, jc:jc + 1], scale=1.0),
                          reads=["ps%d" % b0, "const10"], writes=["gh%d" % jc])
                    s.add("act", lambda e, jc=jc, b1=b1, tsl=tsl: e.activation(out=G3[:, 16 + jc, tsl], in_=PS[b1][:, :], func=AF.Sigmoid, bias=bgate[:, 16 + jc:17 + jc], scale=1.0),
                          reads=["ps%d" % b1, "const10"], writes=["ga%d" % jc])
            if jb + 2 < NJB:
                load_gblock(jb + 2)
        xchg_finish(1)
        for g in range(4):
            kind, c2 = g // 2, g % 2
            base = kind * 8 + c2
            dst = _ap(YT, base * 1024, [[YT.ap[0][0], 128], [2 * 1024, 4], [1, 1024]])
            s.add("sp", lambda e, g=g, dst=dst: e.dma_start(out=dst, in_=ystage_d[g].rearrange("p (q t) -> p q t", q=4)),
                  reads=["ystage%d" % g], writes=["YT"], dma="YT")
        s.barrier()
        A.off = p4_mark
        wb = [A.bf16(2 * 8 * WBLK) for _ in range(2)]
        wb4 = [w.rearrange("p (a k n) -> p a k n", a=2, k=8) for w in wb]
        mtmp = [[A.f32(512) for _ in range(2)] for _ in range(2)]

        def load_bblock(jb):
            sl = jb % 2
            wn = "wb%d" % sl
            c0 = jb * WBLK
            s.add("pool", lambda e, sl=sl, c0=c0: e.dma_start(out=wb4[sl][:, 0, :, :], in_=wbh_v[:, :, c0:c0 + WBLK]), writes=[wn], dma=wn)
            s.add("pool", lambda e, sl=sl, c0=c0: e.dma_start(out=wb4[sl][:, 1, :, :], in_=wba_v[:, :, c0:c0 + WBLK]), writes=[wn], dma=wn)
        load_bblock(0)
        cntB = 0
        for jb in range(NJB):
            sl = jb % 2
            wn = "wb%d" % sl
            if jb + 1 < NJB:
                load_bblock(jb + 1)
            for sub in range(WBLK // 128):
                jc = jb * (WBLK // 128) + sub
                so = sub * 128
                for tc_ in range(2):
                    tsl = slice(512 * tc_, 512 * (tc_ + 1))
                    par = cntB % 2
                    cntB += 1
                    b2, b3 = (2, 3) if par == 0 else (4, 5)
                    m1_, m2_ = mtmp[par]
                    pn = "_%d" % par
                    mm_group(PS[b2][:, :], lambda k, sl=sl, so=so: wb4[sl][:, 0, k, so:so + 128], lambda k, tsl=tsl: YT3[:, k, tsl], 8, [wn, "YT"], "ps%d" % b2)
                    mm_group(PS[b3][:, :], lambda k, sl=sl, so=so: wb4[sl][:, 1, k, so:so + 128], lambda k, tsl=tsl: YT3[:, 8 + k, tsl], 8, [wn, "YT"], "ps%d" % b3)
                    s.add("dve", lambda e, b2=b2, m1_=m1_, jc=jc, tsl=tsl: e.tensor_tensor(out=m1_, in0=PS[b2][:, :], in1=G3[:, jc, tsl], op=ALU.mult),
                          reads=["ps%d" % b2, "gh%d" % jc], writes=["m1" + pn])
                    s.add("dve", lambda e, b3=b3, m2_=m2_, jc=jc, tsl=tsl: e.tensor_tensor(out=m2_, in0=PS[b3][:, :], in1=G3[:, 16 + jc, tsl], op=ALU.mult),
                          reads=["ps%d" % b3, "ga%d" % jc], writes=["m2" + pn])
                    s.add("pool", lambda e, jc=jc, tsl=tsl, m1_=m1_, m2_=m2_: e.tensor_tensor(out=mT3[:, jc, tsl], in0=m1_, in1=m2_, op=ALU.add),
                          reads=["m1" + pn, "m2" + pn], writes=["gh%d" % jc])
        s.barrier()
        if dbg == "p4b":
            A.off = p4_mark
            dump(mT[:, 0:8192], 8192, "mT")
            s.barrier()
    if upto("p4"):
        A.off = mt_end
        x1 = A.f32(8 * D)
        x13 = x1.rearrange("p (t d) -> p t d", t=8)
        wo = [A.bf16(16 * 512) for _ in range(2)]
        wo3 = [w.rearrange("p (k n) -> p k n", k=16) for w in wo]
        xs3 = [A.bf16(D) for _ in range(2)]
        junk3 = A.bf16(D)
        gbc3 = A.f32(D)
        assert A.off <= h2T_off
        s.add("sp", lambda e: e.dma_start(out=gbc3, in_=g2_d.partition_broadcast(128)), writes=["gbc3"], dma="gbc")
        for t in range(8):
            s.add("sp", lambda e, t=t: e.dma_start(out=x13[:, t, :], in_=xo_v[t]), writes=["x1_%d" % t], dma="x1ld")
        wout_v = wout_d.rearrange("(k p) n -> p k n", p=128)
        for db in range(4):
            sl = db % 2
            wn = "wo%d" % sl
            s.add("pool", lambda e, sl=sl, db=db: e.dma_start(out=wo3[sl], in_=wout_v[:, :, 512 * db:512 * (db + 1)]), writes=[wn], dma=wn)
            for t in range(8):
                bank = 2 + (t % 2)
                mm_group(PS[bank][:, :], lambda k, t=t: mT3[:, k, 128 * t:128 * (t + 1)], lambda k, sl=sl: wo3[sl][:, k, :], 16, [wn, "mT"], "ps%d" % bank)
                dsl = slice(512 * db, 512 * (db + 1))
                s.add("dve", lambda e, t=t, dsl=dsl, bank=bank: e.tensor_tensor(out=x13[:, t, dsl], in0=PS[bank][:, :], in1=x13[:, t, dsl], op=ALU.add),
                      reads=["ps%d" % bank, "x1_%d" % t], writes=["x1_%d" % t])
        x1_v = x1_d.rearrange("(n p) d -> n p d", p=128)
        for t in range(8):
            s.add("sp", lambda e, t=t: e.dma_start(out=x1_v[t], in_=x13[:, t, :]), reads=["x1_%d" % t], writes=["x1d"], dma="x1st")
            sl = t % 2
            rms_scale(x13[:, t, :], "x1_%d" % t, xs3[sl], "fxs%d" % sl, gbc3, "gbc3", 48 + t, junk3)
            transposes_to(xs3[sl], "fxs%d" % sl, h2T3, t * 128, "fhT")
        s.barrier()
        if dbg == "p4":
            A.off = mt_end + 8 * D * 4
            dump(x1[:, 0:8192], 8192, "x1_3")
            s.barrier()
    if upto("p5"):
        A.off = base_mark
        aT = A.bf16(NFF * 1024)
        aT3 = aT.rearrange("p (f t) -> p f t", f=NFF)
        p5_mark = A.off
        wgu = [A.bf16(2 * 16 * 256) for _ in range(2)]
        wgu4 = [w.rearrange("p (a k n) -> p a k n", a=2, k=16) for w in wgu]
        sgl = [A.f32(512) for _ in range(2)]
        assert A.off <= h2T_off
        wfg_v = wfg_d.rearrange("(k p) n -> p k n", p=128)
        wfu_v = wfu_d.rearrange("(k p) n -> p k n", p=128)
        for fb in range(DFF // 256):
            sl = fb % 2
            wn = "wgu%d" % sl
            c0 = fb * 256
            s.add("pool", lambda e, sl=sl, c0=c0: e.dma_start(out=wgu4[sl][:, 0, :, :], in_=wfg_v[:, :, c0:c0 + 256]), writes=[wn], dma=wn)
            s.add("pool", lambda e, sl=sl, c0=c0: e.dma_start(out=wgu4[sl][:, 1, :, :], in_=wfu_v[:, :, c0:c0 + 256]), writes=[wn], dma=wn)
            for sub in range(2):
                fc = fb * 2 + sub
                so = sub * 128
                for tc_ in range(2):
                    tsl = slice(512 * tc_, 512 * (tc_ + 1))
                    u = (fc * 2 + tc_) % 2
                    bg, bu = 2 + 2 * u, 3 + 2 * u
                    mm_group(PS[bg][:, :], lambda k, sl=sl, so=so: wgu4[sl][:, 0, k, so:so + 128], lambda k, tsl=tsl: h2T3[:, k, tsl], 16, [wn, "fhT_a", "fhT_d"], "ps%d" % bg)
                    mm_group(PS[bu][:, :], lambda k, sl=sl, so=so: wgu4[sl][:, 1, k, so:so + 128], lambda k, tsl=tsl: h2T3[:, k, tsl], 16, [wn, "fhT_a", "fhT_d"], "ps%d" % bu)
                    s.add("act", lambda e, u=u, bg=bg: e.activation(out=sgl[u], in_=PS[bg][:, :], func=AF.Silu), reads=["ps%d" % bg], writes=["sgl%d" % u])
                    s.add("dve", lambda e, u=u, bu=bu, fc=fc, tsl=tsl: e.tensor_tensor(out=aT3[:, fc, tsl], in0=PS[bu][:, :], in1=sgl[u], op=ALU.mult),
                          reads=["ps%d" % bu, "sgl%d" % u], writes=["aT"])
        s.barrier()
        A.off = p5_mark
        wd = [A.bf16(NFF * 512) for _ in range(2)]
        wd3 = [w.rearrange("p (f n) -> p f n", f=NFF) for w in wd]
        xr = [A.f32(512) for _ in range(2)]
        ot = [A.f32(512) for _ in range(2)]
        wfd_v = wfd_d.rearrange("(f p) n -> p f n", p=128)
        out_v = out_d.rearrange("(n p) d -> n p d", p=128)
        cnt = 0
        for db in range(4):
            sl = db % 2
            wn = "wd%d" % sl
            for f0 in (0, 22):
                s.add("pool", lambda e, sl=sl, f0=f0, db=db: e.dma_start(out=wd3[sl][:, f0:f0 + 22, :], in_=wfd_v[:, f0:f0 + 22, 512 * db:512 * (db + 1)]), writes=[wn], dma=wn)
            for t in range(8):
                bank = 2 + (t % 2)
                u = cnt % 2
                cnt += 1
                dsl = slice(512 * db, 512 * (db + 1))
                s.add("sp", lambda e, t=t, dsl=dsl, u=u: e.dma_start(out=xr[u], in_=x1_v[t][:, dsl]), reads=["x1d"], writes=["xr%d" % u], dma="xr%d" % u)
                mm_group(PS[bank][:, :], lambda k, t=t: aT3[:, k, 128 * t:128 * (t + 1)], lambda k, sl=sl: wd3[sl][:, k, :], NFF, [wn, "aT"], "ps%d" % bank)
                s.add("dve", lambda e, u=u, bank=bank: e.tensor_tensor(out=ot[u], in0=PS[bank][:, :], in1=xr[u], op=ALU.add),
                      reads=["ps%d" % bank, "xr%d" % u], writes=["ot%d" % u])
                s.add("sp", lambda e, t=t, dsl=dsl, u=u: e.dma_start(out=out_v[t][:, dsl], in_=ot[u]), reads=["ot%d" % u], writes=["outd"], dma="ot%d" % u)
    s.barrier()

    dma_keys = list(s.dma_cnt.keys())
    eng_sems = {e: nc.alloc_semaphore("sem_" + e) for e in ENGS}
    dma_sems = {k: nc.alloc_semaphore("dsem_%d" % i) for i, k in enumerate(dma_keys)}
    with nc.Block() as block:
        s.emit_all(block, eng_sems, dma_sems)
    return nc


_CONST_CACHE = {}


def _constants():
    if _CONST_CACHE:
        return _CONST_CACHE
    bf = ml_dtypes.bfloat16
    c = {}
    c["identb"] = np.eye(128, dtype=np.float32).astype(bf)
    c["identf"] = np.eye(128, dtype=np.float32)
    c["onesdiv"] = np.full((128, 128), 1.0 / 128.0, dtype=np.float32).astype(bf)
    perm = np.zeros((128, 128), dtype=np.float32)
    for d in range(128):
        partner = d + 32 if (d % 64) < 32 else d - 32
        perm[partner, d] = 1.0
    c["perm"] = perm.astype(bf)
    inv = (10000.0 ** (-np.arange(0, 64, 2, dtype=np.float32) / 64.0)).astype(np.float32)
    pos = np.arange(64, dtype=np.float32)
    ropec = np.zeros((128, 64), dtype=np.float32)
    ropes = np.zeros((128, 64), dtype=np.float32)
    for d in range(128):
        ang = (pos * inv[d % 32]).astype(np.float32)
        ropec[d] = np.cos(ang)
        sgn = -1.0 if (d % 64) < 32 else 1.0
        ropes[d] = sgn * np.sin(ang)
    c["ropec"] = ropec
    c["ropes"] = ropes
    L = S
    posf = np.arange(L, dtype=np.float32)
    t = (posf / np.float32(L - 1)).astype(np.float32)
    fb = np.linspace(1e-4, 15, 16, dtype=np.float32)
    ang = ((np.float32(2.0 * math.pi) * posf / np.float32(L))[:, None] * fb[None, :]).astype(np.float32)
    zemb = np.concatenate([t[:, None], np.cos(ang), -np.sin(ang)], axis=-1).astype(np.float32)
    c["zemb"] = np.ascontiguousarray(zemb.T)
    max_decay = math.log(1e-2) / 0.3
    min_decay = math.log(1e-2) / 1.5
    deltas = np.abs(np.linspace(min_decay, max_decay, 1024, dtype=np.float32))
    c["decay_full"] = np.exp(-t[:, None] * deltas[None, :]).astype(np.float32)
    m = np.arange(S, dtype=np.int64)[:, None]
    f = np.arange(S, dtype=np.int64)[None, :]
    ph = (m * (2 * f + 1)) % (2 * 8192)
    th = ph.astype(np.float64) * (2.0 * math.pi / (2 * 8192))
    C = np.cos(th).astype(np.float32).astype(bf)
    Sn = np.sin(th).astype(np.float32).astype(bf)
    def fwd_layout(M):
        return M.reshape(32, 128, 32, 128).transpose(2, 1, 0, 3)
    def inv_layout(M):
        return M.reshape(32, 128, 32, 128).transpose(0, 3, 2, 1)
    c["dftF"] = np.ascontiguousarray(np.stack([fwd_layout(C), fwd_layout(Sn)], axis=2)).reshape(32, 128, 2 * 32 * 128)
    c["dftI"] = np.ascontiguousarray(np.stack([inv_layout(C), inv_layout(Sn)], axis=2)).reshape(32, 128, 2 * 32 * 128)
    _CONST_CACHE.update(c)
    return _CONST_CACHE


def _chunkcols(v):
    v = np.asarray(v, dtype=np.float32)
    return np.ascontiguousarray(v.reshape(-1, 128).T)


def _core_inputs(inp, b, r):
    c = _constants()
    f32 = np.float32
    w_in = inp["w_in"][0]
    hsl = lambda base: slice(base + 256 * r, base + 256 * (r + 1))
    kvh = r // 2
    cols = np.concatenate([
        np.arange(0 + 256 * r, 0 + 256 * (r + 1)),
        np.arange(1024 + 256 * r, 1024 + 256 * (r + 1)),
        np.arange(2048 + 256 * r, 2048 + 256 * (r + 1)),
        np.arange(3072 + 256 * r, 3072 + 256 * (r + 1)),
        np.arange(4096 + 128 * kvh, 4096 + 128 * (kvh + 1)),
        np.arange(4352 + 128 * kvh, 4352 + 128 * (kvh + 1)),
    ])
    m = {}
    m["x"] = np.ascontiguousarray(inp["x"][b])
    m["xo"] = np.ascontiguousarray(inp["x"][b, 1024 * r:1024 * (r + 1)])
    m["wmix"] = np.ascontiguousarray(w_in[:, cols])
    m["wgate"] = np.ascontiguousarray(w_in[:, 4608:8704])
    m["bgate"] = _chunkcols(inp["b_gate"][0])
    m["g1"] = np.ascontiguousarray(inp["mix_norm_g"][0])
    cw = inp["hy_conv_w"][0]
    cb = inp["hy_conv_b"][0]
    convw = np.zeros((128, 18), dtype=f32)
    convb = np.zeros((128, 6), dtype=f32)
    for cc in range(6):
        base = (cc // 2) * 1024 + 256 * r + (cc % 2) * 128
        for j in range(3):
            convw[:, cc * 3 + j] = cw[j, base:base + 128]
        convb[:, cc] = cb[base:base + 128]
    m["convw"] = convw
    m["convb"] = convb
    m["fw1"] = np.ascontiguousarray(inp["flt_w1"][0])
    m["fw2"] = np.ascontiguousarray(inp["flt_w2"][0])
    m["fw3"] = np.ascontiguousarray(inp["flt_w3"][0])
    w4 = inp["flt_w4"][0]
    m["fw4"] = np.ascontiguousarray(np.concatenate([w4[:, hsl(0)], w4[:, hsl(1024)]], axis=1))
    m["fvec"] = np.ascontiguousarray(np.stack([inp["flt_freq"][0], inp["flt_b1"][0], inp["flt_b2"][0], inp["flt_b3"][0]], axis=1))
    m["hybias"] = _chunkcols(inp["hy_bias"][0][hsl(0)])
    m["gqk"] = np.ascontiguousarray(np.stack([inp["q_norm_g"][0], inp["k_norm_g"][0]], axis=1))
    m["wbh"] = inp["w_br_hyena"][0]
    m["wba"] = inp["w_br_attn"][0]
    m["wout"] = inp["w_out"][0]
    m["g2"] = np.ascontiguousarray(inp["ffn_norm_g"][0])
    m["wfg"] = inp["w_ffn_gate"][0]
    m["wfu"] = inp["w_ffn_up"][0]
    m["wfd"] = inp["w_ffn_down"][0]
    for k in ("identb", "identf", "onesdiv", "perm", "ropec", "ropes", "zemb", "dftF", "dftI"):
        m[k] = c[k]
    m["decay"] = np.ascontiguousarray(c["decay_full"][:, 256 * r:256 * (r + 1)])
    out = {}
    for k, v in m.items():
        v = np.asarray(v)
        if k in _SHAPES and tuple(v.shape) != _SHAPES[k]:
            v = np.zeros(_SHAPES[k], dtype=v.dtype)
        out[k] = np.ascontiguousarray(v)
    return out


_NC_CACHE = {}


def kernel(**inputs):
    inp = {k: np.asarray(v) for k, v in inputs.items()}
    key = _DEBUG["stop"]
    if key not in _NC_CACHE:
        _NC_CACHE[key] = build_program()
    nc = _NC_CACHE[key]
    in_maps = [_core_inputs(inp, c // 4, c % 4) for c in range(8)]
    res = run_bass_kernel_spmd(nc, in_maps, core_ids=list(range(8)))
    if key is not None:
        return [r["dbg"] for r in res.results]
    out = np.zeros((2, S, D), dtype=np.float32)
    for c in range(8):
        b, r = c // 4, c % 4
        out[b, 1024 * r:1024 * (r + 1)] = res.results[c]["out"]
    return out
```

```python
import math
import numpy as np
import ml_dtypes
import concourse.bass as bass
import concourse.mybir as mybir
from concourse.bass_utils import run_bass_kernel_spmd

F32 = mybir.dt.float32
BF16 = mybir.dt.bfloat16
AF = mybir.ActivationFunctionType
ALU = mybir.AluOpType

D = 2048
S = 4096
DFF = 5632
EPS = 1e-6
NFF = DFF // 128
ENGS = ["pe", "act", "dve", "pool", "sp"]
_DEBUG = {"stop": None}
_RID = {}
_SHAPES = {}


class _Op:
    __slots__ = ("eng", "emit", "deps", "dma_deps", "is_dma", "semkey", "signaled", "sig_idx", "inc")


class Sched:
    def __init__(self):
        self.ops = {e: [] for e in ENGS}
        self.lastw = {}
        self.readers = {}
        self.dma_cnt = {}

    def add(self, eng, emit, reads=(), writes=(), dma=None, inc=16):
        op = _Op()
        op.inc = inc
        op.eng = eng
        op.emit = emit
        op.is_dma = dma is not None
        op.semkey = dma
        op.signaled = False
        op.sig_idx = 0
        deps = {}
        same_raw = set()
        for r in reads:
            w = self.lastw.get(r)
            if w is not None:
                deps[id(w)] = w
                if w.eng == eng and eng != "pe" and not w.is_dma:
                    same_raw.add(id(w))
        for wn in writes:
            w = self.lastw.get(wn)
            if w is not None:
                deps[id(w)] = w
            for rd in self.readers.get(wn, ()):
                deps[id(rd)] = rd
        for r in reads:
            self.readers.setdefault(r, []).append(op)
        for wn in writes:
            self.lastw[wn] = op
            self.readers[wn] = []
        op.deps = []
        op.dma_deps = {}
        for d in deps.values():
            if d is op:
                continue
            if d.is_dma:
                op.dma_deps[d.semkey] = self.dma_cnt[d.semkey]
            elif d.eng != eng or op.is_dma or id(d) in same_raw or eng != "pe":
                op.deps.append(d)
                d.signaled = True
        if op.is_dma:
            self.dma_cnt[dma] = self.dma_cnt.get(dma, 0) + inc
        self.ops[eng].append(op)
        return op

    def barrier(self):
        lasts = {}
        for e in ENGS:
            for op in reversed(self.ops[e]):
                if not op.is_dma and op.emit is not None:
                    lasts[e] = op
                    break
        for e in ENGS:
            op = _Op()
            op.inc = 0
            op.eng = e
            op.emit = None
            op.is_dma = False
            op.semkey = None
            op.signaled = False
            op.sig_idx = 0
            op.deps = []
            for e2, l in lasts.items():
                if e2 != e:
                    op.deps.append(l)
                    l.signaled = True
            op.dma_deps = dict(self.dma_cnt)
            self.ops[e].append(op)

    def emit_all(self, block, eng_sems, dma_sems):
        for e in ENGS:
            c = 0
            for op in self.ops[e]:
                if op.signaled and not op.is_dma:
                    c += 1
                    op.sig_idx = c

        def make(engname):
            def fn(e):
                waited = {}
                if engname == "sp":
                    _RID["v"] = e.snap(e.partition_id() % 4, min_val=0, max_val=3)
                for op in self.ops[engname]:
                    for d in op.deps:
                        key = ("e", d.eng)
                        if waited.get(key, 0) < d.sig_idx:
                            e.wait_ge(eng_sems[d.eng], d.sig_idx)
                            waited[key] = d.sig_idx
                    for k, v in op.dma_deps.items():
                        key = ("d", k)
                        if waited.get(key, 0) < v:
                            e.wait_ge(dma_sems[k], v)
                            waited[key] = v
                    if op.emit is None:
                        continue
                    ins = op.emit(e)
                    if op.is_dma:
                        ins.then_inc(dma_sems[op.semkey], op.inc)
                    elif op.signaled:
                        ins.then_inc(eng_sems[engname], 1)
            return fn

        block.tensor(make("pe"))
        block.scalar(make("act"))
        block.vector(make("dve"))
        block.gpsimd(make("pool"))
        block.sync(make("sp"))


class Arena:
    def __init__(self, nc, nbytes):
        self.h32 = nc.alloc_sbuf_tensor("arena", [128, nbytes // 4], F32)
        self.h16 = self.h32.bitcast(BF16)
        self.nbytes = nbytes
        self.off = 0

    def alloc(self, nbytes):
        nbytes = (nbytes + 63) // 64 * 64
        o = self.off
        self.off += nbytes
        assert self.off <= self.nbytes, ("SBUF arena overflow", self.off, self.nbytes)
        return o

    def f32(self, n):
        o = self.alloc(4 * n)
        return self.h32[:, o // 4:o // 4 + n]

    def bf16(self, n):
        o = self.alloc(2 * n)
        return self.h16[:, o // 2:o // 2 + n]


def _ap(t, extra_off, dims):
    return bass.AP(t.tensor, t.offset + extra_off, dims)


def build_program():
    nc = bass.Bass("TRN2", target_bir_lowering=False)
    s = Sched()
    dbg = _DEBUG["stop"]
    ORDER = ["p1", "p3", "p2a", "p2f", "p2", "p4b", "p4", "p5"]

    def upto(name):
        return dbg is None or ORDER.index(name) <= ORDER.index(dbg)

    def din(name, shape, dt=F32):
        need = {"wgate": "p4b", "wbh": "p4b", "wba": "p4b", "wout": "p4", "wfg": "p5", "wfu": "p5", "wfd": "p5",
                "dftF": "p2f", "dftI": "p2"}
        if name in need and not upto(need[name]):
            shape = [32, 128, 128] if len(shape) == 3 else [128, 128]
        _SHAPES[name] = tuple(shape)
        return nc.dram_tensor(name, list(shape), dt, kind="ExternalInput").ap()

    x_d = din("x", [S, D])
    xo_d = din("xo", [1024, D])
    wmix_d = din("wmix", [D, 1280])
    wgate_d = din("wgate", [D, 4096])
    bgate_d = din("bgate", [128, 32])
    g1_d = din("g1", [D])
    convw_d = din("convw", [128, 18])
    convb_d = din("convb", [128, 6])
    fw1_d = din("fw1", [33, 64])
    fw2_d = din("fw2", [64, 64])
    fw3_d = din("fw3", [64, 64])
    fw4_d = din("fw4", [64, 512])
    fvec_d = din("fvec", [64, 4])
    hyb_d = din("hybias", [128, 2])
    gqk_d = din("gqk", [128, 2])
    wbh_d = din("wbh", [1024, D])
    wba_d = din("wba", [1024, D])
    wout_d = din("wout", [D, D])
    g2_d = din("g2", [D])
    wfg_d = din("wfg", [D, DFF])
    wfu_d = din("wfu", [D, DFF])
    wfd_d = din("wfd", [DFF, D])
    identb_d = din("identb", [128, 128], BF16)
    identf_d = din("identf", [128, 128])
    onesdiv_d = din("onesdiv", [128, 128], BF16)
    perm_d = din("perm", [128, 128], BF16)
    ropec_d = din("ropec", [128, 64])
    ropes_d = din("ropes", [128, 64])
    zemb_d = din("zemb", [33, S])
    decay_d = din("decay", [S, 256])
    dftF_d = din("dftF", [32, 128, 2 * 32 * 128], BF16)
    dftI_d = din("dftI", [32, 128, 2 * 32 * 128], BF16)

    out_d = nc.dram_tensor("out", [1024, D], F32, kind="ExternalOutput").ap()
    yloc_d = nc.dram_tensor("yloc", [512, S], BF16, kind="Internal").ap()
    cloc_d = nc.dram_tensor("cloc", [128, S], BF16, kind="Internal").ap()
    cgat_d = nc.dram_tensor("cgat", [4 * 128, S], BF16, kind="Internal").ap()
    ystage_d = nc.dram_tensor("ystage", [4, 128, 4 * 1024], BF16, kind="Internal").ap()
    x1_d = nc.dram_tensor("x1s", [1024, D], F32, kind="Internal").ap()
    dbg_d = None
    if dbg is not None:
        dbg_d = nc.dram_tensor("dbg", [128, 8192], F32, kind="ExternalOutput").ap()

    A = Arena(nc, 207 * 1024)
    PS = [nc.alloc_psum_tensor("psb%d" % i, [128, 512], F32) for i in range(8)]
    PSB = [p.bitcast(BF16) for p in PS]

    identb = A.bf16(128)
    identf = A.f32(128)
    onesdiv = A.bf16(128)
    perm = A.bf16(128)
    ropec = A.f32(64)
    ropes = A.f32(64)
    gqk = A.f32(2)
    convw = A.f32(18)
    convb = A.f32(6)
    hyb = A.f32(2)
    bgate = A.f32(32)
    fvec = A.f32(4)
    stats = A.f32(64)
    negpi = A.f32(1)
    onescol = A.f32(1)
    epst = A.f32(1)
    small_loads = [(identb, identb_d), (identf, identf_d), (onesdiv, onesdiv_d), (perm, perm_d),
                   (ropec, ropec_d), (ropes, ropes_d), (gqk, gqk_d), (convw, convw_d), (convb, convb_d),
                   (hyb, hyb_d), (bgate, bgate_d)]
    for i, (dst, src) in enumerate(small_loads):
        s.add("sp", lambda e, dst=dst, src=src: e.dma_start(out=dst, in_=src), writes=["const%d" % i], dma="const")
    s.add("sp", lambda e: e.dma_start(out=fvec[0:64, :], in_=fvec_d), writes=["fvec"], dma="const")
    s.add("dve", lambda e: e.memset(negpi, -math.pi), writes=["negpi"])
    s.add("dve", lambda e: e.memset(onescol, 1.0), writes=["onescol"])
    s.add("dve", lambda e: e.memset(epst, EPS), writes=["epst"])
    base_mark = A.off

    def dump(ap_sb, ncols, name):
        tmp = A.f32(ncols)
        s.add("dve", lambda e: e.tensor_copy(tmp, ap_sb), reads=[name], writes=["dbgtmp"])
        s.add("sp", lambda e: e.dma_start(out=dbg_d[:, 0:ncols], in_=tmp), reads=["dbgtmp"], writes=["dbgout"], dma="dbg")

    def mm_group(bank_ap, lhs_fn, rhs_fn, nk, reads, bankname):
        def f(e):
            ins = None
            for k in range(nk):
                ins = e.matmul(bank_ap, lhs_fn(k), rhs_fn(k), start=(k == 0), stop=(k == nk - 1))
            return ins
        s.add("pe", f, reads=reads, writes=[bankname])

    cgv = cgat_d.rearrange("(q p) t -> p q t", q=4)

    def xchg_start(g):
        s.add("sp", lambda e: e.dma_start(out=cloc_d, in_=yloc_d[g * 128:(g + 1) * 128, :]),
              reads=["yloc"], writes=["cloc"], dma="cloc")
        s.add("pool", lambda e: e.collective_compute("AllGather", ALU.bypass, replica_groups=[[0, 1, 2, 3], [4, 5, 6, 7]],
                                                     ins=[cloc_d], outs=[cgat_d]),
              reads=["cloc"], writes=["cgat"], dma="cc", inc=1)

    def xchg_finish(g):
        s.add("sp", lambda e: e.dma_start(out=ystage_d[g].rearrange("p (q t) -> p q t", q=4),
                                          in_=cgv[:, :, bass.ds(_RID["v"] * 1024, 1024)]),
              reads=["cgat"], writes=["ystage%d" % g], dma="ystage")

    _TCNT = [0]

    def transposes_to(src_bf, srcname, hdst3, col0, hname, tbanks=(0, 1)):
        for g4 in range(4):
            bank = tbanks[_TCNT[0] % len(tbanks)]
            _TCNT[0] += 1
            pst = PSB[bank][:, 0:512].rearrange("p (a t) -> p a t", a=4)
            pname = "ps%d" % bank

            def tr(e, g4=g4, pst=pst):
                ins = None
                for a in range(4):
                    dk = g4 * 4 + a
                    ins = e.transpose(out=pst[:, a, :], in_=src_bf[:, dk * 128:(dk + 1) * 128], identity=identb)
                return ins
            s.add("pe", tr, reads=[srcname, "const0"], writes=[pname])
            dst = hdst3[:, g4 * 4:(g4 + 1) * 4, col0:col0 + 128]
            if g4 % 2 == 0:
                s.add("act", lambda e, dst=dst, pst=pst: e.activation(out=dst, in_=pst, func=AF.Copy),
                      reads=[pname], writes=[hname + "_a"])
            else:
                s.add("dve", lambda e, dst=dst, pst=pst: e.tensor_copy(dst, pst),
                      reads=[pname], writes=[hname + "_d"])

    def rms_scale(src_f32, srcname, dst_bf, dstname, gtile, gname, statcol, junkbuf):
        sc = stats[:, statcol:statcol + 1]
        sn = "stat%d" % statcol
        s.add("act", lambda e: e.activation(out=junkbuf, in_=src_f32, func=AF.Square, accum_out=sc),
              reads=[srcname], writes=["junk", sn])
        s.add("act", lambda e: e.activation(out=sc, in_=sc, func=AF.Sqrt, bias=epst, scale=1.0 / D), reads=[sn, "epst"], writes=[sn])
        s.add("dve", lambda e: e.reciprocal(sc, sc), reads=[sn], writes=[sn])
        s.add("dve", lambda e: e.scalar_tensor_tensor(out=dst_bf, in0=src_f32, scalar=sc, in1=gtile, op0=ALU.mult, op1=ALU.mult),
              reads=[srcname, sn, gname], writes=[dstname])

    RAW = A.bf16(6 * 4098)
    RAW3 = RAW.rearrange("p (c t) -> p c t", c=6)
    raw_end = A.off
    QT = A.bf16(2 * S)
    KT = A.bf16(S)
    V = A.bf16(32 * 128)
    QT3 = QT.rearrange("p (h t) -> p h t", h=2)
    V3 = V.rearrange("p (c d) -> p c d", c=32)
    p1_mark = A.off

    wm = A.bf16(16 * 1280)
    wm3 = wm.rearrange("p (k n) -> p k n", k=16)
    xt = [A.f32(D) for _ in range(2)]
    xs = [A.bf16(D) for _ in range(2)]
    junk = A.bf16(D)
    hT = [A.bf16(16 * 512) for _ in range(2)]
    hT3 = [h.rearrange("p (k t) -> p k t", k=16) for h in hT]
    gbc = A.f32(D)
    sqs = [A.bf16(512) for _ in range(1)]
    rstdqs = [A.f32(512)] * 3
    qns = [A.bf16(512) for _ in range(3)]
    rawq = [A.f32(512) for _ in range(3)]
    t1 = A.f32(512)
    t2 = A.f32(512)
    print("P1 arena end", A.off, "of", A.nbytes)

    wmix_v = wmix_d.rearrange("(k p) n -> p k n", p=128)
    for k0 in (0, 8):
        s.add("pool", lambda e, k0=k0: e.dma_start(out=wm3[:, k0:k0 + 8, :], in_=wmix_v[:, k0:k0 + 8, :]), writes=["wm"], dma="wm")
    s.add("sp", lambda e: e.dma_start(out=gbc, in_=g1_d.partition_broadcast(128)), writes=["gbc"], dma="gbc")
    s.add("dve", lambda e: e.memset(RAW3[:, :, 0:1], 0.0), writes=["rawpad0"])
    s.add("dve", lambda e: e.memset(RAW3[:, :, 4097:4098], 0.0), writes=["rawpad1"])
    x_v = x_d.rearrange("(n p) d -> n p d", p=128)
    pstep_c = ropec.ap[0][0]
    pstep_s = ropes.ap[0][0]

    def tile_step(jn, i):
        hs_ = jn % 2
        tile = 4 * jn + i
        sl = i % 2
        s.add("sp", lambda e, sl=sl, tile=tile: e.dma_start(out=xt[sl], in_=x_v[tile]), writes=["xt%d" % sl], dma="xt%d" % sl)
        rms_scale(xt[sl], "xt%d" % sl, xs[sl], "xs%d" % sl, gbc, "gbc", tile, junk)
        transposes_to(xs[sl], "xs%d" % sl, hT3[hs_], i * 128, "hT%d" % hs_, tbanks=(0, 1, 7))

    def chunk_steps(j):
        hs = j % 2
        hname = "hT%d" % hs

        def qk_proj(u):
            bank = 2 + (u % 2)
            col = 768 + u * 128
            mm_group(PS[bank][:, :], lambda k, col=col: wm3[:, k, col:col + 128],
                     lambda k, hs=hs: hT3[hs][:, k, :], 16, [hname + "_a", hname + "_d", "wm"], "ps%d" % bank)
            s.add("act", lambda e, bank=bank, u=u: e.activation(out=rawq[u], in_=PS[bank][:, :], func=AF.Copy),
                  reads=["ps%d" % bank], writes=["rawq%d" % u])

        def hy_proj(cc):
            bank = 2 + (cc % 2)
            mm_group(PS[bank][:, :], lambda k, cc=cc: wm3[:, k, cc * 128:(cc + 1) * 128],
                     lambda k, hs=hs: hT3[hs][:, k, :], 16, [hname + "_a", hname + "_d", "wm"], "ps%d" % bank)
            dst = RAW3[:, cc, 1 + 512 * j:1 + 512 * (j + 1)]
            if cc % 2 == 0:
                s.add("act", lambda e, dst=dst, bank=bank: e.activation(out=dst, in_=PS[bank][:, :], func=AF.Copy),
                      reads=["ps%d" % bank], writes=["raw_a"])
            else:
                s.add("dve", lambda e, dst=dst, bank=bank: e.tensor_copy(dst, PS[bank][:, :]),
                      reads=["ps%d" % bank], writes=["raw_d"])

        def qk_square(u):
            s.add("act", lambda e, u=u: e.activation(out=sqs[0], in_=rawq[u], func=AF.Square),
                  reads=["rawq%d" % u], writes=["sq"])

        def qk_norm(u):
            s.add("pe", lambda e, u=u: e.matmul(PS[4][:, :], onesdiv, sqs[0], start=True, stop=True),
                  reads=["sq", "const2"], writes=["ps4"])
            s.add("act", lambda e, u=u: e.activation(out=rstdqs[u], in_=PS[4][:, :], func=AF.Sqrt, bias=epst, scale=1.0),
                  reads=["ps4", "epst"], writes=["rstdq"])
            s.add("dve", lambda e, u=u: e.reciprocal(rstdqs[u], rstdqs[u]), reads=["rstdq"], writes=["rstdq"])
            gcol = gqk[:, 0:1] if u < 2 else gqk[:, 1:2]
            s.add("dve", lambda e, u=u, gcol=gcol: e.scalar_tensor_tensor(
                out=qns[u], in0=rawq[u], scalar=gcol, in1=rstdqs[u], op0=ALU.mult, op1=ALU.mult),
                reads=["rawq%d" % u, "rstdq", "const6"], writes=["qn%d" % u])

        def qk_rope(u):
            qn_ = qns[u]
            s.add("pe", lambda e, qn_=qn_: e.matmul(PS[5][:, :], perm, qn_, start=True, stop=True),
                  reads=["qn%d" % u, "const3"], writes=["ps5"])
            for half in range(2):
                p0 = half * 64
                if half == 0:
                    cap = _ap(ropec, p0 * pstep_c + 8 * j, [[pstep_c, 64], [1, 8], [0, 64]])
                    sap = _ap(ropes, p0 * pstep_s + 8 * j, [[pstep_s, 64], [1, 8], [0, 64]])
                else:
                    cap = _ap(ropec, p0 * pstep_c, [[pstep_c, 64], [0, 8], [1, 64]])
                    sap = _ap(ropes, p0 * pstep_s, [[pstep_s, 64], [0, 8], [1, 64]])
                qv = qn_[p0:p0 + 64, :].rearrange("p (a b) -> p a b", a=8)
                t1v = t1[p0:p0 + 64, :].rearrange("p (a b) -> p a b", a=8)
                t2v = t2[p0:p0 + 64, :].rearrange("p (a b) -> p a b", a=8)
                pv = PS[5][p0:p0 + 64, :].rearrange("p (a b) -> p a b", a=8)
                s.add("pool", lambda e, t1v=t1v, qv=qv, cap=cap: e.tensor_tensor(out=t1v, in0=qv, in1=cap, op=ALU.mult),
                      reads=["qn%d" % u, "const4"], writes=["t1_%d" % half])
                s.add("dve", lambda e, t2v=t2v, pv=pv, sap=sap: e.tensor_tensor(out=t2v, in0=pv, in1=sap, op=ALU.mult),
                      reads=["ps5", "const5"], writes=["t2_%d" % half])
            dstq = QT3[:, u, 512 * j:512 * (j + 1)] if u < 2 else KT[:, 512 * j:512 * (j + 1)]
            s.add("pool", lambda e, dstq=dstq: e.tensor_tensor(out=dstq, in0=t1, in1=t2, op=ALU.add),
                  reads=["t1_0", "t1_1", "t2_0", "t2_1"], writes=["qk"])
        def v_step():
            def vmm(e, hs=hs):
                ins = None
                for i in range(4):
                    for k in range(16):
                        ins = e.matmul(PS[6][:, i * 128:(i + 1) * 128], hT3[hs][:, k, i * 128:(i + 1) * 128],
                                       wm3[:, k, 1152:1280], start=(k == 0), stop=(k == 15))
                return ins
            s.add("pe", vmm, reads=[hname + "_a", hname + "_d", "wm"], writes=["ps6"])
            s.add("act", lambda e, j=j: e.activation(out=V3[:, 4 * j:4 * j + 4, :],
                                                     in_=PS[6][:, :].rearrange("p (a d) -> p a d", a=4), func=AF.Copy),
                  reads=["ps6"], writes=["v"])
        steps = [lambda: qk_proj(0), lambda: qk_proj(1), lambda: qk_proj(2)]
        for cc in range(3):
            steps.append(lambda cc=cc: (qk_square(cc), hy_proj(cc), qk_norm(cc)))
        for cc in range(3, 6):
            steps.append(lambda cc=cc: (hy_proj(cc), qk_rope(cc - 3)))
        steps.append(v_step)
        return steps

    for i in range(4):
        tile_step(0, i)
    for j in range(8):
        steps = chunk_steps(j)
        nxt = [(j + 1, i) for i in range(4)] if j + 1 < 8 else []
        for idx, st in enumerate(steps):
            st()
            if idx in (1, 3, 5, 7) and nxt:
                tile_step(*nxt.pop(0))
        while nxt:
            tile_step(*nxt.pop(0))
    s.barrier()
    A.off = p1_mark
    if dbg == "p1":
        dump(QT[:, 0:8192], 8192, "qk")
        s.barrier()

    if upto("p3"):
        PT = [A.bf16(512) for _ in range(3)]
        rec = A.f32(512)
        yab = [A.bf16(512) for _ in range(2)]
        ones128 = A.bf16(128)
        s.add("dve", lambda e: e.memset(ones128, 1.0), writes=["ones128"])
        scale = 128.0 ** -0.5
        unit = 0
        for h in range(2):
            for qc in range(8):
                ob = 4 + 2 * (unit % 2)
                def st_mm(sc_):
                    sb = sc_ % 2
                    s.add("pe", lambda e, sb=sb, sc_=sc_, h=h, qc=qc: e.matmul(
                        PS[sb][:, :], KT[:, 128 * sc_:128 * (sc_ + 1)], QT3[:, h, 512 * qc:512 * (qc + 1)], start=True, stop=True),
                        reads=["qk"], writes=["ps%d" % sb])
                    pt = sc_ % 3
                    s.add("act", lambda e, sb=sb, pt=pt: e.activation(out=PT[pt], in_=PS[sb][:, :], func=AF.Exp, scale=scale),
                          reads=["ps%d" % sb], writes=["PT%d" % pt])

                def pv_mm(sc_):
                    pt = sc_ % 3
                    s.add("pe", lambda e, pt=pt, sc_=sc_, ob=ob: e.matmul(PS[ob][:, :], V3[:, sc_, :], PT[pt], start=(sc_ == 0), stop=(sc_ == 31)),
                          reads=["PT%d" % pt, "v"], writes=["ps%d" % ob])
                    s.add("pe", lambda e, pt=pt, sc_=sc_, ob=ob: e.matmul(PS[ob + 1][:, :], ones128, PT[pt], start=(sc_ == 0), stop=(sc_ == 31)),
                          reads=["PT%d" % pt, "ones128"], writes=["ps%d" % (ob + 1)])
                st_mm(0)
                for sc_ in range(32):
                    if sc_ + 1 < 32:
                        st_mm(sc_ + 1)
                    pv_mm(sc_)
                ys = unit % 2
                s.add("dve", lambda e, ob=ob: e.reciprocal(rec, PS[ob + 1][:, :]), reads=["ps%d" % (ob + 1)], writes=["rec"])
                s.add("dve", lambda e, ob=ob, ys=ys: e.tensor_tensor(out=yab[ys], in0=PS[ob][:, :], in1=rec, op=ALU.mult),
                      reads=["ps%d" % ob, "rec"], writes=["yab%d" % ys])
                s.add("sp", lambda e, h=h, qc=qc, ys=ys: e.dma_start(out=yloc_d[256 + h * 128:256 + (h + 1) * 128, 512 * qc:512 * (qc + 1)], in_=yab[ys]),
                      reads=["yab%d" % ys], writes=["yloc"], dma="yab%d" % ys)
                unit += 1
        s.barrier()
        if upto("p4b"):
            xchg_start(2)
        if dbg == "p3":
            s.add("sp", lambda e: e.dma_start(out=xt[0].bitcast(BF16), in_=yloc_d[256:384, :]), reads=["yloc"], writes=["dbgld"], dma="dbgld")
            dump(xt[0].bitcast(BF16)[:, 0:4096], 4096, "dbgld")
            s.barrier()
    A.off = raw_end

    x0T = A.bf16(2 * S)
    zT = A.bf16(2 * S)
    x0T3 = x0T.rearrange("p (c t) -> p c t", c=2)
    zT3 = zT.rearrange("p (c t) -> p c t", c=2)
    p2_mark = A.off
    if upto("p2a"):
        tbuf = [A.f32(S) for _ in range(4)]

        def conv_chain(cc, tb, tbn, eng, dst, dstn):
            w0 = convw[:, cc * 3 + 0:cc * 3 + 1]
            w1 = convw[:, cc * 3 + 1:cc * 3 + 2]
            w2 = convw[:, cc * 3 + 2:cc * 3 + 3]
            bb = convb[:, cc:cc + 1]
            s.add(eng, lambda e: e.tensor_scalar(tb, RAW3[:, cc, 1:4097], w1, bb, op0=ALU.mult, op1=ALU.add),
                  reads=["raw_a", "raw_d", "const7", "const8"], writes=[tbn])
            s.add(eng, lambda e: e.scalar_tensor_tensor(out=tb, in0=RAW3[:, cc, 0:4096], scalar=w0, in1=tb, op0=ALU.mult, op1=ALU.add),
                  reads=["raw_a", "raw_d", "rawpad0", tbn, "const7"], writes=[tbn])
            s.add(eng, lambda e: e.scalar_tensor_tensor(out=dst, in0=RAW3[:, cc, 2:4098], scalar=w2, in1=tb, op0=ALU.mult, op1=ALU.add),
                  reads=["raw_a", "raw_d", "rawpad1", tbn, "const7"], writes=[dstn])
        conv_chain(2, tbuf[0], "tb0", "dve", tbuf[0], "tb0")
        conv_chain(3, tbuf[1], "tb1", "dve", tbuf[1], "tb1")
        conv_chain(4, tbuf[2], "tb2", "dve", tbuf[2], "tb2")
        conv_chain(5, tbuf[3], "tb3", "dve", tbuf[3], "tb3")
        for c2 in range(2):
            s.add("dve", lambda e, c2=c2: e.tensor_tensor(out=zT3[:, c2, :], in0=tbuf[2 + c2], in1=tbuf[c2], op=ALU.mult),
                  reads=["tb%d" % c2, "tb%d" % (2 + c2)], writes=["zT"])
        conv_chain(0, tbuf[0], "tb0", "dve", x0T3[:, 0, :], "x0T")
        conv_chain(1, tbuf[1], "tb1", "dve", x0T3[:, 1, :], "x0T")
        s.barrier()
        if dbg == "p2a":
            A.off = p2_mark
            dump(zT[:, 0:8192], 8192, "zT")
            s.barrier()
    A.off = p2_mark

    if upto("p2f"):
        ZH = RAW[:, 0:32 * 768]
        ZH3 = ZH.rearrange("p (c n) -> p c n", c=32)
        R = A.bf16(32 * 512)
        R3 = R.rearrange("p (c n) -> p c n", c=32)
        dbuf = [A.bf16(2 * 32 * 128) for _ in range(2)]
        dbuf4 = [b.rearrange("p (a c n) -> p a c n", a=2, c=32) for b in dbuf]
        fw1 = A.f32(64)
        fw2 = A.f32(64)
        fw3 = A.f32(64)
        fw4 = A.f32(512)
        zemb = [A.f32(512) for _ in range(2)]
        hA = A.f32(512)
        hB = A.f32(512)
        uu = A.f32(512)
        val = A.f32(512)
        absv = A.f32(256)
        dec = [A.f32(256) for _ in range(2)]
        bsc = A.f32(4)
        scn = A.f32(2)
        sP = A.f32(256)
        sQ = A.f32(256)
        tt = [A.f32(256) for _ in range(4)]
        ysb = A.f32(256)
        ysb3 = ysb.rearrange("p (a c) -> p a c", a=2)
        ub = A.f32(512)
        yhs = [A.bf16(512) for _ in range(2)]
        dt_ = A.f32(4112) if dbg == "p2f" else None

        if upto("p4b"):
            xchg_finish(2)
            xchg_start(3)
        s.add("sp", lambda e: e.dma_start(out=fw1[0:33, :], in_=fw1_d), writes=["fw1"], dma="fw")
        s.add("sp", lambda e: e.dma_start(out=fw2[0:64, :], in_=fw2_d), writes=["fw2"], dma="fw")
        s.add("sp", lambda e: e.dma_start(out=fw3[0:64, :], in_=fw3_d), writes=["fw3"], dma="fw")
        s.add("sp", lambda e: e.dma_start(out=fw4[0:64, :], in_=fw4_d), writes=["fw4"], dma="fw")
        for cc in range(2):
            for g in range(8):
                bank = g % 2
                pst = PSB[bank][:, 0:512].rearrange("p (a t) -> p a t", a=4)

                def trz(e, cc=cc, g=g, pst=pst):
                    ins = None
                    for a in range(4):
                        ch = g * 4 + a
                        ins = e.transpose(out=pst[:, a, :], in_=zT3[:, cc, ch * 128:(ch + 1) * 128], identity=identb)
                    return ins
                s.add("pe", trz, reads=["zT", "const0"], writes=["ps%d" % bank])
                dst = ZH3[:, g * 4:(g + 1) * 4, 256 + cc * 128:256 + (cc + 1) * 128]
                s.add("dve", lambda e, dst=dst, pst=pst: e.tensor_copy(dst, pst), reads=["ps%d" % bank], writes=["ZHz"])
        for l in range(3):
            s.add("dve", lambda e, l=l: e.tensor_tensor(out=bsc[0:64, l:l + 1], in0=fvec[0:64, 0:1], in1=fvec[0:64, l + 1:l + 2], op=ALU.mult),
                  reads=["fvec"], writes=["bsc"])
        s.add("dve", lambda e: e.tensor_scalar(bsc[0:64, 0:3], bsc[0:64, 0:3], 1.0 / (2.0 * math.pi), 16.5, op0=ALU.mult, op1=ALU.add),
              reads=["bsc"], writes=["bsc"])
        s.add("dve", lambda e: e.tensor_scalar(bsc[0:64, 3:4], fvec[0:64, 0:1], 1.0 / (2.0 * math.pi), None, op0=ALU.mult),
              reads=["fvec", "bsc"], writes=["bsc"])
        frp = bsc[0:64, 3:4]
        ki = A.h32.bitcast(mybir.dt.int32)[:, A.alloc(4 * 512) // 4:][:, 0:512]
        kf = A.f32(512)

        def sin_layer(ps_ap, l, dst, rn, wn):
            s.add("dve", lambda e: e.tensor_scalar(uu[0:64, :], ps_ap, frp, bsc[0:64, l:l + 1], op0=ALU.mult, op1=ALU.add),
                  reads=rn + ["bsc"], writes=["uu"])
            s.add("dve", lambda e: e.tensor_copy(ki[0:64, :], uu[0:64, :]), reads=["uu"], writes=["ki"])
            s.add("dve", lambda e: e.tensor_copy(kf[0:64, :], ki[0:64, :]), reads=["ki"], writes=["kf"])
            s.add("dve", lambda e: e.tensor_tensor(out=uu[0:64, :], in0=uu[0:64, :], in1=kf[0:64, :], op=ALU.subtract),
                  reads=["uu", "kf"], writes=["uu"])
            s.add("dve", lambda e: e.scalar_tensor_tensor(out=uu[0:64, :], in0=uu[0:64, :], scalar=0.0, in1=uu[0:64, :], op0=ALU.is_lt, op1=ALU.add),
                  reads=["uu"], writes=["uu"])
            s.add("act", lambda e: e.activation(out=dst, in_=uu[0:64, :], func=AF.Sin, bias=negpi[0:64, :], scale=2.0 * math.pi),
                  reads=["uu", "negpi"], writes=wn)

        decay_v = decay_d.rearrange("(c p) n -> c p n", p=128)
        pstep_d = dec[0].ap[0][0]
        for j in range(8):
            zs = j % 2
            s.add("sp", lambda e, j=j, zs=zs: e.dma_start(out=zemb[zs][0:33, :], in_=zemb_d[:, 512 * j:512 * (j + 1)]),
                  writes=["zemb%d" % zs], dma="zemb%d" % zs)
            s.add("pe", lambda e, zs=zs: e.matmul(PS[2][0:64, :], fw1[0:33, :], zemb[zs][0:33, :], start=True, stop=True),
                  reads=["fw1", "zemb%d" % zs], writes=["ps2"])
            sin_layer(PS[2][0:64, :], 0, hA[0:64, :], ["ps2"], ["hA"])
            if dbg == "p2f" and j == 0:
                s.add("dve", lambda e: e.tensor_copy(dt_[0:64, 2048:2560], hA[0:64, :]), reads=["hA"], writes=["dbgtmp"])
                s.add("dve", lambda e: e.tensor_copy(dt_[64:128, 2048:2560], PS[2][0:64, :]), reads=["hA", "ps2"], writes=["dbgtmp"])
            s.add("pe", lambda e: e.matmul(PS[3][0:64, :], fw2[0:64, :], hA[0:64, :], start=True, stop=True),
                  reads=["fw2", "hA"], writes=["ps3"])
            sin_layer(PS[3][0:64, :], 1, hB[0:64, :], ["ps3"], ["hB"])
            if dbg == "p2f" and j == 0:
                s.add("dve", lambda e: e.tensor_copy(dt_[0:64, 2560:3072], hB[0:64, :]), reads=["hB"], writes=["dbgtmp"])
            s.add("pe", lambda e: e.matmul(PS[2][0:64, :], fw3[0:64, :], hB[0:64, :], start=True, stop=True),
                  reads=["fw3", "hB"], writes=["ps2"])
            if dbg == "p2f" and j == 0:
                s.add("dve", lambda e: e.tensor_copy(dt_[64:128, 2560:3072], PS[2][0:64, :]), reads=["ps2"], writes=["dbgtmp"])
            sin_layer(PS[2][0:64, :], 2, hA[0:64, :], ["ps2"], ["hA"])
            if dbg == "p2f" and j == 0:
                s.add("dve", lambda e: e.tensor_copy(dt_[64:128, 3072:3584], uu[0:64, :]), reads=["uu"], writes=["dbgtmp"])
                s.add("dve", lambda e: e.tensor_copy(dt_[0:64, 3072:3584], hA[0:64, :]), reads=["hA"], writes=["dbgtmp"])
            for i in range(4):
                ch = 4 * j + i
                ds_ = ch % 2
                s.add("sp", lambda e, ch=ch, ds_=ds_: e.dma_start(out=dec[ds_], in_=decay_v[ch]),
                      writes=["dec%d" % ds_], dma="dec%d" % ds_)
                s.add("pe", lambda e, i=i: e.matmul(PS[3][:, :], hA[0:64, i * 128:(i + 1) * 128], fw4[0:64, :], start=True, stop=True),
                      reads=["hA", "fw4"], writes=["ps3"])
                dbc = _ap(dec[ds_], 0, [[pstep_d, 128], [0, 2], [1, 256]])
                s.add("dve", lambda e, dbc=dbc: e.tensor_tensor(out=val.rearrange("p (a c) -> p a c", a=2),
                                                                 in0=PS[3][:, :].rearrange("p (a c) -> p a c", a=2), in1=dbc, op=ALU.mult),
                      reads=["ps3", "dec%d" % ds_], writes=["val"])
                if ch == 0:
                    s.add("dve", lambda e: e.memset(val[0:1, 256:512], 0.0), reads=["val"], writes=["val"])
                s.add("pool", lambda e, ch=ch: e.tensor_tensor(out=ZH3[:, ch, 0:256], in0=val[:, 0:256], in1=val[:, 256:512], op=ALU.add),
                      reads=["val"], writes=["ZHp"])
                s.add("pool", lambda e, ch=ch: e.tensor_tensor(out=ZH3[:, ch, 512:768], in0=val[:, 0:256], in1=val[:, 256:512], op=ALU.subtract),
                      reads=["val"], writes=["ZHm"])
                s.add("act", lambda e: e.activation(out=val, in_=val, func=AF.Abs), reads=["val", "ZHp", "ZHm"], writes=["val"])
                s.add("dve", lambda e: e.tensor_tensor(out=absv, in0=val[:, 0:256], in1=val[:, 256:512], op=ALU.add),
                      reads=["val"], writes=["absv"])
                for c2 in range(2):
                    s.add("pe", lambda e, c2=c2, ch=ch: e.matmul(PS[4 + c2][:, 0:1], absv[:, c2 * 128:(c2 + 1) * 128], onescol,
                                                                  start=(ch == 0), stop=(ch == 31)),
                          reads=["absv", "onescol"], writes=["ps%d" % (4 + c2)])
        for c2 in range(2):
            s.add("dve", lambda e, c2=c2: e.reciprocal(scn[:, c2:c2 + 1], PS[4 + c2][:, 0:1]), reads=["ps%d" % (4 + c2)], writes=["scn"])
        s.add("dve", lambda e: e.tensor_scalar(scn, scn, 2.0 / 8192.0, None, op0=ALU.mult), reads=["scn"], writes=["scn"])

        ZHALL = ["ZHz", "ZHp", "ZHm"]
        for i in range(32):
            sl = i % 2
            dn = "dbuf%d" % sl
            s.add("sp", lambda e, i=i, sl=sl: e.dma_start(out=dbuf[sl], in_=dftF_d[i]), writes=[dn], dma=dn)
            mm_group(PS[0][:, :], lambda k, sl=sl: dbuf4[sl][:, 0, k, :], lambda k: ZH3[:, k, 0:512], 32, [dn] + ZHALL, "ps0")
            mm_group(PS[1][:, :], lambda k, sl=sl: dbuf4[sl][:, 1, k, :], lambda k: ZH3[:, k, 256:768], 32, [dn] + ZHALL, "ps1")
            s.add("act", lambda e: e.activation(out=sP, in_=PS[0][:, 0:256], func=AF.Copy), reads=["ps0"], writes=["sP"])
            s.add("act", lambda e: e.activation(out=sQ, in_=PS[1][:, 256:512], func=AF.Copy), reads=["ps1"], writes=["sQ"])
            s.add("dve", lambda e: e.tensor_tensor(out=tt[0], in0=PS[0][:, 256:512], in1=sP, op=ALU.mult), reads=["ps0", "sP"], writes=["tt0"])
            s.add("dve", lambda e: e.tensor_tensor(out=tt[1], in0=PS[1][:, 0:256], in1=sQ, op=ALU.mult), reads=["ps1", "sQ"], writes=["tt1"])
            s.add("dve", lambda e: e.tensor_tensor(out=tt[2], in0=PS[0][:, 256:512], in1=sQ, op=ALU.mult), reads=["ps0", "sQ"], writes=["tt2"])
            s.add("dve", lambda e: e.tensor_tensor(out=tt[3], in0=PS[1][:, 0:256], in1=sP, op=ALU.mult), reads=["ps1", "sP"], writes=["tt3"])
            s.add("pool", lambda e, i=i: e.tensor_tensor(out=R3[:, i, 0:256], in0=tt[0], in1=tt[1], op=ALU.subtract), reads=["tt0", "tt1"], writes=["R"])
            s.add("pool", lambda e, i=i: e.tensor_tensor(out=R3[:, i, 256:512], in0=tt[2], in1=tt[3], op=ALU.add), reads=["tt2", "tt3"], writes=["R"])
        if dbg == "p2f":
            s.barrier()
            s.add("dve", lambda e: e.tensor_copy(dt_[:, 0:1024].rearrange("p (c n) -> p c n", c=4), ZH3[:, 0:4, 0:256]), reads=["ZHp"], writes=["dbgtmp"])
            s.add("dve", lambda e: e.tensor_copy(dt_[:, 1024:2048].rearrange("p (c n) -> p c n", c=4), ZH3[:, 0:4, 512:768]), reads=["ZHm"], writes=["dbgtmp"])
            s.add("dve", lambda e: e.tensor_copy(dt_[:, 3584:4096], R3[:, 5, :]), reads=["R"], writes=["dbgtmp"])
            s.add("dve", lambda e: e.tensor_copy(dt_[:, 3072:3584], R3[:, 0, :]), reads=["R"], writes=["dbgtmp"])
            s.add("dve", lambda e: e.tensor_copy(dt_[:, 4096:4098], scn), reads=["scn"], writes=["dbgtmp"])
            s.add("sp", lambda e: e.dma_start(out=dbg_d[:, 0:4112], in_=dt_), reads=["dbgtmp"], writes=["dbgout"], dma="dbg")
            s.barrier()
        if upto("p4b"):
            xchg_finish(3)
        ysbs = [ysb, A.f32(256)]
        ysb3s = [y.rearrange("p (a c) -> p a c", a=2) for y in ysbs]

        def inv_issue(tch):
            sl = tch % 2
            dn = "dbuf%d" % sl
            ib = tch % 2
            s.add("sp", lambda e, tch=tch, sl=sl: e.dma_start(out=dbuf[sl], in_=dftI_d[tch]), writes=[dn], dma=dn)

            def inv(e, sl=sl, ib=ib):
                ins = None
                for k in range(32):
                    ins = e.matmul(PS[ib][:, 0:256], dbuf4[sl][:, 0, k, :], R3[:, k, 0:256], start=(k == 0), stop=False)
                    ins = e.matmul(PS[ib][:, 0:256], dbuf4[sl][:, 1, k, :], R3[:, k, 256:512], start=False, stop=(k == 31))
                return ins
            s.add("pe", inv, reads=[dn, "R"], writes=["ps%d" % ib])
            s.add("act", lambda e, ib=ib: e.activation(out=ysbs[ib], in_=PS[ib][:, 0:256], func=AF.Copy), reads=["ps%d" % ib], writes=["ysb%d" % ib])

        def inv_transposes(tch):
            ib = tch % 2
            a = tch % 4
            tb0 = 2 + 2 * ((tch // 4) % 2)
            for c2 in range(2):
                s.add("pe", lambda e, c2=c2, a=a, ib=ib, tb0=tb0: e.transpose(out=PS[tb0 + c2][:, a * 128:(a + 1) * 128], in_=ysb3s[ib][:, c2, :], identity=identf),
                      reads=["ysb%d" % ib, "const1"], writes=["ps%d" % (tb0 + c2)])

        ubs = [ub, A.f32(512)]

        def epilogue(tg):
            tb0 = 2 + 2 * (tg % 2)
            for c2 in range(2):
                tsl = slice(512 * tg, 512 * (tg + 1))
                yb = (tg * 2 + c2) % 2
                ubx = ubs[c2]
                un = "ub%d" % c2
                s.add("pool", lambda e, c2=c2, tsl=tsl, ubx=ubx: e.tensor_scalar(ubx, zT3[:, c2, tsl], hyb[:, c2:c2 + 1], None, op0=ALU.mult),
                      reads=["zT", "const9"], writes=[un])
                s.add("dve", lambda e, c2=c2, ubx=ubx, tb0=tb0: e.scalar_tensor_tensor(out=ubx, in0=PS[tb0 + c2][:, :], scalar=scn[:, c2:c2 + 1], in1=ubx,
                                                                                  op0=ALU.mult, op1=ALU.add),
                      reads=["ps%d" % (tb0 + c2), "scn", un], writes=[un])
                s.add("dve", lambda e, c2=c2, tsl=tsl, yb=yb, ubx=ubx: e.tensor_tensor(out=yhs[yb], in0=ubx, in1=x0T3[:, c2, tsl], op=ALU.mult),
                      reads=[un, "x0T"], writes=["yhs%d" % yb])
                s.add("sp", lambda e, c2=c2, tsl=tsl, yb=yb: e.dma_start(out=yloc_d[c2 * 128:(c2 + 1) * 128, tsl], in_=yhs[yb]),
                      reads=["yhs%d" % yb], writes=["yloc"], dma="yhs%d" % yb)
        if upto("p2"):
            inv_issue(0)
            for tch in range(32):
                if tch + 1 < 32:
                    inv_issue(tch + 1)
                inv_transposes(tch)
                if tch % 4 == 3:
                    epilogue(tch // 4)
        s.barrier()
        if dbg == "p2":
            s.add("sp", lambda e: e.dma_start(out=dbuf[0][:, 0:4096], in_=yloc_d[0:128, :]), reads=["yloc"], writes=["dbgld2"], dma="dbgld")
            tmpf = dbuf[1].bitcast(F32)
            s.add("dve", lambda e: e.tensor_copy(tmpf, dbuf[0][:, 0:4096]), reads=["dbgld2"], writes=["dbgtmp"])
            s.add("sp", lambda e: e.dma_start(out=dbg_d[:, 0:4096], in_=tmpf), reads=["dbgtmp"], writes=["dbgout"], dma="dbg")
            s.barrier()
    A.off = base_mark

    if upto("p4b"):
        xchg_start(0)
        G = A.bf16(32 * 1024)
        G3 = G.rearrange("p (k t) -> p k t", k=32)
        mT3 = G3[:, 0:16, :]
        mT = G[:, 0:16 * 1024]
        mt_end = A.off - 16 * 1024 * 2
        hTo = A.bf16(16 * 1024)
        hTo3 = hTo.rearrange("p (k t) -> p k t", k=16)
        YT = A.bf16(16 * 1024)
        YT3 = YT.rearrange("p (k t) -> p k t", k=16)
        p4_mark = A.off
        h2T_off = 175 * 1024
        h2T = A.h16[:, h2T_off // 2:h2T_off // 2 + 16 * 1024]
        h2T3 = h2T.rearrange("p (k t) -> p k t", k=16)
        xt2 = [A.f32(D) for _ in range(2)]
        xs2 = [A.bf16(D) for _ in range(2)]
        junk2 = A.bf16(D)
        gbc2 = A.f32(D)
        WBLK = 256
        wg = [A.bf16(2 * 16 * WBLK) for _ in range(2)]
        wg4 = [w.rearrange("p (a k n) -> p a k n", a=2, k=16) for w in wg]
        s.add("sp", lambda e: e.dma_start(out=gbc2, in_=g1_d.partition_broadcast(128)), writes=["gbc2"], dma="gbc")
        xo_v = xo_d.rearrange("(n p) d -> n p d", p=128)
        for t in range(8):
            sl = t % 2
            s.add("sp", lambda e, sl=sl, t=t: e.dma_start(out=xt2[sl], in_=xo_v[t]), writes=["oxt%d" % sl], dma="oxt%d" % sl)
            rms_scale(xt2[sl], "oxt%d" % sl, xs2[sl], "oxs%d" % sl, gbc2, "gbc2", 32 + t, junk2)
            transposes_to(xs2[sl], "oxs%d" % sl, hTo3, t * 128, "ohT")
        wgate_v = wgate_d.rearrange("(k p) n -> p k n", p=128)
        wbh_v = wbh_d.rearrange("(k p) n -> p k n", p=128)
        wba_v = wba_d.rearrange("(k p) n -> p k n", p=128)
        NJB = D // WBLK

        def load_gblock(jb):
            sl = jb % 2
            wn = "wg%d" % sl
            c0 = jb * WBLK
            s.add("pool", lambda e, sl=sl, c0=c0: e.dma_start(out=wg4[sl][:, 0, :, :], in_=wgate_v[:, :, c0:c0 + WBLK]), writes=[wn], dma=wn)
            s.add("pool", lambda e, sl=sl, c0=c0: e.dma_start(out=wg4[sl][:, 1, :, :], in_=wgate_v[:, :, D + c0:D + c0 + WBLK]), writes=[wn], dma=wn)
        load_gblock(0)
        load_gblock(1)
        xchg_finish(0)
        xchg_start(1)
        cntA = 0
        for jb in range(NJB):
            sl = jb % 2
            wn = "wg%d" % sl
            for sub in range(WBLK // 128):
                jc = jb * (WBLK // 128) + sub
                so = sub * 128
                for tc_ in range(2):
                    tsl = slice(512 * tc_, 512 * (tc_ + 1))
                    par = cntA % 3
                    cntA += 1
                    b0, b1 = 2 + 2 * par, 3 + 2 * par
                    mm_group(PS[b0][:, :], lambda k, sl=sl, so=so: wg4[sl][:, 0, k, so:so + 128], lambda k, tsl=tsl: hTo3[:, k, tsl], 16, [wn, "ohT_a", "ohT_d"], "ps%d" % b0)
                    mm_group(PS[b1][:, :], lambda k, sl=sl, so=so: wg4[sl][:, 1, k, so:so + 128], lambda k, tsl=tsl: hTo3[:, k, tsl], 16, [wn, "ohT_a", "ohT_d"], "ps%d" % b1)
                    s.add("act", lambda e, jc=jc, b0=b0, tsl=tsl: e.activation(out=G3[:, jc, tsl], in_=PS[b0][:, :], func=AF.Sigmoid, bias=bgate[:, jc:jc + 1], scale=1.0),
                          reads=["ps%d" % b0, "const10"], writes=["gh%d" % jc])
                    s.add("act", lambda e, jc=jc, b1=b1, tsl=tsl: e.activation(out=G3[:, 16 + jc, tsl], in_=PS[b1][:, :], func=AF.Sigmoid, bias=bgate[:, 16 + jc:17 + jc], scale=1.0),
                          reads=["ps%d" % b1, "const10"], writes=["ga%d" % jc])
            if jb + 2 < NJB:
                load_gblock(jb + 2)
        xchg_finish(1)
        for g in range(4):
            kind, c2 = g // 2, g % 2
            base = kind * 8 + c2
            dst = _ap(YT, base * 1024, [[YT.ap[0][0], 128], [2 * 1024, 4], [1, 1024]])
            s.add("sp", lambda e, g=g, dst=dst: e.dma_start(out=dst, in_=ystage_d[g].rearrange("p (q t) -> p q t", q=4)),
                  reads=["ystage%d" % g], writes=["YT"], dma="YT")
        s.barrier()
        A.off = p4_mark
        wb = [A.bf16(2 * 8 * WBLK) for _ in range(2)]
        wb4 = [w.rearrange("p (a k n) -> p a k n", a=2, k=8) for w in wb]
        mtmp = [[A.f32(512) for _ in range(2)] for _ in range(2)]

        def load_bblock(jb):
            sl = jb % 2
            wn = "wb%d" % sl
            c0 = jb * WBLK
            s.add("pool", lambda e, sl=sl, c0=c0: e.dma_start(out=wb4[sl][:, 0, :, :], in_=wbh_v[:, :, c0:c0 + WBLK]), writes=[wn], dma=wn)
            s.add("pool", lambda e, sl=sl, c0=c0: e.dma_start(out=wb4[sl][:, 1, :, :], in_=wba_v[:, :, c0:c0 + WBLK]), writes=[wn], dma=wn)
        load_bblock(0)
        cntB = 0
        for jb in range(NJB):
            sl = jb % 2
            wn = "wb%d" % sl
            if jb + 1 < NJB:
                load_bblock(jb + 1)
            for sub in range(WBLK // 128):
                jc = jb * (WBLK // 128) + sub
                so = sub * 128
                for tc_ in range(2):
                    tsl = slice(512 * tc_, 512 * (tc_ + 1))
                    par = cntB % 2
                    cntB += 1
                    b2, b3 = (2, 3) if par == 0 else (4, 5)
                    m1_, m2_ = mtmp[par]
                    pn = "_%d" % par
                    mm_group(PS[b2][:, :], lambda k, sl=sl, so=so: wb4[sl][:, 0, k, so:so + 128], lambda k, tsl=tsl: YT3[:, k, tsl], 8, [wn, "YT"], "ps%d" % b2)
                    mm_group(PS[b3][:, :], lambda k, sl=sl, so=so: wb4[sl][:, 1, k, so:so + 128], lambda k, tsl=tsl: YT3[:, 8 + k, tsl], 8, [wn, "YT"], "ps%d" % b3)
                    s.add("dve", lambda e, b2=b2, m1_=m1_, jc=jc, tsl=tsl: e.tensor_tensor(out=m1_, in0=PS[b2][:, :], in1=G3[:, jc, tsl], op=ALU.mult),
                          reads=["ps%d" % b2, "gh%d" % jc], writes=["m1" + pn])
                    s.add("dve", lambda e, b3=b3, m2_=m2_, jc=jc, tsl=tsl: e.tensor_tensor(out=m2_, in0=PS[b3][:, :], in1=G3[:, 16 + jc, tsl], op=ALU.mult),
                          reads=["ps%d" % b3, "ga%d" % jc], writes=["m2" + pn])
                    s.add("pool", lambda e, jc=jc, tsl=tsl, m1_=m1_, m2_=m2_: e.tensor_tensor(out=mT3[:, jc, tsl], in0=m1_, in1=m2_, op=ALU.add),
                          reads=["m1" + pn, "m2" + pn], writes=["gh%d" % jc])
        s.barrier()
        if dbg == "p4b":
            A.off = p4_mark
            dump(mT[:, 0:8192], 8192, "mT")
            s.barrier()
    if upto("p4"):
        A.off = mt_end
        x1 = A.f32(8 * D)
        x13 = x1.rearrange("p (t d) -> p t d", t=8)
        wo = [A.bf16(16 * 512) for _ in range(2)]
        wo3 = [w.rearrange("p (k n) -> p k n", k=16) for w in wo]
        xs3 = [A.bf16(D) for _ in range(2)]
        junk3 = A.bf16(D)
        gbc3 = A.f32(D)
        assert A.off <= h2T_off
        s.add("sp", lambda e: e.dma_start(out=gbc3, in_=g2_d.partition_broadcast(128)), writes=["gbc3"], dma="gbc")
        for t in range(8):
            s.add("sp", lambda e, t=t: e.dma_start(out=x13[:, t, :], in_=xo_v[t]), writes=["x1_%d" % t], dma="x1ld")
        wout_v = wout_d.rearrange("(k p) n -> p k n", p=128)
        for db in range(4):
            sl = db % 2
            wn = "wo%d" % sl
            s.add("pool", lambda e, sl=sl, db=db: e.dma_start(out=wo3[sl], in_=wout_v[:, :, 512 * db:512 * (db + 1)]), writes=[wn], dma=wn)
            for t in range(8):
                bank = 2 + (t % 2)
                mm_group(PS[bank][:, :], lambda k, t=t: mT3[:, k, 128 * t:128 * (t + 1)], lambda k, sl=sl: wo3[sl][:, k, :], 16, [wn, "mT"], "ps%d" % bank)
                dsl = slice(512 * db, 512 * (db + 1))
                s.add("dve", lambda e, t=t, dsl=dsl, bank=bank: e.tensor_tensor(out=x13[:, t, dsl], in0=PS[bank][:, :], in1=x13[:, t, dsl], op=ALU.add),
                      reads=["ps%d" % bank, "x1_%d" % t], writes=["x1_%d" % t])
        x1_v = x1_d.rearrange("(n p) d -> n p d", p=128)
        for t in range(8):
            s.add("sp", lambda e, t=t: e.dma_start(out=x1_v[t], in_=x13[:, t, :]), reads=["x1_%d" % t], writes=["x1d"], dma="x1st")
            sl = t % 2
            rms_scale(x13[:, t, :], "x1_%d" % t, xs3[sl], "fxs%d" % sl, gbc3, "gbc3", 48 + t, junk3)
            transposes_to(xs3[sl], "fxs%d" % sl, h2T3, t * 128, "fhT", tbanks=(0, 1, 4, 5, 6, 7))
        s.barrier()
        if dbg == "p4":
            A.off = mt_end + 8 * D * 4
            dump(x1[:, 0:8192], 8192, "x1_3")
            s.barrier()
    if upto("p5"):
        A.off = base_mark
        aT = A.bf16(NFF * 1024)
        aT3 = aT.rearrange("p (f t) -> p f t", f=NFF)
        p5_mark = A.off
        wgu = [A.bf16(2 * 16 * 256) for _ in range(2)]
        wgu4 = [w.rearrange("p (a k n) -> p a k n", a=2, k=16) for w in wgu]
        sgl = [A.f32(512) for _ in range(2)]
        assert A.off <= h2T_off
        wfg_v = wfg_d.rearrange("(k p) n -> p k n", p=128)
        wfu_v = wfu_d.rearrange("(k p) n -> p k n", p=128)
        for fb in range(DFF // 256):
            sl = fb % 2
            wn = "wgu%d" % sl
            c0 = fb * 256
            s.add("pool", lambda e, sl=sl, c0=c0: e.dma_start(out=wgu4[sl][:, 0, :, :], in_=wfg_v[:, :, c0:c0 + 256]), writes=[wn], dma=wn)
            s.add("pool", lambda e, sl=sl, c0=c0: e.dma_start(out=wgu4[sl][:, 1, :, :], in_=wfu_v[:, :, c0:c0 + 256]), writes=[wn], dma=wn)
            for sub in range(2):
                fc = fb * 2 + sub
                so = sub * 128
                for tc_ in range(2):
                    tsl = slice(512 * tc_, 512 * (tc_ + 1))
                    u = (fc * 2 + tc_) % 2
                    bg, bu = 2 + 2 * u, 3 + 2 * u
                    mm_group(PS[bg][:, :], lambda k, sl=sl, so=so: wgu4[sl][:, 0, k, so:so + 128], lambda k, tsl=tsl: h2T3[:, k, tsl], 16, [wn, "fhT_a", "fhT_d"], "ps%d" % bg)
                    mm_group(PS[bu][:, :], lambda k, sl=sl, so=so: wgu4[sl][:, 1, k, so:so + 128], lambda k, tsl=tsl: h2T3[:, k, tsl], 16, [wn, "fhT_a", "fhT_d"], "ps%d" % bu)
                    s.add("act", lambda e, u=u, bg=bg: e.activation(out=sgl[u], in_=PS[bg][:, :], func=AF.Silu), reads=["ps%d" % bg], writes=["sgl%d" % u])
                    s.add("dve", lambda e, u=u, bu=bu, fc=fc, tsl=tsl: e.tensor_tensor(out=aT3[:, fc, tsl], in0=PS[bu][:, :], in1=sgl[u], op=ALU.mult),
                          reads=["ps%d" % bu, "sgl%d" % u], writes=["aT"])
        s.barrier()
        A.off = p5_mark
        wd = [A.bf16(NFF * 512) for _ in range(2)]
        wd3 = [w.rearrange("p (f n) -> p f n", f=NFF) for w in wd]
        xr = [A.f32(512) for _ in range(2)]
        ot = [A.f32(512) for _ in range(2)]
        wfd_v = wfd_d.rearrange("(f p) n -> p f n", p=128)
        out_v = out_d.rearrange("(n p) d -> n p d", p=128)
        cnt = 0
        for db in range(4):
            sl = db % 2
            wn = "wd%d" % sl
            for f0 in (0, 22):
                s.add("pool", lambda e, sl=sl, f0=f0, db=db: e.dma_start(out=wd3[sl][:, f0:f0 + 22, :], in_=wfd_v[:, f0:f0 + 22, 512 * db:512 * (db + 1)]), writes=[wn], dma=wn)
            for t in range(8):
                bank = 2 + (t % 2)
                u = cnt % 2
                cnt += 1
                dsl = slice(512 * db, 512 * (db + 1))
                s.add("sp", lambda e, t=t, dsl=dsl, u=u: e.dma_start(out=xr[u], in_=x1_v[t][:, dsl]), reads=["x1d"], writes=["xr%d" % u], dma="xr%d" % u)
                mm_group(PS[bank][:, :], lambda k, t=t: aT3[:, k, 128 * t:128 * (t + 1)], lambda k, sl=sl: wd3[sl][:, k, :], NFF, [wn, "aT"], "ps%d" % bank)
                s.add("dve", lambda e, u=u, bank=bank: e.tensor_tensor(out=ot[u], in0=PS[bank][:, :], in1=xr[u], op=ALU.add),
                      reads=["ps%d" % bank, "xr%d" % u], writes=["ot%d" % u])
                s.add("sp", lambda e, t=t, dsl=dsl, u=u: e.dma_start(out=out_v[t][:, dsl], in_=ot[u]), reads=["ot%d" % u], writes=["outd"], dma="ot%d" % u)
    s.barrier()

    dma_keys = list(s.dma_cnt.keys())
    eng_sems = {e: nc.alloc_semaphore("sem_" + e) for e in ENGS}
    dma_sems = {k: nc.alloc_semaphore("dsem_%d" % i) for i, k in enumerate(dma_keys)}
    with nc.Block() as block:
        s.emit_all(block, eng_sems, dma_sems)
    return nc


_CONST_CACHE = {}


def _constants():
    if _CONST_CACHE:
        return _CONST_CACHE
    bf = ml_dtypes.bfloat16
    c = {}
    c["identb"] = np.eye(128, dtype=np.float32).astype(bf)
    c["identf"] = np.eye(128, dtype=np.float32)
    c["onesdiv"] = np.full((128, 128), 1.0 / 128.0, dtype=np.float32).astype(bf)
    perm = np.zeros((128, 128), dtype=np.float32)
    for d in range(128):
        partner = d + 32 if (d % 64) < 32 else d - 32
        perm[partner, d] = 1.0
    c["perm"] = perm.astype(bf)
    inv = (10000.0 ** (-np.arange(0, 64, 2, dtype=np.float32) / 64.0)).astype(np.float32)
    pos = np.arange(64, dtype=np.float32)
    ropec = np.zeros((128, 64), dtype=np.float32)
    ropes = np.zeros((128, 64), dtype=np.float32)
    for d in range(128):
        ang = (pos * inv[d % 32]).astype(np.float32)
        ropec[d] = np.cos(ang)
        sgn = -1.0 if (d % 64) < 32 else 1.0
        ropes[d] = sgn * np.sin(ang)
    c["ropec"] = ropec
    c["ropes"] = ropes
    L = S
    posf = np.arange(L, dtype=np.float32)
    t = (posf / np.float32(L - 1)).astype(np.float32)
    fb = np.linspace(1e-4, 15, 16, dtype=np.float32)
    ang = ((np.float32(2.0 * math.pi) * posf / np.float32(L))[:, None] * fb[None, :]).astype(np.float32)
    zemb = np.concatenate([t[:, None], np.cos(ang), -np.sin(ang)], axis=-1).astype(np.float32)
    c["zemb"] = np.ascontiguousarray(zemb.T)
    max_decay = math.log(1e-2) / 0.3
    min_decay = math.log(1e-2) / 1.5
    deltas = np.abs(np.linspace(min_decay, max_decay, 1024, dtype=np.float32))
    c["decay_full"] = np.exp(-t[:, None] * deltas[None, :]).astype(np.float32)
    m = np.arange(S, dtype=np.int64)[:, None]
    f = np.arange(S, dtype=np.int64)[None, :]
    ph = (m * (2 * f + 1)) % (2 * 8192)
    th = ph.astype(np.float64) * (2.0 * math.pi / (2 * 8192))
    C = np.cos(th).astype(np.float32).astype(bf)
    Sn = np.sin(th).astype(np.float32).astype(bf)
    def fwd_layout(M):
        return M.reshape(32, 128, 32, 128).transpose(2, 1, 0, 3)
    def inv_layout(M):
        return M.reshape(32, 128, 32, 128).transpose(0, 3, 2, 1)
    c["dftF"] = np.ascontiguousarray(np.stack([fwd_layout(C), fwd_layout(Sn)], axis=2)).reshape(32, 128, 2 * 32 * 128)
    c["dftI"] = np.ascontiguousarray(np.stack([inv_layout(C), inv_layout(Sn)], axis=2)).reshape(32, 128, 2 * 32 * 128)
    _CONST_CACHE.update(c)
    return _CONST_CACHE


def _chunkcols(v):
    v = np.asarray(v, dtype=np.float32)
    return np.ascontiguousarray(v.reshape(-1, 128).T)


def _core_inputs(inp, b, r):
    c = _constants()
    f32 = np.float32
    w_in = inp["w_in"][0]
    hsl = lambda base: slice(base + 256 * r, base + 256 * (r + 1))
    kvh = r // 2
    cols = np.concatenate([
        np.arange(0 + 256 * r, 0 + 256 * (r + 1)),
        np.arange(1024 + 256 * r, 1024 + 256 * (r + 1)),
        np.arange(2048 + 256 * r, 2048 + 256 * (r + 1)),
        np.arange(3072 + 256 * r, 3072 + 256 * (r + 1)),
        np.arange(4096 + 128 * kvh, 4096 + 128 * (kvh + 1)),
        np.arange(4352 + 128 * kvh, 4352 + 128 * (kvh + 1)),
    ])
    m = {}
    m["x"] = np.ascontiguousarray(inp["x"][b])
    m["xo"] = np.ascontiguousarray(inp["x"][b, 1024 * r:1024 * (r + 1)])
    m["wmix"] = np.ascontiguousarray(w_in[:, cols])
    m["wgate"] = np.ascontiguousarray(w_in[:, 4608:8704])
    m["bgate"] = _chunkcols(inp["b_gate"][0])
    m["g1"] = np.ascontiguousarray(inp["mix_norm_g"][0])
    cw = inp["hy_conv_w"][0]
    cb = inp["hy_conv_b"][0]
    convw = np.zeros((128, 18), dtype=f32)
    convb = np.zeros((128, 6), dtype=f32)
    for cc in range(6):
        base = (cc // 2) * 1024 + 256 * r + (cc % 2) * 128
        for j in range(3):
            convw[:, cc * 3 + j] = cw[j, base:base + 128]
        convb[:, cc] = cb[base:base + 128]
    m["convw"] = convw
    m["convb"] = convb
    m["fw1"] = np.ascontiguousarray(inp["flt_w1"][0])
    m["fw2"] = np.ascontiguousarray(inp["flt_w2"][0])
    m["fw3"] = np.ascontiguousarray(inp["flt_w3"][0])
    w4 = inp["flt_w4"][0]
    m["fw4"] = np.ascontiguousarray(np.concatenate([w4[:, hsl(0)], w4[:, hsl(1024)]], axis=1))
    m["fvec"] = np.ascontiguousarray(np.stack([inp["flt_freq"][0], inp["flt_b1"][0], inp["flt_b2"][0], inp["flt_b3"][0]], axis=1))
    m["hybias"] = _chunkcols(inp["hy_bias"][0][hsl(0)])
    m["gqk"] = np.ascontiguousarray(np.stack([inp["q_norm_g"][0], inp["k_norm_g"][0]], axis=1))
    m["wbh"] = inp["w_br_hyena"][0]
    m["wba"] = inp["w_br_attn"][0]
    m["wout"] = inp["w_out"][0]
    m["g2"] = np.ascontiguousarray(inp["ffn_norm_g"][0])
    m["wfg"] = inp["w_ffn_gate"][0]
    m["wfu"] = inp["w_ffn_up"][0]
    m["wfd"] = inp["w_ffn_down"][0]
    for k in ("identb", "identf", "onesdiv", "perm", "ropec", "ropes", "zemb", "dftF", "dftI"):
        m[k] = c[k]
    m["decay"] = np.ascontiguousarray(c["decay_full"][:, 256 * r:256 * (r + 1)])
    out = {}
    for k, v in m.items():
        v = np.asarray(v)
        if k in _SHAPES and tuple(v.shape) != _SHAPES[k]:
            v = np.zeros(_SHAPES[k], dtype=v.dtype)
        out[k] = np.ascontiguousarray(v)
    return out


_NC_CACHE = {}


def kernel(**inputs):
    inp = {k: np.asarray(v) for k, v in inputs.items()}
    key = _DEBUG["stop"]
    if key not in _NC_CACHE:
        _NC_CACHE[key] = build_program()
    nc = _NC_CACHE[key]
    in_maps = [_core_inputs(inp, c // 4, c % 4) for c in range(8)]
    res = run_bass_kernel_spmd(nc, in_maps, core_ids=list(range(8)))
    if key is not None:
        return [r["dbg"] for r in res.results]
    out = np.zeros((2, S, D), dtype=np.float32)
    for c in range(8):
        b, r = c // 4, c % 4
        out[b, 1024 * r:1024 * (r + 1)] = res.results[c]["out"]
    return out
```

```python
import math
import numpy as np
import ml_dtypes
import concourse.bass as bass
import concourse.mybir as mybir
from concourse.bass_utils import run_bass_kernel_spmd

F32 = mybir.dt.float32
BF16 = mybir.dt.bfloat16
AF = mybir.ActivationFunctionType
ALU = mybir.AluOpType

D = 2048
S = 4096
DFF = 5632
EPS = 1e-6
NFF = DFF // 128
ENGS = ["pe", "act", "dve", "pool", "sp"]
_DEBUG = {"stop": None}
_RID = {}
_SHAPES = {}


class _Op:
    __slots__ = ("eng", "emit", "deps", "dma_deps", "is_dma", "semkey", "signaled", "sig_idx", "inc")


class Sched:
    def __init__(self):
        self.ops = {e: [] for e in ENGS}
        self.lastw = {}
        self.readers = {}
        self.dma_cnt = {}

    def add(self, eng, emit, reads=(), writes=(), dma=None, inc=16):
        op = _Op()
        op.inc = inc
        op.eng = eng
        op.emit = emit
        op.is_dma = dma is not None
        op.semkey = dma
        op.signaled = False
        op.sig_idx = 0
        deps = {}
        same_raw = set()
        for r in reads:
            w = self.lastw.get(r)
            if w is not None:
                deps[id(w)] = w
                if w.eng == eng and eng != "pe" and not w.is_dma:
                    same_raw.add(id(w))
        for wn in writes:
            w = self.lastw.get(wn)
            if w is not None:
                deps[id(w)] = w
            for rd in self.readers.get(wn, ()):
                deps[id(rd)] = rd
        for r in reads:
            self.readers.setdefault(r, []).append(op)
        for wn in writes:
            self.lastw[wn] = op
            self.readers[wn] = []
        op.deps = []
        op.dma_deps = {}
        for d in deps.values():
            if d is op:
                continue
            if d.is_dma:
                op.dma_deps[d.semkey] = self.dma_cnt[d.semkey]
            elif d.eng != eng or op.is_dma or id(d) in same_raw or eng != "pe":
                op.deps.append(d)
                d.signaled = True
        if op.is_dma:
            self.dma_cnt[dma] = self.dma_cnt.get(dma, 0) + inc
        self.ops[eng].append(op)
        return op

    def barrier(self):
        lasts = {}
        for e in ENGS:
            for op in reversed(self.ops[e]):
                if not op.is_dma and op.emit is not None:
                    lasts[e] = op
                    break
        for e in ENGS:
            op = _Op()
            op.inc = 0
            op.eng = e
            op.emit = None
            op.is_dma = False
            op.semkey = None
            op.signaled = False
            op.sig_idx = 0
            op.deps = []
            for e2, l in lasts.items():
                if e2 != e:
                    op.deps.append(l)
                    l.signaled = True
            op.dma_deps = dict(self.dma_cnt)
            self.ops[e].append(op)

    def emit_all(self, block, eng_sems, dma_sems):
        for e in ENGS:
            c = 0
            for op in self.ops[e]:
                if op.signaled and not op.is_dma:
                    c += 1
                    op.sig_idx = c

        def make(engname):
            def fn(e):
                waited = {}
                if engname == "sp":
                    _RID["v"] = e.snap(e.partition_id() % 4, min_val=0, max_val=3)
                for op in self.ops[engname]:
                    for d in op.deps:
                        key = ("e", d.eng)
                        if waited.get(key, 0) < d.sig_idx:
                            e.wait_ge(eng_sems[d.eng], d.sig_idx)
                            waited[key] = d.sig_idx
                    for k, v in op.dma_deps.items():
                        key = ("d", k)
                        if waited.get(key, 0) < v:
                            e.wait_ge(dma_sems[k], v)
                            waited[key] = v
                    if op.emit is None:
                        continue
                    ins = op.emit(e)
                    if op.is_dma:
                        ins.then_inc(dma_sems[op.semkey], op.inc)
                    elif op.signaled:
                        ins.then_inc(eng_sems[engname], 1)
            return fn

        block.tensor(make("pe"))
        block.scalar(make("act"))
        block.vector(make("dve"))
        block.gpsimd(make("pool"))
        block.sync(make("sp"))


class Arena:
    def __init__(self, nc, nbytes):
        self.h32 = nc.alloc_sbuf_tensor("arena", [128, nbytes // 4], F32)
        self.h16 = self.h32.bitcast(BF16)
        self.nbytes = nbytes
        self.off = 0

    def alloc(self, nbytes):
        nbytes = (nbytes + 63) // 64 * 64
        o = self.off
        self.off += nbytes
        assert self.off <= self.nbytes, ("SBUF arena overflow", self.off, self.nbytes)
        return o

    def f32(self, n):
        o = self.alloc(4 * n)
        return self.h32[:, o // 4:o // 4 + n]

    def bf16(self, n):
        o = self.alloc(2 * n)
        return self.h16[:, o // 2:o // 2 + n]


def _ap(t, extra_off, dims):
    return bass.AP(t.tensor, t.offset + extra_off, dims)


def build_program():
    nc = bass.Bass("TRN2", target_bir_lowering=False)
    s = Sched()
    dbg = _DEBUG["stop"]
    ORDER = ["p1", "p3", "p2a", "p2f", "p2", "p4b", "p4", "p5"]

    def upto(name):
        return dbg is None or ORDER.index(name) <= ORDER.index(dbg)

    def din(name, shape, dt=F32):
        need = {"wgate": "p4b", "wbh": "p4b", "wba": "p4b", "wout": "p4", "wfg": "p5", "wfu": "p5", "wfd": "p5",
                "dftF": "p2f", "dftI": "p2"}
        if name in need and not upto(need[name]):
            shape = [32, 128, 128] if len(shape) == 3 else [128, 128]
        _SHAPES[name] = tuple(shape)
        return nc.dram_tensor(name, list(shape), dt, kind="ExternalInput").ap()

    x_d = din("x", [S, D])
    xo_d = din("xo", [1024, D])
    wmix_d = din("wmix", [D, 1280])
    wgate_d = din("wgate", [D, 4096])
    bgate_d = din("bgate", [128, 32])
    g1_d = din("g1", [D])
    convw_d = din("convw", [128, 18])
    convb_d = din("convb", [128, 6])
    fw1_d = din("fw1", [33, 64])
    fw2_d = din("fw2", [64, 64])
    fw3_d = din("fw3", [64, 64])
    fw4_d = din("fw4", [64, 512])
    fvec_d = din("fvec", [64, 4])
    hyb_d = din("hybias", [128, 2])
    gqk_d = din("gqk", [128, 2])
    wbh_d = din("wbh", [1024, D])
    wba_d = din("wba", [1024, D])
    wout_d = din("wout", [D, D])
    g2_d = din("g2", [D])
    wfg_d = din("wfg", [D, DFF])
    wfu_d = din("wfu", [D, DFF])
    wfd_d = din("wfd", [DFF, D])
    identb_d = din("identb", [128, 128], BF16)
    identf_d = din("identf", [128, 128])
    onesdiv_d = din("onesdiv", [128, 128], BF16)
    perm_d = din("perm", [128, 128], BF16)
    ropec_d = din("ropec", [128, 64])
    ropes_d = din("ropes", [128, 64])
    zemb_d = din("zemb", [33, S])
    decay_d = din("decay", [S, 256])
    dftF_d = din("dftF", [32, 128, 2 * 32 * 128], BF16)
    dftI_d = din("dftI", [32, 128, 2 * 32 * 128], BF16)

    out_d = nc.dram_tensor("out", [1024, D], F32, kind="ExternalOutput").ap()
    yloc_d = nc.dram_tensor("yloc", [512, S], BF16, kind="Internal").ap()
    cloc_d = nc.dram_tensor("cloc", [128, S], BF16, kind="Internal").ap()
    cgat_d = nc.dram_tensor("cgat", [4 * 128, S], BF16, kind="Internal").ap()
    ystage_d = nc.dram_tensor("ystage", [4, 128, 4 * 1024], BF16, kind="Internal").ap()
    x1_d = nc.dram_tensor("x1s", [1024, D], F32, kind="Internal").ap()
    dbg_d = None
    if dbg is not None:
        dbg_d = nc.dram_tensor("dbg", [128, 8192], F32, kind="ExternalOutput").ap()

    A = Arena(nc, 207 * 1024)
    PS = [nc.alloc_psum_tensor("psb%d" % i, [128, 512], F32) for i in range(8)]
    PSB = [p.bitcast(BF16) for p in PS]

    identb = A.bf16(128)
    identf = A.f32(128)
    onesdiv = A.bf16(128)
    perm = A.bf16(128)
    ropec = A.f32(64)
    ropes = A.f32(64)
    gqk = A.f32(2)
    convw = A.f32(18)
    convb = A.f32(6)
    hyb = A.f32(2)
    bgate = A.f32(32)
    fvec = A.f32(4)
    stats = A.f32(64)
    negpi = A.f32(1)
    onescol = A.f32(1)
    epst = A.f32(1)
    small_loads = [(identb, identb_d), (identf, identf_d), (onesdiv, onesdiv_d), (perm, perm_d),
                   (ropec, ropec_d), (ropes, ropes_d), (gqk, gqk_d), (convw, convw_d), (convb, convb_d),
                   (hyb, hyb_d), (bgate, bgate_d)]
    for i, (dst, src) in enumerate(small_loads):
        s.add("sp", lambda e, dst=dst, src=src: e.dma_start(out=dst, in_=src), writes=["const%d" % i], dma="const")
    s.add("sp", lambda e: e.dma_start(out=fvec[0:64, :], in_=fvec_d), writes=["fvec"], dma="const")
    s.add("dve", lambda e: e.memset(negpi, -math.pi), writes=["negpi"])
    s.add("dve", lambda e: e.memset(onescol, 1.0), writes=["onescol"])
    s.add("dve", lambda e: e.memset(epst, EPS), writes=["epst"])
    base_mark = A.off

    def dump(ap_sb, ncols, name):
        tmp = A.f32(ncols)
        s.add("dve", lambda e: e.tensor_copy(tmp, ap_sb), reads=[name], writes=["dbgtmp"])
        s.add("sp", lambda e: e.dma_start(out=dbg_d[:, 0:ncols], in_=tmp), reads=["dbgtmp"], writes=["dbgout"], dma="dbg")

    def mm_group(bank_ap, lhs_fn, rhs_fn, nk, reads, bankname):
        def f(e):
            ins = None
            for k in range(nk):
                ins = e.matmul(bank_ap, lhs_fn(k), rhs_fn(k), start=(k == 0), stop=(k == nk - 1))
            return ins
        s.add("pe", f, reads=reads, writes=[bankname])

    cgv = cgat_d.rearrange("(q p) t -> p q t", q=4)

    def xchg_start(g):
        s.add("sp", lambda e: e.dma_start(out=cloc_d, in_=yloc_d[g * 128:(g + 1) * 128, :]),
              reads=["yloc"], writes=["cloc"], dma="cloc")
        s.add("pool", lambda e: e.collective_compute("AllGather", ALU.bypass, replica_groups=[[0, 1, 2, 3], [4, 5, 6, 7]],
                                                     ins=[cloc_d], outs=[cgat_d]),
              reads=["cloc"], writes=["cgat"], dma="cc", inc=1)

    def xchg_finish(g):
        s.add("sp", lambda e: e.dma_start(out=ystage_d[g].rearrange("p (q t) -> p q t", q=4),
                                          in_=cgv[:, :, bass.ds(_RID["v"] * 1024, 1024)]),
              reads=["cgat"], writes=["ystage%d" % g], dma="ystage")

    def transposes_to(src_bf, srcname, hdst3, col0, hname):
        for g4 in range(4):
            bank = g4 % 2
            pst = PSB[bank][:, 0:512].rearrange("p (a t) -> p a t", a=4)

            def tr(e, g4=g4, pst=pst):
                ins = None
                for a in range(4):
                    dk = g4 * 4 + a
                    ins = e.transpose(out=pst[:, a, :], in_=src_bf[:, dk * 128:(dk + 1) * 128], identity=identb)
                return ins
            s.add("pe", tr, reads=[srcname, "const0"], writes=["ps%d" % bank])
            dst = hdst3[:, g4 * 4:(g4 + 1) * 4, col0:col0 + 128]
            if g4 % 2 == 0:
                s.add("act", lambda e, dst=dst, pst=pst: e.activation(out=dst, in_=pst, func=AF.Copy),
                      reads=["ps%d" % bank], writes=[hname + "_a"])
            else:
                s.add("dve", lambda e, dst=dst, pst=pst: e.tensor_copy(dst, pst),
                      reads=["ps%d" % bank], writes=[hname + "_d"])

    def rms_scale(src_f32, srcname, dst_bf, dstname, gtile, gname, statcol, junkbuf):
        sc = stats[:, statcol:statcol + 1]
        sn = "stat%d" % statcol
        s.add("act", lambda e: e.activation(out=junkbuf, in_=src_f32, func=AF.Square, accum_out=sc),
              reads=[srcname], writes=["junk", sn])
        s.add("act", lambda e: e.activation(out=sc, in_=sc, func=AF.Sqrt, bias=epst, scale=1.0 / D), reads=[sn, "epst"], writes=[sn])
        s.add("dve", lambda e: e.reciprocal(sc, sc), reads=[sn], writes=[sn])
        s.add("dve", lambda e: e.scalar_tensor_tensor(out=dst_bf, in0=src_f32, scalar=sc, in1=gtile, op0=ALU.mult, op1=ALU.mult),
              reads=[srcname, sn, gname], writes=[dstname])

    RAW = A.bf16(6 * 4098)
    RAW3 = RAW.rearrange("p (c t) -> p c t", c=6)
    raw_end = A.off
    QT = A.bf16(2 * S)
    KT = A.bf16(S)
    V = A.bf16(32 * 128)
    QT3 = QT.rearrange("p (h t) -> p h t", h=2)
    V3 = V.rearrange("p (c d) -> p c d", c=32)
    p1_mark = A.off

    wm = A.bf16(16 * 1280)
    wm3 = wm.rearrange("p (k n) -> p k n", k=16)
    xt = [A.f32(D) for _ in range(2)]
    xs = [A.bf16(D) for _ in range(2)]
    junk = A.bf16(D)
    hT = [A.bf16(16 * 512) for _ in range(2)]
    hT3 = [h.rearrange("p (k t) -> p k t", k=16) for h in hT]
    gbc = A.f32(D)
    sqs = [A.bf16(512) for _ in range(1)]
    rstdqs = [A.f32(512)] * 3
    qns = [A.bf16(512) for _ in range(3)]
    rawq = [A.f32(512) for _ in range(3)]
    t1 = A.f32(512)
    t2 = A.f32(512)
    print("P1 arena end", A.off, "of", A.nbytes)

    wmix_v = wmix_d.rearrange("(k p) n -> p k n", p=128)
    for k0 in (0, 8):
        s.add("pool", lambda e, k0=k0: e.dma_start(out=wm3[:, k0:k0 + 8, :], in_=wmix_v[:, k0:k0 + 8, :]), writes=["wm"], dma="wm")
    s.add("sp", lambda e: e.dma_start(out=gbc, in_=g1_d.partition_broadcast(128)), writes=["gbc"], dma="gbc")
    s.add("dve", lambda e: e.memset(RAW3[:, :, 0:1], 0.0), writes=["rawpad0"])
    s.add("dve", lambda e: e.memset(RAW3[:, :, 4097:4098], 0.0), writes=["rawpad1"])
    x_v = x_d.rearrange("(n p) d -> n p d", p=128)
    pstep_c = ropec.ap[0][0]
    pstep_s = ropes.ap[0][0]

    evac_flip = 0
    for j in range(8):
        hs = j % 2
        hname = "hT%d" % hs
        for i in range(4):
            tile = 4 * j + i
            sl = i % 2
            s.add("sp", lambda e, sl=sl, tile=tile: e.dma_start(out=xt[sl], in_=x_v[tile]), writes=["xt%d" % sl], dma="xt%d" % sl)
            rms_scale(xt[sl], "xt%d" % sl, xs[sl], "xs%d" % sl, gbc, "gbc", tile, junk)
            transposes_to(xs[sl], "xs%d" % sl, hT3[hs], i * 128, hname)
        for u in range(3):
            bank = 2 + (u % 2)
            col = 768 + u * 128
            mm_group(PS[bank][:, :], lambda k, col=col: wm3[:, k, col:col + 128],
                     lambda k, hs=hs: hT3[hs][:, k, :], 16, [hname + "_a", hname + "_d", "wm"], "ps%d" % bank)
            s.add("act", lambda e, bank=bank, u=u: e.activation(out=rawq[u], in_=PS[bank][:, :], func=AF.Copy),
                  reads=["ps%d" % bank], writes=["rawq%d" % u])

        def hy_proj(cc):
            bank = 2 + (cc % 2)
            mm_group(PS[bank][:, :], lambda k, cc=cc: wm3[:, k, cc * 128:(cc + 1) * 128],
                     lambda k, hs=hs: hT3[hs][:, k, :], 16, [hname + "_a", hname + "_d", "wm"], "ps%d" % bank)
            dst = RAW3[:, cc, 1 + 512 * j:1 + 512 * (j + 1)]
            if cc % 2 == 0:
                s.add("act", lambda e, dst=dst, bank=bank: e.activation(out=dst, in_=PS[bank][:, :], func=AF.Copy),
                      reads=["ps%d" % bank], writes=["raw_a"])
            else:
                s.add("dve", lambda e, dst=dst, bank=bank: e.tensor_copy(dst, PS[bank][:, :]),
                      reads=["ps%d" % bank], writes=["raw_d"])

        def qk_square(u):
            s.add("act", lambda e, u=u: e.activation(out=sqs[0], in_=rawq[u], func=AF.Square),
                  reads=["rawq%d" % u], writes=["sq"])

        def qk_norm(u):
            s.add("pe", lambda e, u=u: e.matmul(PS[4][:, :], onesdiv, sqs[0], start=True, stop=True),
                  reads=["sq", "const2"], writes=["ps4"])
            s.add("act", lambda e, u=u: e.activation(out=rstdqs[u], in_=PS[4][:, :], func=AF.Sqrt, bias=epst, scale=1.0),
                  reads=["ps4", "epst"], writes=["rstdq"])
            s.add("dve", lambda e, u=u: e.reciprocal(rstdqs[u], rstdqs[u]), reads=["rstdq"], writes=["rstdq"])
            gcol = gqk[:, 0:1] if u < 2 else gqk[:, 1:2]
            s.add("dve", lambda e, u=u, gcol=gcol: e.scalar_tensor_tensor(
                out=qns[u], in0=rawq[u], scalar=gcol, in1=rstdqs[u], op0=ALU.mult, op1=ALU.mult),
                reads=["rawq%d" % u, "rstdq", "const6"], writes=["qn%d" % u])

        def qk_rope(u):
            qn_ = qns[u]
            s.add("pe", lambda e, qn_=qn_: e.matmul(PS[5][:, :], perm, qn_, start=True, stop=True),
                  reads=["qn%d" % u, "const3"], writes=["ps5"])
            for half in range(2):
                p0 = half * 64
                if half == 0:
                    cap = _ap(ropec, p0 * pstep_c + 8 * j, [[pstep_c, 64], [1, 8], [0, 64]])
                    sap = _ap(ropes, p0 * pstep_s + 8 * j, [[pstep_s, 64], [1, 8], [0, 64]])
                else:
                    cap = _ap(ropec, p0 * pstep_c, [[pstep_c, 64], [0, 8], [1, 64]])
                    sap = _ap(ropes, p0 * pstep_s, [[pstep_s, 64], [0, 8], [1, 64]])
                qv = qn_[p0:p0 + 64, :].rearrange("p (a b) -> p a b", a=8)
                t1v = t1[p0:p0 + 64, :].rearrange("p (a b) -> p a b", a=8)
                t2v = t2[p0:p0 + 64, :].rearrange("p (a b) -> p a b", a=8)
                pv = PS[5][p0:p0 + 64, :].rearrange("p (a b) -> p a b", a=8)
                s.add("pool", lambda e, t1v=t1v, qv=qv, cap=cap: e.tensor_tensor(out=t1v, in0=qv, in1=cap, op=ALU.mult),
                      reads=["qn%d" % u, "const4"], writes=["t1_%d" % half])
                s.add("dve", lambda e, t2v=t2v, pv=pv, sap=sap: e.tensor_tensor(out=t2v, in0=pv, in1=sap, op=ALU.mult),
                      reads=["ps5", "const5"], writes=["t2_%d" % half])
            dstq = QT3[:, u, 512 * j:512 * (j + 1)] if u < 2 else KT[:, 512 * j:512 * (j + 1)]
            s.add("pool", lambda e, dstq=dstq: e.tensor_tensor(out=dstq, in0=t1, in1=t2, op=ALU.add),
                  reads=["t1_0", "t1_1", "t2_0", "t2_1"], writes=["qk"])
        for cc in range(3):
            qk_square(cc)
            hy_proj(cc)
            qk_norm(cc)
        for cc in range(3, 6):
            hy_proj(cc)
            qk_rope(cc - 3)

        def vmm(e, hs=hs):
            ins = None
            for i in range(4):
                for k in range(16):
                    ins = e.matmul(PS[6][:, i * 128:(i + 1) * 128], hT3[hs][:, k, i * 128:(i + 1) * 128],
                                   wm3[:, k, 1152:1280], start=(k == 0), stop=(k == 15))
            return ins
        s.add("pe", vmm, reads=[hname + "_a", hname + "_d", "wm"], writes=["ps6"])
        s.add("act", lambda e, j=j: e.activation(out=V3[:, 4 * j:4 * j + 4, :],
                                                 in_=PS[6][:, :].rearrange("p (a d) -> p a d", a=4), func=AF.Copy),
              reads=["ps6"], writes=["v"])
    s.barrier()
    A.off = p1_mark
    if dbg == "p1":
        dump(QT[:, 0:8192], 8192, "qk")
        s.barrier()

    if upto("p3"):
        PT = [A.bf16(512) for _ in range(4)]
        PTs = [A.bf16(512) for _ in range(2)]
        rec = A.f32(512)
        yab = [A.bf16(512) for _ in range(2)]
        ones128 = A.bf16(128)
        s.add("dve", lambda e: e.memset(ones128, 1.0), writes=["ones128"])
        scale = 128.0 ** -0.5
        unit = 0
        for h in range(2):
            for qc in range(8):
                ob = 4 + 2 * (unit % 2)
                def st_mm(sc_):
                    sb = sc_ % 2
                    s.add("pe", lambda e, sb=sb, sc_=sc_, h=h, qc=qc: e.matmul(
                        PS[sb][:, :], KT[:, 128 * sc_:128 * (sc_ + 1)], QT3[:, h, 512 * qc:512 * (qc + 1)], start=True, stop=True),
                        reads=["qk"], writes=["ps%d" % sb])
                    pt = sc_ % 4
                    s.add("act", lambda e, sb=sb, pt=pt: e.activation(out=PT[pt], in_=PS[sb][:, :], func=AF.Exp, scale=scale),
                          reads=["ps%d" % sb], writes=["PT%d" % pt])

                def pv_mm(sc_):
                    pt = sc_ % 4
                    s.add("pe", lambda e, pt=pt, sc_=sc_, ob=ob: e.matmul(PS[ob][:, :], V3[:, sc_, :], PT[pt], start=(sc_ == 0), stop=(sc_ == 31)),
                          reads=["PT%d" % pt, "v"], writes=["ps%d" % ob])

                def pair_add(p):
                    a_, b_ = (2 * p) % 4, (2 * p + 1) % 4
                    ps_ = p % 2
                    s.add("pool", lambda e, a_=a_, b_=b_, ps_=ps_: e.tensor_tensor(out=PTs[ps_], in0=PT[a_], in1=PT[b_], op=ALU.add),
                          reads=["PT%d" % a_, "PT%d" % b_], writes=["PTs%d" % ps_])

                def dn_mm(p):
                    ps_ = p % 2
                    s.add("pe", lambda e, ps_=ps_, p=p, ob=ob: e.matmul(PS[ob + 1][:, :], ones128, PTs[ps_], start=(p == 0), stop=(p == 15)),
                          reads=["PTs%d" % ps_, "ones128"], writes=["ps%d" % (ob + 1)])
                st_mm(0)
                for sc_ in range(32):
                    if sc_ + 1 < 32:
                        st_mm(sc_ + 1)
                    pv_mm(sc_)
                    if sc_ % 2 == 1:
                        pair_add(sc_ // 2)
                    if sc_ % 2 == 0 and sc_ >= 2:
                        dn_mm(sc_ // 2 - 1)
                dn_mm(15)
                ys = unit % 2
                s.add("dve", lambda e, ob=ob: e.reciprocal(rec, PS[ob + 1][:, :]), reads=["ps%d" % (ob + 1)], writes=["rec"])
                s.add("dve", lambda e, ob=ob, ys=ys: e.tensor_tensor(out=yab[ys], in0=PS[ob][:, :], in1=rec, op=ALU.mult),
                      reads=["ps%d" % ob, "rec"], writes=["yab%d" % ys])
                s.add("sp", lambda e, h=h, qc=qc, ys=ys: e.dma_start(out=yloc_d[256 + h * 128:256 + (h + 1) * 128, 512 * qc:512 * (qc + 1)], in_=yab[ys]),
                      reads=["yab%d" % ys], writes=["yloc"], dma="yab%d" % ys)
                unit += 1
        s.barrier()
        if upto("p4b"):
            xchg_start(2)
        if dbg == "p3":
            s.add("sp", lambda e: e.dma_start(out=xt[0].bitcast(BF16), in_=yloc_d[256:384, :]), reads=["yloc"], writes=["dbgld"], dma="dbgld")
            dump(xt[0].bitcast(BF16)[:, 0:4096], 4096, "dbgld")
            s.barrier()
    A.off = raw_end

    x0T = A.bf16(2 * S)
    zT = A.bf16(2 * S)
    x0T3 = x0T.rearrange("p (c t) -> p c t", c=2)
    zT3 = zT.rearrange("p (c t) -> p c t", c=2)
    p2_mark = A.off
    if upto("p2a"):
        tbuf = [A.f32(S) for _ in range(4)]

        def conv_chain(cc, tb, tbn, eng, dst, dstn):
            w0 = convw[:, cc * 3 + 0:cc * 3 + 1]
            w1 = convw[:, cc * 3 + 1:cc * 3 + 2]
            w2 = convw[:, cc * 3 + 2:cc * 3 + 3]
            bb = convb[:, cc:cc + 1]
            s.add(eng, lambda e: e.tensor_scalar(tb, RAW3[:, cc, 1:4097], w1, bb, op0=ALU.mult, op1=ALU.add),
                  reads=["raw_a", "raw_d", "const7", "const8"], writes=[tbn])
            s.add(eng, lambda e: e.scalar_tensor_tensor(out=tb, in0=RAW3[:, cc, 0:4096], scalar=w0, in1=tb, op0=ALU.mult, op1=ALU.add),
                  reads=["raw_a", "raw_d", "rawpad0", tbn, "const7"], writes=[tbn])
            s.add(eng, lambda e: e.scalar_tensor_tensor(out=dst, in0=RAW3[:, cc, 2:4098], scalar=w2, in1=tb, op0=ALU.mult, op1=ALU.add),
                  reads=["raw_a", "raw_d", "rawpad1", tbn, "const7"], writes=[dstn])
        conv_chain(2, tbuf[0], "tb0", "dve", tbuf[0], "tb0")
        conv_chain(3, tbuf[1], "tb1", "dve", tbuf[1], "tb1")
        conv_chain(4, tbuf[2], "tb2", "dve", tbuf[2], "tb2")
        conv_chain(5, tbuf[3], "tb3", "dve", tbuf[3], "tb3")
        for c2 in range(2):
            s.add("dve", lambda e, c2=c2: e.tensor_tensor(out=zT3[:, c2, :], in0=tbuf[2 + c2], in1=tbuf[c2], op=ALU.mult),
                  reads=["tb%d" % c2, "tb%d" % (2 + c2)], writes=["zT"])
        conv_chain(0, tbuf[0], "tb0", "dve", x0T3[:, 0, :], "x0T")
        conv_chain(1, tbuf[1], "tb1", "dve", x0T3[:, 1, :], "x0T")
        s.barrier()
        if dbg == "p2a":
            A.off = p2_mark
            dump(zT[:, 0:8192], 8192, "zT")
            s.barrier()
    A.off = p2_mark

    if upto("p2f"):
        ZH = RAW[:, 0:32 * 768]
        ZH3 = ZH.rearrange("p (c n) -> p c n", c=32)
        R = A.bf16(32 * 512)
        R3 = R.rearrange("p (c n) -> p c n", c=32)
        dbuf = [A.bf16(2 * 32 * 128) for _ in range(2)]
        dbuf4 = [b.rearrange("p (a c n) -> p a c n", a=2, c=32) for b in dbuf]
        fw1 = A.f32(64)
        fw2 = A.f32(64)
        fw3 = A.f32(64)
        fw4 = A.f32(512)
        zemb = [A.f32(512) for _ in range(2)]
        hA = A.f32(512)
        hB = A.f32(512)
        uu = A.f32(512)
        val = A.f32(512)
        absv = A.f32(256)
        dec = [A.f32(256) for _ in range(2)]
        bsc = A.f32(4)
        scn = A.f32(2)
        sP = A.f32(256)
        sQ = A.f32(256)
        tt = [A.f32(256) for _ in range(4)]
        ysb = A.f32(256)
        ysb3 = ysb.rearrange("p (a c) -> p a c", a=2)
        ub = A.f32(512)
        yhs = [A.bf16(512) for _ in range(2)]
        dt_ = A.f32(4112) if dbg == "p2f" else None

        if upto("p4b"):
            xchg_finish(2)
            xchg_start(3)
        s.add("sp", lambda e: e.dma_start(out=fw1[0:33, :], in_=fw1_d), writes=["fw1"], dma="fw")
        s.add("sp", lambda e: e.dma_start(out=fw2[0:64, :], in_=fw2_d), writes=["fw2"], dma="fw")
        s.add("sp", lambda e: e.dma_start(out=fw3[0:64, :], in_=fw3_d), writes=["fw3"], dma="fw")
        s.add("sp", lambda e: e.dma_start(out=fw4[0:64, :], in_=fw4_d), writes=["fw4"], dma="fw")
        for cc in range(2):
            for g in range(8):
                bank = g % 2
                pst = PSB[bank][:, 0:512].rearrange("p (a t) -> p a t", a=4)

                def trz(e, cc=cc, g=g, pst=pst):
                    ins = None
                    for a in range(4):
                        ch = g * 4 + a
                        ins = e.transpose(out=pst[:, a, :], in_=zT3[:, cc, ch * 128:(ch + 1) * 128], identity=identb)
                    return ins
                s.add("pe", trz, reads=["zT", "const0"], writes=["ps%d" % bank])
                dst = ZH3[:, g * 4:(g + 1) * 4, 256 + cc * 128:256 + (cc + 1) * 128]
                s.add("dve", lambda e, dst=dst, pst=pst: e.tensor_copy(dst, pst), reads=["ps%d" % bank], writes=["ZHz"])
        for l in range(3):
            s.add("dve", lambda e, l=l: e.tensor_tensor(out=bsc[0:64, l:l + 1], in0=fvec[0:64, 0:1], in1=fvec[0:64, l + 1:l + 2], op=ALU.mult),
                  reads=["fvec"], writes=["bsc"])
        s.add("dve", lambda e: e.tensor_scalar(bsc[0:64, 0:3], bsc[0:64, 0:3], 1.0 / (2.0 * math.pi), 16.5, op0=ALU.mult, op1=ALU.add),
              reads=["bsc"], writes=["bsc"])
        s.add("dve", lambda e: e.tensor_scalar(bsc[0:64, 3:4], fvec[0:64, 0:1], 1.0 / (2.0 * math.pi), None, op0=ALU.mult),
              reads=["fvec", "bsc"], writes=["bsc"])
        frp = bsc[0:64, 3:4]
        ki = A.h32.bitcast(mybir.dt.int32)[:, A.alloc(4 * 512) // 4:][:, 0:512]
        kf = A.f32(512)

        def sin_layer(ps_ap, l, dst, rn, wn):
            s.add("dve", lambda e: e.tensor_scalar(uu[0:64, :], ps_ap, frp, bsc[0:64, l:l + 1], op0=ALU.mult, op1=ALU.add),
                  reads=rn + ["bsc"], writes=["uu"])
            s.add("dve", lambda e: e.tensor_copy(ki[0:64, :], uu[0:64, :]), reads=["uu"], writes=["ki"])
            s.add("dve", lambda e: e.tensor_copy(kf[0:64, :], ki[0:64, :]), reads=["ki"], writes=["kf"])
            s.add("dve", lambda e: e.tensor_tensor(out=uu[0:64, :], in0=uu[0:64, :], in1=kf[0:64, :], op=ALU.subtract),
                  reads=["uu", "kf"], writes=["uu"])
            s.add("dve", lambda e: e.scalar_tensor_tensor(out=uu[0:64, :], in0=uu[0:64, :], scalar=0.0, in1=uu[0:64, :], op0=ALU.is_lt, op1=ALU.add),
                  reads=["uu"], writes=["uu"])
            s.add("act", lambda e: e.activation(out=dst, in_=uu[0:64, :], func=AF.Sin, bias=negpi[0:64, :], scale=2.0 * math.pi),
                  reads=["uu", "negpi"], writes=wn)

        decay_v = decay_d.rearrange("(c p) n -> c p n", p=128)
        pstep_d = dec[0].ap[0][0]
        for j in range(8):
            zs = j % 2
            s.add("sp", lambda e, j=j, zs=zs: e.dma_start(out=zemb[zs][0:33, :], in_=zemb_d[:, 512 * j:512 * (j + 1)]),
                  writes=["zemb%d" % zs], dma="zemb%d" % zs)
            s.add("pe", lambda e, zs=zs: e.matmul(PS[2][0:64, :], fw1[0:33, :], zemb[zs][0:33, :], start=True, stop=True),
                  reads=["fw1", "zemb%d" % zs], writes=["ps2"])
            sin_layer(PS[2][0:64, :], 0, hA[0:64, :], ["ps2"], ["hA"])
            if dbg == "p2f" and j == 0:
                s.add("dve", lambda e: e.tensor_copy(dt_[0:64, 2048:2560], hA[0:64, :]), reads=["hA"], writes=["dbgtmp"])
                s.add("dve", lambda e: e.tensor_copy(dt_[64:128, 2048:2560], PS[2][0:64, :]), reads=["hA", "ps2"], writes=["dbgtmp"])
            s.add("pe", lambda e: e.matmul(PS[3][0:64, :], fw2[0:64, :], hA[0:64, :], start=True, stop=True),
                  reads=["fw2", "hA"], writes=["ps3"])
            sin_layer(PS[3][0:64, :], 1, hB[0:64, :], ["ps3"], ["hB"])
            if dbg == "p2f" and j == 0:
                s.add("dve", lambda e: e.tensor_copy(dt_[0:64, 2560:3072], hB[0:64, :]), reads=["hB"], writes=["dbgtmp"])
            s.add("pe", lambda e: e.matmul(PS[2][0:64, :], fw3[0:64, :], hB[0:64, :], start=True, stop=True),
                  reads=["fw3", "hB"], writes=["ps2"])
            if dbg == "p2f" and j == 0:
                s.add("dve", lambda e: e.tensor_copy(dt_[64:128, 2560:3072], PS[2][0:64, :]), reads=["ps2"], writes=["dbgtmp"])
            sin_layer(PS[2][0:64, :], 2, hA[0:64, :], ["ps2"], ["hA"])
            if dbg == "p2f" and j == 0:
                s.add("dve", lambda e: e.tensor_copy(dt_[64:128, 3072:3584], uu[0:64, :]), reads=["uu"], writes=["dbgtmp"])
                s.add("dve", lambda e: e.tensor_copy(dt_[0:64, 3072:3584], hA[0:64, :]), reads=["hA"], writes=["dbgtmp"])
            for i in range(4):
                ch = 4 * j + i
                ds_ = ch % 2
                s.add("sp", lambda e, ch=ch, ds_=ds_: e.dma_start(out=dec[ds_], in_=decay_v[ch]),
                      writes=["dec%d" % ds_], dma="dec%d" % ds_)
                s.add("pe", lambda e, i=i: e.matmul(PS[3][:, :], hA[0:64, i * 128:(i + 1) * 128], fw4[0:64, :], start=True, stop=True),
                      reads=["hA", "fw4"], writes=["ps3"])
                dbc = _ap(dec[ds_], 0, [[pstep_d, 128], [0, 2], [1, 256]])
                s.add("dve", lambda e, dbc=dbc: e.tensor_tensor(out=val.rearrange("p (a c) -> p a c", a=2),
                                                                 in0=PS[3][:, :].rearrange("p (a c) -> p a c", a=2), in1=dbc, op=ALU.mult),
                      reads=["ps3", "dec%d" % ds_], writes=["val"])
                if ch == 0:
                    s.add("dve", lambda e: e.memset(val[0:1, 256:512], 0.0), reads=["val"], writes=["val"])
                s.add("pool", lambda e, ch=ch: e.tensor_tensor(out=ZH3[:, ch, 0:256], in0=val[:, 0:256], in1=val[:, 256:512], op=ALU.add),
                      reads=["val"], writes=["ZHp"])
                s.add("pool", lambda e, ch=ch: e.tensor_tensor(out=ZH3[:, ch, 512:768], in0=val[:, 0:256], in1=val[:, 256:512], op=ALU.subtract),
                      reads=["val"], writes=["ZHm"])
                s.add("act", lambda e: e.activation(out=val, in_=val, func=AF.Abs), reads=["val", "ZHp", "ZHm"], writes=["val"])
                s.add("dve", lambda e: e.tensor_tensor(out=absv, in0=val[:, 0:256], in1=val[:, 256:512], op=ALU.add),
                      reads=["val"], writes=["absv"])
                for c2 in range(2):
                    s.add("pe", lambda e, c2=c2, ch=ch: e.matmul(PS[4 + c2][:, 0:1], absv[:, c2 * 128:(c2 + 1) * 128], onescol,
                                                                  start=(ch == 0), stop=(ch == 31)),
                          reads=["absv", "onescol"], writes=["ps%d" % (4 + c2)])
        for c2 in range(2):
            s.add("dve", lambda e, c2=c2: e.reciprocal(scn[:, c2:c2 + 1], PS[4 + c2][:, 0:1]), reads=["ps%d" % (4 + c2)], writes=["scn"])
        s.add("dve", lambda e: e.tensor_scalar(scn, scn, 2.0 / 8192.0, None, op0=ALU.mult), reads=["scn"], writes=["scn"])

        ZHALL = ["ZHz", "ZHp", "ZHm"]
        for i in range(32):
            sl = i % 2
            dn = "dbuf%d" % sl
            s.add("sp", lambda e, i=i, sl=sl: e.dma_start(out=dbuf[sl], in_=dftF_d[i]), writes=[dn], dma=dn)
            mm_group(PS[0][:, :], lambda k, sl=sl: dbuf4[sl][:, 0, k, :], lambda k: ZH3[:, k, 0:512], 32, [dn] + ZHALL, "ps0")
            mm_group(PS[1][:, :], lambda k, sl=sl: dbuf4[sl][:, 1, k, :], lambda k: ZH3[:, k, 256:768], 32, [dn] + ZHALL, "ps1")
            s.add("act", lambda e: e.activation(out=sP, in_=PS[0][:, 0:256], func=AF.Copy), reads=["ps0"], writes=["sP"])
            s.add("act", lambda e: e.activation(out=sQ, in_=PS[1][:, 256:512], func=AF.Copy), reads=["ps1"], writes=["sQ"])
            s.add("dve", lambda e: e.tensor_tensor(out=tt[0], in0=PS[0][:, 256:512], in1=sP, op=ALU.mult), reads=["ps0", "sP"], writes=["tt0"])
            s.add("dve", lambda e: e.tensor_tensor(out=tt[1], in0=PS[1][:, 0:256], in1=sQ, op=ALU.mult), reads=["ps1", "sQ"], writes=["tt1"])
            s.add("dve", lambda e: e.tensor_tensor(out=tt[2], in0=PS[0][:, 256:512], in1=sQ, op=ALU.mult), reads=["ps0", "sQ"], writes=["tt2"])
            s.add("dve", lambda e: e.tensor_tensor(out=tt[3], in0=PS[1][:, 0:256], in1=sP, op=ALU.mult), reads=["ps1", "sP"], writes=["tt3"])
            s.add("pool", lambda e, i=i: e.tensor_tensor(out=R3[:, i, 0:256], in0=tt[0], in1=tt[1], op=ALU.subtract), reads=["tt0", "tt1"], writes=["R"])
            s.add("pool", lambda e, i=i: e.tensor_tensor(out=R3[:, i, 256:512], in0=tt[2], in1=tt[3], op=ALU.add), reads=["tt2", "tt3"], writes=["R"])
        if dbg == "p2f":
            s.barrier()
            s.add("dve", lambda e: e.tensor_copy(dt_[:, 0:1024].rearrange("p (c n) -> p c n", c=4), ZH3[:, 0:4, 0:256]), reads=["ZHp"], writes=["dbgtmp"])
            s.add("dve", lambda e: e.tensor_copy(dt_[:, 1024:2048].rearrange("p (c n) -> p c n", c=4), ZH3[:, 0:4, 512:768]), reads=["ZHm"], writes=["dbgtmp"])
            s.add("dve", lambda e: e.tensor_copy(dt_[:, 3584:4096], R3[:, 5, :]), reads=["R"], writes=["dbgtmp"])
            s.add("dve", lambda e: e.tensor_copy(dt_[:, 3072:3584], R3[:, 0, :]), reads=["R"], writes=["dbgtmp"])
            s.add("dve", lambda e: e.tensor_copy(dt_[:, 4096:4098], scn), reads=["scn"], writes=["dbgtmp"])
            s.add("sp", lambda e: e.dma_start(out=dbg_d[:, 0:4112], in_=dt_), reads=["dbgtmp"], writes=["dbgout"], dma="dbg")
            s.barrier()
        if upto("p4b"):
            xchg_finish(3)
        ysbs = [ysb, A.f32(256)]
        ysb3s = [y.rearrange("p (a c) -> p a c", a=2) for y in ysbs]

        def inv_issue(tch):
            sl = tch % 2
            dn = "dbuf%d" % sl
            ib = tch % 2
            s.add("sp", lambda e, tch=tch, sl=sl: e.dma_start(out=dbuf[sl], in_=dftI_d[tch]), writes=[dn], dma=dn)

            def inv(e, sl=sl, ib=ib):
                ins = None
                for k in range(32):
                    ins = e.matmul(PS[ib][:, 0:256], dbuf4[sl][:, 0, k, :], R3[:, k, 0:256], start=(k == 0), stop=False)
                    ins = e.matmul(PS[ib][:, 0:256], dbuf4[sl][:, 1, k, :], R3[:, k, 256:512], start=False, stop=(k == 31))
                return ins
            s.add("pe", inv, reads=[dn, "R"], writes=["ps%d" % ib])
            s.add("act", lambda e, ib=ib: e.activation(out=ysbs[ib], in_=PS[ib][:, 0:256], func=AF.Copy), reads=["ps%d" % ib], writes=["ysb%d" % ib])

        def inv_transposes(tch):
            ib = tch % 2
            a = tch % 4
            tb0 = 2 + 2 * ((tch // 4) % 2)
            for c2 in range(2):
                s.add("pe", lambda e, c2=c2, a=a, ib=ib, tb0=tb0: e.transpose(out=PS[tb0 + c2][:, a * 128:(a + 1) * 128], in_=ysb3s[ib][:, c2, :], identity=identf),
                      reads=["ysb%d" % ib, "const1"], writes=["ps%d" % (tb0 + c2)])

        ubs = [ub, A.f32(512)]

        def epilogue(tg):
            tb0 = 2 + 2 * (tg % 2)
            for c2 in range(2):
                tsl = slice(512 * tg, 512 * (tg + 1))
                yb = (tg * 2 + c2) % 2
                ubx = ubs[c2]
                un = "ub%d" % c2
                s.add("pool", lambda e, c2=c2, tsl=tsl, ubx=ubx: e.tensor_scalar(ubx, zT3[:, c2, tsl], hyb[:, c2:c2 + 1], None, op0=ALU.mult),
                      reads=["zT", "const9"], writes=[un])
                s.add("dve", lambda e, c2=c2, ubx=ubx, tb0=tb0: e.scalar_tensor_tensor(out=ubx, in0=PS[tb0 + c2][:, :], scalar=scn[:, c2:c2 + 1], in1=ubx,
                                                                                  op0=ALU.mult, op1=ALU.add),
                      reads=["ps%d" % (tb0 + c2), "scn", un], writes=[un])
                s.add("dve", lambda e, c2=c2, tsl=tsl, yb=yb, ubx=ubx: e.tensor_tensor(out=yhs[yb], in0=ubx, in1=x0T3[:, c2, tsl], op=ALU.mult),
                      reads=[un, "x0T"], writes=["yhs%d" % yb])
                s.add("sp", lambda e, c2=c2, tsl=tsl, yb=yb: e.dma_start(out=yloc_d[c2 * 128:(c2 + 1) * 128, tsl], in_=yhs[yb]),
                      reads=["yhs%d" % yb], writes=["yloc"], dma="yhs%d" % yb)
        if upto("p2"):
            inv_issue(0)
            for tch in range(32):
                if tch + 1 < 32:
                    inv_issue(tch + 1)
                inv_transposes(tch)
                if tch % 4 == 3:
                    epilogue(tch // 4)
        s.barrier()
        if dbg == "p2":
            s.add("sp", lambda e: e.dma_start(out=dbuf[0][:, 0:4096], in_=yloc_d[0:128, :]), reads=["yloc"], writes=["dbgld2"], dma="dbgld")
            tmpf = dbuf[1].bitcast(F32)
            s.add("dve", lambda e: e.tensor_copy(tmpf, dbuf[0][:, 0:4096]), reads=["dbgld2"], writes=["dbgtmp"])
            s.add("sp", lambda e: e.dma_start(out=dbg_d[:, 0:4096], in_=tmpf), reads=["dbgtmp"], writes=["dbgout"], dma="dbg")
            s.barrier()
    A.off = base_mark

    if upto("p4b"):
        xchg_start(0)
        G = A.bf16(32 * 1024)
        G3 = G.rearrange("p (k t) -> p k t", k=32)
        mT3 = G3[:, 0:16, :]
        mT = G[:, 0:16 * 1024]
        mt_end = A.off - 16 * 1024 * 2
        hTo = A.bf16(16 * 1024)
        hTo3 = hTo.rearrange("p (k t) -> p k t", k=16)
        YT = A.bf16(16 * 1024)
        YT3 = YT.rearrange("p (k t) -> p k t", k=16)
        p4_mark = A.off
        h2T_off = 175 * 1024
        h2T = A.h16[:, h2T_off // 2:h2T_off // 2 + 16 * 1024]
        h2T3 = h2T.rearrange("p (k t) -> p k t", k=16)
        xt2 = [A.f32(D) for _ in range(2)]
        xs2 = [A.bf16(D) for _ in range(2)]
        junk2 = A.bf16(D)
        gbc2 = A.f32(D)
        WBLK = 256
        wg = [A.bf16(2 * 16 * WBLK) for _ in range(2)]
        wg4 = [w.rearrange("p (a k n) -> p a k n", a=2, k=16) for w in wg]
        s.add("sp", lambda e: e.dma_start(out=gbc2, in_=g1_d.partition_broadcast(128)), writes=["gbc2"], dma="gbc")
        xo_v = xo_d.rearrange("(n p) d -> n p d", p=128)
        for t in range(8):
            sl = t % 2
            s.add("sp", lambda e, sl=sl, t=t: e.dma_start(out=xt2[sl], in_=xo_v[t]), writes=["oxt%d" % sl], dma="oxt%d" % sl)
            rms_scale(xt2[sl], "oxt%d" % sl, xs2[sl], "oxs%d" % sl, gbc2, "gbc2", 32 + t, junk2)
            transposes_to(xs2[sl], "oxs%d" % sl, hTo3, t * 128, "ohT")
        wgate_v = wgate_d.rearrange("(k p) n -> p k n", p=128)
        wbh_v = wbh_d.rearrange("(k p) n -> p k n", p=128)
        wba_v = wba_d.rearrange("(k p) n -> p k n", p=128)
        NJB = D // WBLK

        def load_gblock(jb):
            sl = jb % 2
            wn = "wg%d" % sl
            c0 = jb * WBLK
            s.add("pool", lambda e, sl=sl, c0=c0: e.dma_start(out=wg4[sl][:, 0, :, :], in_=wgate_v[:, :, c0:c0 + WBLK]), writes=[wn], dma=wn)
            s.add("pool", lambda e, sl=sl, c0=c0: e.dma_start(out=wg4[sl][:, 1, :, :], in_=wgate_v[:, :, D + c0:D + c0 + WBLK]), writes=[wn], dma=wn)
        load_gblock(0)
        load_gblock(1)
        xchg_finish(0)
        xchg_start(1)
        cntA = 0
        for jb in range(NJB):
            sl = jb % 2
            wn = "wg%d" % sl
            for sub in range(WBLK // 128):
                jc = jb * (WBLK // 128) + sub
                so = sub * 128
                for tc_ in range(2):
                    tsl = slice(512 * tc_, 512 * (tc_ + 1))
                    par = cntA % 3
                    cntA += 1
                    b0, b1 = 2 + 2 * par, 3 + 2 * par
                    mm_group(PS[b0][:, :], lambda k, sl=sl, so=so: wg4[sl][:, 0, k, so:so + 128], lambda k, tsl=tsl: hTo3[:, k, tsl], 16, [wn, "ohT_a", "ohT_d"], "ps%d" % b0)
                    mm_group(PS[b1][:, :], lambda k, sl=sl, so=so: wg4[sl][:, 1, k, so:so + 128], lambda k, tsl=tsl: hTo3[:, k, tsl], 16, [wn, "ohT_a", "ohT_d"], "ps%d" % b1)
                    s.add("act", lambda e, jc=jc, b0=b0, tsl=tsl: e.activation(out=G3[:, jc, tsl], in_=PS[b0][:, :], func=AF.Sigmoid, bias=bgate[:, jc:jc + 1], scale=1.0),
                          reads=["ps%d" % b0, "const10"], writes=["gh%d" % jc])
                    s.add("act", lambda e, jc=jc, b1=b1, tsl=tsl: e.activation(out=G3[:, 16 + jc, tsl], in_=PS[b1][:, :], func=AF.Sigmoid, bias=bgate[:, 16 + jc:17 + jc], scale=1.0),
                          reads=["ps%d" % b1, "const10"], writes=["ga%d" % jc])
            if jb + 2 < NJB:
                load_gblock(jb + 2)
        xchg_finish(1)
        for g in range(4):
            kind, c2 = g // 2, g % 2
            base = kind * 8 + c2
            dst = _ap(YT, base * 1024, [[YT.ap[0][0], 128], [2 * 1024, 4], [1, 1024]])
            s.add("sp", lambda e, g=g, dst=dst: e.dma_start(out=dst, in_=ystage_d[g].rearrange("p (q t) -> p q t", q=4)),
                  reads=["ystage%d" % g], writes=["YT"], dma="YT")
        s.barrier()
        A.off = p4_mark
        wb = [A.bf16(2 * 8 * WBLK) for _ in range(2)]
        wb4 = [w.rearrange("p (a k n) -> p a k n", a=2, k=8) for w in wb]
        mtmp = [[A.f32(512) for _ in range(2)] for _ in range(2)]

        def load_bblock(jb):
            sl = jb % 2
            wn = "wb%d" % sl
            c0 = jb * WBLK
            s.add("pool", lambda e, sl=sl, c0=c0: e.dma_start(out=wb4[sl][:, 0, :, :], in_=wbh_v[:, :, c0:c0 + WBLK]), writes=[wn], dma=wn)
            s.add("pool", lambda e, sl=sl, c0=c0: e.dma_start(out=wb4[sl][:, 1, :, :], in_=wba_v[:, :, c0:c0 + WBLK]), writes=[wn], dma=wn)
        load_bblock(0)
        cntB = 0
        for jb in range(NJB):
            sl = jb % 2
            wn = "wb%d" % sl
            if jb + 1 < NJB:
                load_bblock(jb + 1)
            for sub in range(WBLK // 128):
                jc = jb * (WBLK // 128) + sub
                so = sub * 128
                for tc_ in range(2):
                    tsl = slice(512 * tc_, 512 * (tc_ + 1))
                    par = cntB % 2
                    cntB += 1
                    b2, b3 = (2, 3) if par == 0 else (4, 5)
                    m1_, m2_ = mtmp[par]
                    pn = "_%d" % par
                    mm_group(PS[b2][:, :], lambda k, sl=sl, so=so: wb4[sl][:, 0, k, so:so + 128], lambda k, tsl=tsl: YT3[:, k, tsl], 8, [wn, "YT"], "ps%d" % b2)
                    mm_group(PS[b3][:, :], lambda k, sl=sl, so=so: wb4[sl][:, 1, k, so:so + 128], lambda k, tsl=tsl: YT3[:, 8 + k, tsl], 8, [wn, "YT"], "ps%d" % b3)
                    s.add("dve", lambda e, b2=b2, m1_=m1_, jc=jc, tsl=tsl: e.tensor_tensor(out=m1_, in0=PS[b2][:, :], in1=G3[:, jc, tsl], op=ALU.mult),
                          reads=["ps%d" % b2, "gh%d" % jc], writes=["m1" + pn])
                    s.add("dve", lambda e, b3=b3, m2_=m2_, jc=jc, tsl=tsl: e.tensor_tensor(out=m2_, in0=PS[b3][:, :], in1=G3[:, 16 + jc, tsl], op=ALU.mult),
                          reads=["ps%d" % b3, "ga%d" % jc], writes=["m2" + pn])
                    s.add("pool", lambda e, jc=jc, tsl=tsl, m1_=m1_, m2_=m2_: e.tensor_tensor(out=mT3[:, jc, tsl], in0=m1_, in1=m2_, op=ALU.add),
                          reads=["m1" + pn, "m2" + pn], writes=["gh%d" % jc])
        s.barrier()
        if dbg == "p4b":
            A.off = p4_mark
            dump(mT[:, 0:8192], 8192, "mT")
            s.barrier()
    if upto("p4"):
        A.off = mt_end
        x1 = A.f32(8 * D)
        x13 = x1.rearrange("p (t d) -> p t d", t=8)
        wo = [A.bf16(16 * 512) for _ in range(2)]
        wo3 = [w.rearrange("p (k n) -> p k n", k=16) for w in wo]
        xs3 = [A.bf16(D) for _ in range(2)]
        junk3 = A.bf16(D)
        gbc3 = A.f32(D)
        assert A.off <= h2T_off
        s.add("sp", lambda e: e.dma_start(out=gbc3, in_=g2_d.partition_broadcast(128)), writes=["gbc3"], dma="gbc")
        for t in range(8):
            s.add("sp", lambda e, t=t: e.dma_start(out=x13[:, t, :], in_=xo_v[t]), writes=["x1_%d" % t], dma="x1ld")
        wout_v = wout_d.rearrange("(k p) n -> p k n", p=128)
        for db in range(4):
            sl = db % 2
            wn = "wo%d" % sl
            s.add("pool", lambda e, sl=sl, db=db: e.dma_start(out=wo3[sl], in_=wout_v[:, :, 512 * db:512 * (db + 1)]), writes=[wn], dma=wn)
            for t in range(8):
                bank = 2 + (t % 2)
                mm_group(PS[bank][:, :], lambda k, t=t: mT3[:, k, 128 * t:128 * (t + 1)], lambda k, sl=sl: wo3[sl][:, k, :], 16, [wn, "mT"], "ps%d" % bank)
                dsl = slice(512 * db, 512 * (db + 1))
                s.add("dve", lambda e, t=t, dsl=dsl, bank=bank: e.tensor_tensor(out=x13[:, t, dsl], in0=PS[bank][:, :], in1=x13[:, t, dsl], op=ALU.add),
                      reads=["ps%d" % bank, "x1_%d" % t], writes=["x1_%d" % t])
        x1_v = x1_d.rearrange("(n p) d -> n p d", p=128)
        for t in range(8):
            s.add("sp", lambda e, t=t: e.dma_start(out=x1_v[t], in_=x13[:, t, :]), reads=["x1_%d" % t], writes=["x1d"], dma="x1st")
            sl = t % 2
            rms_scale(x13[:, t, :], "x1_%d" % t, xs3[sl], "fxs%d" % sl, gbc3, "gbc3", 48 + t, junk3)
            transposes_to(xs3[sl], "fxs%d" % sl, h2T3, t * 128, "fhT")
        s.barrier()
        if dbg == "p4":
            A.off = mt_end + 8 * D * 4
            dump(x1[:, 0:8192], 8192, "x1_3")
            s.barrier()
    if upto("p5"):
        A.off = base_mark
        aT = A.bf16(NFF * 1024)
        aT3 = aT.rearrange("p (f t) -> p f t", f=NFF)
        p5_mark = A.off
        wgu = [A.bf16(2 * 16 * 256) for _ in range(2)]
        wgu4 = [w.rearrange("p (a k n) -> p a k n", a=2, k=16) for w in wgu]
        sgl = [A.f32(512) for _ in range(2)]
        assert A.off <= h2T_off
        wfg_v = wfg_d.rearrange("(k p) n -> p k n", p=128)
        wfu_v = wfu_d.rearrange("(k p) n -> p k n", p=128)
        for fb in range(DFF // 256):
            sl = fb % 2
            wn = "wgu%d" % sl
            c0 = fb * 256
            s.add("pool", lambda e, sl=sl, c0=c0: e.dma_start(out=wgu4[sl][:, 0, :, :], in_=wfg_v[:, :, c0:c0 + 256]), writes=[wn], dma=wn)
            s.add("pool", lambda e, sl=sl, c0=c0: e.dma_start(out=wgu4[sl][:, 1, :, :], in_=wfu_v[:, :, c0:c0 + 256]), writes=[wn], dma=wn)
            for sub in range(2):
                fc = fb * 2 + sub
                so = sub * 128
                for tc_ in range(2):
                    tsl = slice(512 * tc_, 512 * (tc_ + 1))
                    u = (fc * 2 + tc_) % 2
                    bg, bu = 2 + 2 * u, 3 + 2 * u
                    mm_group(PS[bg][:, :], lambda k, sl=sl, so=so: wgu4[sl][:, 0, k, so:so + 128], lambda k, tsl=tsl: h2T3[:, k, tsl], 16, [wn, "fhT_a", "fhT_d"], "ps%d" % bg)
                    mm_group(PS[bu][:, :], lambda k, sl=sl, so=so: wgu4[sl][:, 1, k, so:so + 128], lambda k, tsl=tsl: h2T3[:, k, tsl], 16, [wn, "fhT_a", "fhT_d"], "ps%d" % bu)
                    s.add("act", lambda e, u=u, bg=bg: e.activation(out=sgl[u], in_=PS[bg][:, :], func=AF.Silu), reads=["ps%d" % bg], writes=["sgl%d" % u])
                    s.add("dve", lambda e, u=u, bu=bu, fc=fc, tsl=tsl: e.tensor_tensor(out=aT3[:, fc, tsl], in0=PS[bu][:, :], in1=sgl[u], op=ALU.mult),
                          reads=["ps%d" % bu, "sgl%d" % u], writes=["aT"])
        s.barrier()
        A.off = p5_mark
        wd = [A.bf16(NFF * 512) for _ in range(2)]
        wd3 = [w.rearrange("p (f n) -> p f n", f=NFF) for w in wd]
        xr = [A.f32(512) for _ in range(2)]
        ot = [A.f32(512) for _ in range(2)]
        wfd_v = wfd_d.rearrange("(f p) n -> p f n", p=128)
        out_v = out_d.rearrange("(n p) d -> n p d", p=128)
        cnt = 0
        for db in range(4):
            sl = db % 2
            wn = "wd%d" % sl
            for f0 in (0, 22):
                s.add("pool", lambda e, sl=sl, f0=f0, db=db: e.dma_start(out=wd3[sl][:, f0:f0 + 22, :], in_=wfd_v[:, f0:f0 + 22, 512 * db:512 * (db + 1)]), writes=[wn], dma=wn)
            for t in range(8):
                bank = 2 + (t % 2)
                u = cnt % 2
                cnt += 1
                dsl = slice(512 * db, 512 * (db + 1))
                s.add("sp", lambda e, t=t, dsl=dsl, u=u: e.dma_start(out=xr[u], in_=x1_v[t][:, dsl]), reads=["x1d"], writes=["xr%d" % u], dma="xr%d" % u)
                mm_group(PS[bank][:, :], lambda k, t=t: aT3[:, k, 128 * t:128 * (t + 1)], lambda k, sl=sl: wd3[sl][:, k, :], NFF, [wn, "aT"], "ps%d" % bank)
                s.add("dve", lambda e, u=u, bank=bank: e.tensor_tensor(out=ot[u], in0=PS[bank][:, :], in1=xr[u], op=ALU.add),
                      reads=["ps%d" % bank, "xr%d" % u], writes=["ot%d" % u])
                s.add("sp", lambda e, t=t, dsl=dsl, u=u: e.dma_start(out=out_v[t][:, dsl], in_=ot[u]), reads=["ot%d" % u], writes=["outd"], dma="ot%d" % u)
    s.barrier()

    dma_keys = list(s.dma_cnt.keys())
    eng_sems = {e: nc.alloc_semaphore("sem_" + e) for e in ENGS}
    dma_sems = {k: nc.alloc_semaphore("dsem_%d" % i) for i, k in enumerate(dma_keys)}
    with nc.Block() as block:
        s.emit_all(block, eng_sems, dma_sems)
    return nc


_CONST_CACHE = {}


def _constants():
    if _CONST_CACHE:
        return _CONST_CACHE
    bf = ml_dtypes.bfloat16
    c = {}
    c["identb"] = np.eye(128, dtype=np.float32).astype(bf)
    c["identf"] = np.eye(128, dtype=np.float32)
    c["onesdiv"] = np.full((128, 128), 1.0 / 128.0, dtype=np.float32).astype(bf)
    perm = np.zeros((128, 128), dtype=np.float32)
    for d in range(128):
        partner = d + 32 if (d % 64) < 32 else d - 32
        perm[partner, d] = 1.0
    c["perm"] = perm.astype(bf)
    inv = (10000.0 ** (-np.arange(0, 64, 2, dtype=np.float32) / 64.0)).astype(np.float32)
    pos = np.arange(64, dtype=np.float32)
    ropec = np.zeros((128, 64), dtype=np.float32)
    ropes = np.zeros((128, 64), dtype=np.float32)
    for d in range(128):
        ang = (pos * inv[d % 32]).astype(np.float32)
        ropec[d] = np.cos(ang)
        sgn = -1.0 if (d % 64) < 32 else 1.0
        ropes[d] = sgn * np.sin(ang)
    c["ropec"] = ropec
    c["ropes"] = ropes
    L = S
    posf = np.arange(L, dtype=np.float32)
    t = (posf / np.float32(L - 1)).astype(np.float32)
    fb = np.linspace(1e-4, 15, 16, dtype=np.float32)
    ang = ((np.float32(2.0 * math.pi) * posf / np.float32(L))[:, None] * fb[None, :]).astype(np.float32)
    zemb = np.concatenate([t[:, None], np.cos(ang), -np.sin(ang)], axis=-1).astype(np.float32)
    c["zemb"] = np.ascontiguousarray(zemb.T)
    max_decay = math.log(1e-2) / 0.3
    min_decay = math.log(1e-2) / 1.5
    deltas = np.abs(np.linspace(min_decay, max_decay, 1024, dtype=np.float32))
    c["decay_full"] = np.exp(-t[:, None] * deltas[None, :]).astype(np.float32)
    m = np.arange(S, dtype=np.int64)[:, None]
    f = np.arange(S, dtype=np.int64)[None, :]
    ph = (m * (2 * f + 1)) % (2 * 8192)
    th = ph.astype(np.float64) * (2.0 * math.pi / (2 * 8192))
    C = np.cos(th).astype(np.float32).astype(bf)
    Sn = np.sin(th).astype(np.float32).astype(bf)
    def fwd_layout(M):
        return M.reshape(32, 128, 32, 128).transpose(2, 1, 0, 3)
    def inv_layout(M):
        return M.reshape(32, 128, 32, 128).transpose(0, 3, 2, 1)
    c["dftF"] = np.ascontiguousarray(np.stack([fwd_layout(C), fwd_layout(Sn)], axis=2)).reshape(32, 128, 2 * 32 * 128)
    c["dftI"] = np.ascontiguousarray(np.stack([inv_layout(C), inv_layout(Sn)], axis=2)).reshape(32, 128, 2 * 32 * 128)
    _CONST_CACHE.update(c)
    return _CONST_CACHE


def _chunkcols(v):
    v = np.asarray(v, dtype=np.float32)
    return np.ascontiguousarray(v.reshape(-1, 128).T)


def _core_inputs(inp, b, r):
    c = _constants()
    f32 = np.float32
    w_in = inp["w_in"][0]
    hsl = lambda base: slice(base + 256 * r, base + 256 * (r + 1))
    kvh = r // 2
    cols = np.concatenate([
        np.arange(0 + 256 * r, 0 + 256 * (r + 1)),
        np.arange(1024 + 256 * r, 1024 + 256 * (r + 1)),
        np.arange(2048 + 256 * r, 2048 + 256 * (r + 1)),
        np.arange(3072 + 256 * r, 3072 + 256 * (r + 1)),
        np.arange(4096 + 128 * kvh, 4096 + 128 * (kvh + 1)),
        np.arange(4352 + 128 * kvh, 4352 + 128 * (kvh + 1)),
    ])
    m = {}
    m["x"] = np.ascontiguousarray(inp["x"][b])
    m["xo"] = np.ascontiguousarray(inp["x"][b, 1024 * r:1024 * (r + 1)])
    m["wmix"] = np.ascontiguousarray(w_in[:, cols])
    m["wgate"] = np.ascontiguousarray(w_in[:, 4608:8704])
    m["bgate"] = _chunkcols(inp["b_gate"][0])
    m["g1"] = np.ascontiguousarray(inp["mix_norm_g"][0])
    cw = inp["hy_conv_w"][0]
    cb = inp["hy_conv_b"][0]
    convw = np.zeros((128, 18), dtype=f32)
    convb = np.zeros((128, 6), dtype=f32)
    for cc in range(6):
        base = (cc // 2) * 1024 + 256 * r + (cc % 2) * 128
        for j in range(3):
            convw[:, cc * 3 + j] = cw[j, base:base + 128]
        convb[:, cc] = cb[base:base + 128]
    m["convw"] = convw
    m["convb"] = convb
    m["fw1"] = np.ascontiguousarray(inp["flt_w1"][0])
    m["fw2"] = np.ascontiguousarray(inp["flt_w2"][0])
    m["fw3"] = np.ascontiguousarray(inp["flt_w3"][0])
    w4 = inp["flt_w4"][0]
    m["fw4"] = np.ascontiguousarray(np.concatenate([w4[:, hsl(0)], w4[:, hsl(1024)]], axis=1))
    m["fvec"] = np.ascontiguousarray(np.stack([inp["flt_freq"][0], inp["flt_b1"][0], inp["flt_b2"][0], inp["flt_b3"][0]], axis=1))
    m["hybias"] = _chunkcols(inp["hy_bias"][0][hsl(0)])
    m["gqk"] = np.ascontiguousarray(np.stack([inp["q_norm_g"][0], inp["k_norm_g"][0]], axis=1))
    m["wbh"] = inp["w_br_hyena"][0]
    m["wba"] = inp["w_br_attn"][0]
    m["wout"] = inp["w_out"][0]
    m["g2"] = np.ascontiguousarray(inp["ffn_norm_g"][0])
    m["wfg"] = inp["w_ffn_gate"][0]
    m["wfu"] = inp["w_ffn_up"][0]
    m["wfd"] = inp["w_ffn_down"][0]
    for k in ("identb", "identf", "onesdiv", "perm", "ropec", "ropes", "zemb", "dftF", "dftI"):
        m[k] = c[k]
    m["decay"] = np.ascontiguousarray(c["decay_full"][:, 256 * r:256 * (r + 1)])
    out = {}
    for k, v in m.items():
        v = np.asarray(v)
        if k in _SHAPES and tuple(v.shape) != _SHAPES[k]:
            v = np.zeros(_SHAPES[k], dtype=v.dtype)
        out[k] = np.ascontiguousarray(v)
    return out


_NC_CACHE = {}


def kernel(**inputs):
    inp = {k: np.asarray(v) for k, v in inputs.items()}
    key = _DEBUG["stop"]
    if key not in _NC_CACHE:
        _NC_CACHE[key] = build_program()
    nc = _NC_CACHE[key]
    in_maps = [_core_inputs(inp, c // 4, c % 4) for c in range(8)]
    res = run_bass_kernel_spmd(nc, in_maps, core_ids=list(range(8)))
    if key is not None:
        return [r["dbg"] for r in res.results]
    out = np.zeros((2, S, D), dtype=np.float32)
    for c in range(8):
        b, r = c // 4, c % 4
        out[b, 1024 * r:1024 * (r + 1)] = res.results[c]["out"]
    return out
```

```python
import math
import numpy as np
import ml_dtypes
import concourse.bass as bass
import concourse.mybir as mybir
from concourse.bass_utils import run_bass_kernel_spmd

F32 = mybir.dt.float32
BF16 = mybir.dt.bfloat16
AF = mybir.ActivationFunctionType
ALU = mybir.AluOpType

D = 2048
S = 4096
DFF = 5632
EPS = 1e-6
NFF = DFF // 128
ENGS = ["pe", "act", "dve", "pool", "sp"]
_DEBUG = {"stop": None}
_RID = {}
_SHAPES = {}


class _Op:
    __slots__ = ("eng", "emit", "deps", "dma_deps", "is_dma", "semkey", "signaled", "sig_idx", "inc")


class Sched:
    def __init__(self):
        self.ops = {e: [] for e in ENGS}
        self.lastw = {}
        self.readers = {}
        self.dma_cnt = {}

    def add(self, eng, emit, reads=(), writes=(), dma=None, inc=16):
        op = _Op()
        op.inc = inc
        op.eng = eng
        op.emit = emit
        op.is_dma = dma is not None
        op.semkey = dma
        op.signaled = False
        op.sig_idx = 0
        deps = {}
        same_raw = set()
        for r in reads:
            w = self.lastw.get(r)
            if w is not None:
                deps[id(w)] = w
                if w.eng == eng and eng != "pe" and not w.is_dma:
                    same_raw.add(id(w))
        for wn in writes:
            w = self.lastw.get(wn)
            if w is not None:
                deps[id(w)] = w
            for rd in self.readers.get(wn, ()):
                deps[id(rd)] = rd
        for r in reads:
            self.readers.setdefault(r, []).append(op)
        for wn in writes:
            self.lastw[wn] = op
            self.readers[wn] = []
        op.deps = []
        op.dma_deps = {}
        for d in deps.values():
            if d is op:
                continue
            if d.is_dma:
                op.dma_deps[d.semkey] = self.dma_cnt[d.semkey]
            elif d.eng != eng or op.is_dma or id(d) in same_raw or eng != "pe":
                op.deps.append(d)
                d.signaled = True
        if op.is_dma:
            self.dma_cnt[dma] = self.dma_cnt.get(dma, 0) + inc
        self.ops[eng].append(op)
        return op

    def barrier(self):
        lasts = {}
        for e in ENGS:
            for op in reversed(self.ops[e]):
                if not op.is_dma and op.emit is not None:
                    lasts[e] = op
                    break
        for e in ENGS:
            op = _Op()
            op.inc = 0
            op.eng = e
            op.emit = None
            op.is_dma = False
            op.semkey = None
            op.signaled = False
            op.sig_idx = 0
            op.deps = []
            for e2, l in lasts.items():
                if e2 != e:
                    op.deps.append(l)
                    l.signaled = True
            op.dma_deps = dict(self.dma_cnt)
            self.ops[e].append(op)

    def emit_all(self, block, eng_sems, dma_sems):
        for e in ENGS:
            c = 0
            for op in self.ops[e]:
                if op.signaled and not op.is_dma:
                    c += 1
                    op.sig_idx = c

        def make(engname):
            def fn(e):
                waited = {}
                if engname == "sp":
                    _RID["v"] = e.snap(e.partition_id() % 4, min_val=0, max_val=3)
                for op in self.ops[engname]:
                    for d in op.deps:
                        key = ("e", d.eng)
                        if waited.get(key, 0) < d.sig_idx:
                            e.wait_ge(eng_sems[d.eng], d.sig_idx)
                            waited[key] = d.sig_idx
                    for k, v in op.dma_deps.items():
                        key = ("d", k)
                        if waited.get(key, 0) < v:
                            e.wait_ge(dma_sems[k], v)
                            waited[key] = v
                    if op.emit is None:
                        continue
                    ins = op.emit(e)
                    if op.is_dma:
                        ins.then_inc(dma_sems[op.semkey], op.inc)
                    elif op.signaled:
                        ins.then_inc(eng_sems[engname], 1)
            return fn

        block.tensor(make("pe"))
        block.scalar(make("act"))
        block.vector(make("dve"))
        block.gpsimd(make("pool"))
        block.sync(make("sp"))


class Arena:
    def __init__(self, nc, nbytes):
        self.h32 = nc.alloc_sbuf_tensor("arena", [128, nbytes // 4], F32)
        self.h16 = self.h32.bitcast(BF16)
        self.nbytes = nbytes
        self.off = 0

    def alloc(self, nbytes):
        nbytes = (nbytes + 63) // 64 * 64
        o = self.off
        self.off += nbytes
        assert self.off <= self.nbytes, ("SBUF arena overflow", self.off, self.nbytes)
        return o

    def f32(self, n):
        o = self.alloc(4 * n)
        return self.h32[:, o // 4:o // 4 + n]

    def bf16(self, n):
        o = self.alloc(2 * n)
        return self.h16[:, o // 2:o // 2 + n]


def _ap(t, extra_off, dims):
    return bass.AP(t.tensor, t.offset + extra_off, dims)


def build_program():
    nc = bass.Bass("TRN2", target_bir_lowering=False)
    s = Sched()
    dbg = _DEBUG["stop"]
    ORDER = ["p1", "p3", "p2a", "p2f", "p2", "p4b", "p4", "p5"]

    def upto(name):
        return dbg is None or ORDER.index(name) <= ORDER.index(dbg)

    def din(name, shape, dt=F32):
        need = {"wgate": "p4b", "wbh": "p4b", "wba": "p4b", "wout": "p4", "wfg": "p5", "wfu": "p5", "wfd": "p5",
                "dftF": "p2f", "dftI": "p2"}
        if name in need and not upto(need[name]):
            shape = [32, 128, 128] if len(shape) == 3 else [128, 128]
        _SHAPES[name] = tuple(shape)
        return nc.dram_tensor(name, list(shape), dt, kind="ExternalInput").ap()

    x_d = din("x", [S, D])
    xo_d = din("xo", [1024, D])
    wmix_d = din("wmix", [D, 1280])
    wgate_d = din("wgate", [D, 4096])
    bgate_d = din("bgate", [128, 32])
    g1_d = din("g1", [D])
    convw_d = din("convw", [128, 18])
    convb_d = din("convb", [128, 6])
    fw1_d = din("fw1", [33, 64])
    fw2_d = din("fw2", [64, 64])
    fw3_d = din("fw3", [64, 64])
    fw4_d = din("fw4", [64, 512])
    fvec_d = din("fvec", [64, 4])
    hyb_d = din("hybias", [128, 2])
    gqk_d = din("gqk", [128, 2])
    wbh_d = din("wbh", [1024, D])
    wba_d = din("wba", [1024, D])
    wout_d = din("wout", [D, D])
    g2_d = din("g2", [D])
    wfg_d = din("wfg", [D, DFF])
    wfu_d = din("wfu", [D, DFF])
    wfd_d = din("wfd", [DFF, D])
    identb_d = din("identb", [128, 128], BF16)
    identf_d = din("identf", [128, 128])
    onesdiv_d = din("onesdiv", [128, 128], BF16)
    perm_d = din("perm", [128, 128], BF16)
    ropec_d = din("ropec", [128, 64])
    ropes_d = din("ropes", [128, 64])
    zemb_d = din("zemb", [33, S])
    decay_d = din("decay", [S, 256])
    dftF_d = din("dftF", [32, 128, 2 * 32 * 128], BF16)
    dftI_d = din("dftI", [32, 128, 2 * 32 * 128], BF16)

    out_d = nc.dram_tensor("out", [1024, D], F32, kind="ExternalOutput").ap()
    yloc_d = nc.dram_tensor("yloc", [512, S], BF16, kind="Internal").ap()
    cloc_d = nc.dram_tensor("cloc", [128, S], BF16, kind="Internal").ap()
    cgat_d = nc.dram_tensor("cgat", [4 * 128, S], BF16, kind="Internal").ap()
    ystage_d = nc.dram_tensor("ystage", [4, 128, 4 * 1024], BF16, kind="Internal").ap()
    x1_d = nc.dram_tensor("x1s", [1024, D], F32, kind="Internal").ap()
    dbg_d = None
    if dbg is not None:
        dbg_d = nc.dram_tensor("dbg", [128, 8192], F32, kind="ExternalOutput").ap()

    A = Arena(nc, 207 * 1024)
    PS = [nc.alloc_psum_tensor("psb%d" % i, [128, 512], F32) for i in range(8)]
    PSB = [p.bitcast(BF16) for p in PS]

    identb = A.bf16(128)
    identf = A.f32(128)
    onesdiv = A.bf16(128)
    perm = A.bf16(128)
    ropec = A.f32(64)
    ropes = A.f32(64)
    gqk = A.f32(2)
    convw = A.f32(18)
    convb = A.f32(6)
    hyb = A.f32(2)
    bgate = A.f32(32)
    fvec = A.f32(4)
    stats = A.f32(64)
    negpi = A.f32(1)
    onescol = A.f32(1)
    epst = A.f32(1)
    small_loads = [(identb, identb_d), (identf, identf_d), (onesdiv, onesdiv_d), (perm, perm_d),
                   (ropec, ropec_d), (ropes, ropes_d), (gqk, gqk_d), (convw, convw_d), (convb, convb_d),
                   (hyb, hyb_d), (bgate, bgate_d)]
    for i, (dst, src) in enumerate(small_loads):
        s.add("sp", lambda e, dst=dst, src=src: e.dma_start(out=dst, in_=src), writes=["const%d" % i], dma="const")
    s.add("sp", lambda e: e.dma_start(out=fvec[0:64, :], in_=fvec_d), writes=["fvec"], dma="const")
    s.add("dve", lambda e: e.memset(negpi, -math.pi), writes=["negpi"])
    s.add("dve", lambda e: e.memset(onescol, 1.0), writes=["onescol"])
    s.add("dve", lambda e: e.memset(epst, EPS), writes=["epst"])
    base_mark = A.off

    def dump(ap_sb, ncols, name):
        tmp = A.f32(ncols)
        s.add("dve", lambda e: e.tensor_copy(tmp, ap_sb), reads=[name], writes=["dbgtmp"])
        s.add("sp", lambda e: e.dma_start(out=dbg_d[:, 0:ncols], in_=tmp), reads=["dbgtmp"], writes=["dbgout"], dma="dbg")

    def mm_group(bank_ap, lhs_fn, rhs_fn, nk, reads, bankname):
        def f(e):
            ins = None
            for k in range(nk):
                ins = e.matmul(bank_ap, lhs_fn(k), rhs_fn(k), start=(k == 0), stop=(k == nk - 1))
            return ins
        s.add("pe", f, reads=reads, writes=[bankname])

    cgv = cgat_d.rearrange("(q p) t -> p q t", q=4)

    def xchg_start(g):
        s.add("sp", lambda e: e.dma_start(out=cloc_d, in_=yloc_d[g * 128:(g + 1) * 128, :]),
              reads=["yloc"], writes=["cloc"], dma="cloc")
        s.add("pool", lambda e: e.collective_compute("AllGather", ALU.bypass, replica_groups=[[0, 1, 2, 3], [4, 5, 6, 7]],
                                                     ins=[cloc_d], outs=[cgat_d]),
              reads=["cloc"], writes=["cgat"], dma="cc", inc=1)

    def xchg_finish(g):
        s.add("sp", lambda e: e.dma_start(out=ystage_d[g].rearrange("p (q t) -> p q t", q=4),
                                          in_=cgv[:, :, bass.ds(_RID["v"] * 1024, 1024)]),
              reads=["cgat"], writes=["ystage%d" % g], dma="ystage")

    def transposes_to(src_bf, srcname, hdst3, col0, hname):
        for g4 in range(4):
            bank = g4 % 2
            pst = PSB[bank][:, 0:512].rearrange("p (a t) -> p a t", a=4)

            def tr(e, g4=g4, pst=pst):
                ins = None
                for a in range(4):
                    dk = g4 * 4 + a
                    ins = e.transpose(out=pst[:, a, :], in_=src_bf[:, dk * 128:(dk + 1) * 128], identity=identb)
                return ins
            s.add("pe", tr, reads=[srcname, "const0"], writes=["ps%d" % bank])
            dst = hdst3[:, g4 * 4:(g4 + 1) * 4, col0:col0 + 128]
            if g4 % 2 == 0:
                s.add("act", lambda e, dst=dst, pst=pst: e.activation(out=dst, in_=pst, func=AF.Copy),
                      reads=["ps%d" % bank], writes=[hname + "_a"])
            else:
                s.add("dve", lambda e, dst=dst, pst=pst: e.tensor_copy(dst, pst),
                      reads=["ps%d" % bank], writes=[hname + "_d"])

    def rms_scale(src_f32, srcname, dst_bf, dstname, gtile, gname, statcol, junkbuf):
        sc = stats[:, statcol:statcol + 1]
        sn = "stat%d" % statcol
        s.add("act", lambda e: e.activation(out=junkbuf, in_=src_f32, func=AF.Square, accum_out=sc),
              reads=[srcname], writes=["junk", sn])
        s.add("act", lambda e: e.activation(out=sc, in_=sc, func=AF.Sqrt, bias=epst, scale=1.0 / D), reads=[sn, "epst"], writes=[sn])
        s.add("dve", lambda e: e.reciprocal(sc, sc), reads=[sn], writes=[sn])
        s.add("dve", lambda e: e.scalar_tensor_tensor(out=dst_bf, in0=src_f32, scalar=sc, in1=gtile, op0=ALU.mult, op1=ALU.mult),
              reads=[srcname, sn, gname], writes=[dstname])

    RAW = A.bf16(6 * 4098)
    RAW3 = RAW.rearrange("p (c t) -> p c t", c=6)
    raw_end = A.off
    QT = A.bf16(2 * S)
    KT = A.bf16(S)
    V = A.bf16(32 * 128)
    QT3 = QT.rearrange("p (h t) -> p h t", h=2)
    V3 = V.rearrange("p (c d) -> p c d", c=32)
    p1_mark = A.off

    wm = A.bf16(16 * 1280)
    wm3 = wm.rearrange("p (k n) -> p k n", k=16)
    xt = [A.f32(D) for _ in range(2)]
    xs = [A.bf16(D) for _ in range(2)]
    junk = A.bf16(D)
    hT = [A.bf16(16 * 512) for _ in range(2)]
    hT3 = [h.rearrange("p (k t) -> p k t", k=16) for h in hT]
    gbc = A.f32(D)
    sqs = [A.bf16(512) for _ in range(1)]
    rstdqs = [A.f32(512)] * 3
    qns = [A.bf16(512) for _ in range(3)]
    rawq = [A.f32(512) for _ in range(3)]
    t1 = A.f32(512)
    t2 = A.f32(512)
    print("P1 arena end", A.off, "of", A.nbytes)

    wmix_v = wmix_d.rearrange("(k p) n -> p k n", p=128)
    for k0 in (0, 8):
        s.add("pool", lambda e, k0=k0: e.dma_start(out=wm3[:, k0:k0 + 8, :], in_=wmix_v[:, k0:k0 + 8, :]), writes=["wm"], dma="wm")
    s.add("sp", lambda e: e.dma_start(out=gbc, in_=g1_d.partition_broadcast(128)), writes=["gbc"], dma="gbc")
    s.add("dve", lambda e: e.memset(RAW3[:, :, 0:1], 0.0), writes=["rawpad0"])
    s.add("dve", lambda e: e.memset(RAW3[:, :, 4097:4098], 0.0), writes=["rawpad1"])
    x_v = x_d.rearrange("(n p) d -> n p d", p=128)
    pstep_c = ropec.ap[0][0]
    pstep_s = ropes.ap[0][0]

    evac_flip = 0
    for j in range(8):
        hs = j % 2
        hname = "hT%d" % hs
        for i in range(4):
            tile = 4 * j + i
            sl = i % 2
            s.add("sp", lambda e, sl=sl, tile=tile: e.dma_start(out=xt[sl], in_=x_v[tile]), writes=["xt%d" % sl], dma="xt%d" % sl)
            rms_scale(xt[sl], "xt%d" % sl, xs[sl], "xs%d" % sl, gbc, "gbc", tile, junk)
            transposes_to(xs[sl], "xs%d" % sl, hT3[hs], i * 128, hname)
        for u in range(3):
            bank = 2 + (u % 2)
            col = 768 + u * 128
            mm_group(PS[bank][:, :], lambda k, col=col: wm3[:, k, col:col + 128],
                     lambda k, hs=hs: hT3[hs][:, k, :], 16, [hname + "_a", hname + "_d", "wm"], "ps%d" % bank)
            s.add("act", lambda e, bank=bank, u=u: e.activation(out=rawq[u], in_=PS[bank][:, :], func=AF.Copy),
                  reads=["ps%d" % bank], writes=["rawq%d" % u])

        def hy_proj(cc):
            bank = 2 + (cc % 2)
            mm_group(PS[bank][:, :], lambda k, cc=cc: wm3[:, k, cc * 128:(cc + 1) * 128],
                     lambda k, hs=hs: hT3[hs][:, k, :], 16, [hname + "_a", hname + "_d", "wm"], "ps%d" % bank)
            dst = RAW3[:, cc, 1 + 512 * j:1 + 512 * (j + 1)]
            if cc % 2 == 0:
                s.add("act", lambda e, dst=dst, bank=bank: e.activation(out=dst, in_=PS[bank][:, :], func=AF.Copy),
                      reads=["ps%d" % bank], writes=["raw_a"])
            else:
                s.add("dve", lambda e, dst=dst, bank=bank: e.tensor_copy(dst, PS[bank][:, :]),
                      reads=["ps%d" % bank], writes=["raw_d"])

        def qk_square(u):
            s.add("act", lambda e, u=u: e.activation(out=sqs[0], in_=rawq[u], func=AF.Square),
                  reads=["rawq%d" % u], writes=["sq"])

        def qk_norm(u):
            s.add("pe", lambda e, u=u: e.matmul(PS[4][:, :], onesdiv, sqs[0], start=True, stop=True),
                  reads=["sq", "const2"], writes=["ps4"])
            s.add("act", lambda e, u=u: e.activation(out=rstdqs[u], in_=PS[4][:, :], func=AF.Sqrt, bias=epst, scale=1.0),
                  reads=["ps4", "epst"], writes=["rstdq"])
            s.add("dve", lambda e, u=u: e.reciprocal(rstdqs[u], rstdqs[u]), reads=["rstdq"], writes=["rstdq"])
            gcol = gqk[:, 0:1] if u < 2 else gqk[:, 1:2]
            s.add("dve", lambda e, u=u, gcol=gcol: e.scalar_tensor_tensor(
                out=qns[u], in0=rawq[u], scalar=gcol, in1=rstdqs[u], op0=ALU.mult, op1=ALU.mult),
                reads=["rawq%d" % u, "rstdq", "const6"], writes=["qn%d" % u])

        def qk_rope(u):
            qn_ = qns[u]
            s.add("pe", lambda e, qn_=qn_: e.matmul(PS[5][:, :], perm, qn_, start=True, stop=True),
                  reads=["qn%d" % u, "const3"], writes=["ps5"])
            for half in range(2):
                p0 = half * 64
                if half == 0:
                    cap = _ap(ropec, p0 * pstep_c + 8 * j, [[pstep_c, 64], [1, 8], [0, 64]])
                    sap = _ap(ropes, p0 * pstep_s + 8 * j, [[pstep_s, 64], [1, 8], [0, 64]])
                else:
                    cap = _ap(ropec, p0 * pstep_c, [[pstep_c, 64], [0, 8], [1, 64]])
                    sap = _ap(ropes, p0 * pstep_s, [[pstep_s, 64], [0, 8], [1, 64]])
                qv = qn_[p0:p0 + 64, :].rearrange("p (a b) -> p a b", a=8)
                t1v = t1[p0:p0 + 64, :].rearrange("p (a b) -> p a b", a=8)
                t2v = t2[p0:p0 + 64, :].rearrange("p (a b) -> p a b", a=8)
                pv = PS[5][p0:p0 + 64, :].rearrange("p (a b) -> p a b", a=8)
                s.add("pool", lambda e, t1v=t1v, qv=qv, cap=cap: e.tensor_tensor(out=t1v, in0=qv, in1=cap, op=ALU.mult),
                      reads=["qn%d" % u, "const4"], writes=["t1_%d" % half])
                s.add("dve", lambda e, t2v=t2v, pv=pv, sap=sap: e.tensor_tensor(out=t2v, in0=pv, in1=sap, op=ALU.mult),
                      reads=["ps5", "const5"], writes=["t2_%d" % half])
            dstq = QT3[:, u, 512 * j:512 * (j + 1)] if u < 2 else KT[:, 512 * j:512 * (j + 1)]
            s.add("pool", lambda e, dstq=dstq: e.tensor_tensor(out=dstq, in0=t1, in1=t2, op=ALU.add),
                  reads=["t1_0", "t1_1", "t2_0", "t2_1"], writes=["qk"])
        for cc in range(3):
            qk_square(cc)
            hy_proj(cc)
            qk_norm(cc)
        for cc in range(3, 6):
            hy_proj(cc)
            qk_rope(cc - 3)

        def vmm(e, hs=hs):
            ins = None
            for i in range(4):
                for k in range(16):
                    ins = e.matmul(PS[6][:, i * 128:(i + 1) * 128], hT3[hs][:, k, i * 128:(i + 1) * 128],
                                   wm3[:, k, 1152:1280], start=(k == 0), stop=(k == 15))
            return ins
        s.add("pe", vmm, reads=[hname + "_a", hname + "_d", "wm"], writes=["ps6"])
        s.add("act", lambda e, j=j: e.activation(out=V3[:, 4 * j:4 * j + 4, :],
                                                 in_=PS[6][:, :].rearrange("p (a d) -> p a d", a=4), func=AF.Copy),
              reads=["ps6"], writes=["v"])
    s.barrier()
    A.off = p1_mark
    if dbg == "p1":
        dump(QT[:, 0:8192], 8192, "qk")
        s.barrier()

    if upto("p3"):
        PT = [A.bf16(512) for _ in range(3)]
        rec = A.f32(512)
        yab = [A.bf16(512) for _ in range(2)]
        ones128 = A.bf16(128)
        s.add("dve", lambda e: e.memset(ones128, 1.0), writes=["ones128"])
        scale = 128.0 ** -0.5
        unit = 0
        for h in range(2):
            for qc in range(8):
                ob = 4 + 2 * (unit % 2)
                def st_mm(sc_):
                    sb = sc_ % 2
                    s.add("pe", lambda e, sb=sb, sc_=sc_, h=h, qc=qc: e.matmul(
                        PS[sb][:, :], KT[:, 128 * sc_:128 * (sc_ + 1)], QT3[:, h, 512 * qc:512 * (qc + 1)], start=True, stop=True),
                        reads=["qk"], writes=["ps%d" % sb])
                    pt = sc_ % 3
                    s.add("act", lambda e, sb=sb, pt=pt: e.activation(out=PT[pt], in_=PS[sb][:, :], func=AF.Exp, scale=scale),
                          reads=["ps%d" % sb], writes=["PT%d" % pt])

                def pv_mm(sc_):
                    pt = sc_ % 3
                    s.add("pe", lambda e, pt=pt, sc_=sc_, ob=ob: e.matmul(PS[ob][:, :], V3[:, sc_, :], PT[pt], start=(sc_ == 0), stop=(sc_ == 31)),
                          reads=["PT%d" % pt, "v"], writes=["ps%d" % ob])
                    s.add("pe", lambda e, pt=pt, sc_=sc_, ob=ob: e.matmul(PS[ob + 1][:, :], ones128, PT[pt], start=(sc_ == 0), stop=(sc_ == 31)),
                          reads=["PT%d" % pt, "ones128"], writes=["ps%d" % (ob + 1)])
                st_mm(0)
                for sc_ in range(32):
                    if sc_ + 1 < 32:
                        st_mm(sc_ + 1)
                    pv_mm(sc_)
                ys = unit % 2
                s.add("dve", lambda e, ob=ob: e.reciprocal(rec, PS[ob + 1][:, :]), reads=["ps%d" % (ob + 1)], writes=["rec"])
                s.add("dve", lambda e, ob=ob, ys=ys: e.tensor_tensor(out=yab[ys], in0=PS[ob][:, :], in1=rec, op=ALU.mult),
                      reads=["ps%d" % ob, "rec"], writes=["yab%d" % ys])
                s.add("sp", lambda e, h=h, qc=qc, ys=ys: e.dma_start(out=yloc_d[256 + h * 128:256 + (h + 1) * 128, 512 * qc:512 * (qc + 1)], in_=yab[ys]),
                      reads=["yab%d" % ys], writes=["yloc"], dma="yab%d" % ys)
                unit += 1
        s.barrier()
        if upto("p4b"):
            xchg_start(2)
        if dbg == "p3":
            s.add("sp", lambda e: e.dma_start(out=xt[0].bitcast(BF16), in_=yloc_d[256:384, :]), reads=["yloc"], writes=["dbgld"], dma="dbgld")
            dump(xt[0].bitcast(BF16)[:, 0:4096], 4096, "dbgld")
            s.barrier()
    A.off = raw_end

    x0T = A.bf16(2 * S)
    zT = A.bf16(2 * S)
    x0T3 = x0T.rearrange("p (c t) -> p c t", c=2)
    zT3 = zT.rearrange("p (c t) -> p c t", c=2)
    p2_mark = A.off
    if upto("p2a"):
        tbuf = [A.f32(S) for _ in range(4)]

        def conv_chain(cc, tb, tbn, eng, dst, dstn):
            w0 = convw[:, cc * 3 + 0:cc * 3 + 1]
            w1 = convw[:, cc * 3 + 1:cc * 3 + 2]
            w2 = convw[:, cc * 3 + 2:cc * 3 + 3]
            bb = convb[:, cc:cc + 1]
            s.add(eng, lambda e: e.tensor_scalar(tb, RAW3[:, cc, 1:4097], w1, bb, op0=ALU.mult, op1=ALU.add),
                  reads=["raw_a", "raw_d", "const7", "const8"], writes=[tbn])
            s.add(eng, lambda e: e.scalar_tensor_tensor(out=tb, in0=RAW3[:, cc, 0:4096], scalar=w0, in1=tb, op0=ALU.mult, op1=ALU.add),
                  reads=["raw_a", "raw_d", "rawpad0", tbn, "const7"], writes=[tbn])
            s.add(eng, lambda e: e.scalar_tensor_tensor(out=dst, in0=RAW3[:, cc, 2:4098], scalar=w2, in1=tb, op0=ALU.mult, op1=ALU.add),
                  reads=["raw_a", "raw_d", "rawpad1", tbn, "const7"], writes=[dstn])
        conv_chain(2, tbuf[0], "tb0", "dve", tbuf[0], "tb0")
        conv_chain(3, tbuf[1], "tb1", "dve", tbuf[1], "tb1")
        conv_chain(4, tbuf[2], "tb2", "dve", tbuf[2], "tb2")
        conv_chain(5, tbuf[3], "tb3", "dve", tbuf[3], "tb3")
        for c2 in range(2):
            s.add("dve", lambda e, c2=c2: e.tensor_tensor(out=zT3[:, c2, :], in0=tbuf[2 + c2], in1=tbuf[c2], op=ALU.mult),
                  reads=["tb%d" % c2, "tb%d" % (2 + c2)], writes=["zT"])
        conv_chain(0, tbuf[0], "tb0", "dve", x0T3[:, 0, :], "x0T")
        conv_chain(1, tbuf[1], "tb1", "dve", x0T3[:, 1, :], "x0T")
        s.barrier()
        if dbg == "p2a":
            A.off = p2_mark
            dump(zT[:, 0:8192], 8192, "zT")
            s.barrier()
    A.off = p2_mark

    if upto("p2f"):
        ZH = RAW[:, 0:32 * 768]
        ZH3 = ZH.rearrange("p (c n) -> p c n", c=32)
        R = A.bf16(32 * 512)
        R3 = R.rearrange("p (c n) -> p c n", c=32)
        dbuf = [A.bf16(2 * 32 * 128) for _ in range(2)]
        dbuf4 = [b.rearrange("p (a c n) -> p a c n", a=2, c=32) for b in dbuf]
        fw1 = A.f32(64)
        fw2 = A.f32(64)
        fw3 = A.f32(64)
        fw4 = A.f32(512)
        zemb = [A.f32(512) for _ in range(2)]
        hA = A.f32(512)
        hB = A.f32(512)
        uu = A.f32(512)
        val = A.f32(512)
        absv = A.f32(256)
        dec = [A.f32(256) for _ in range(2)]
        bsc = A.f32(4)
        scn = A.f32(2)
        sP = A.f32(256)
        sQ = A.f32(256)
        tt = [A.f32(256) for _ in range(4)]
        ysb = A.f32(256)
        ysb3 = ysb.rearrange("p (a c) -> p a c", a=2)
        ub = A.f32(512)
        yhs = [A.bf16(512) for _ in range(2)]
        dt_ = A.f32(4112) if dbg == "p2f" else None

        if upto("p4b"):
            xchg_finish(2)
            xchg_start(3)
        s.add("sp", lambda e: e.dma_start(out=fw1[0:33, :], in_=fw1_d), writes=["fw1"], dma="fw")
        s.add("sp", lambda e: e.dma_start(out=fw2[0:64, :], in_=fw2_d), writes=["fw2"], dma="fw")
        s.add("sp", lambda e: e.dma_start(out=fw3[0:64, :], in_=fw3_d), writes=["fw3"], dma="fw")
        s.add("sp", lambda e: e.dma_start(out=fw4[0:64, :], in_=fw4_d), writes=["fw4"], dma="fw")
        for cc in range(2):
            for g in range(8):
                bank = g % 2
                pst = PSB[bank][:, 0:512].rearrange("p (a t) -> p a t", a=4)

                def trz(e, cc=cc, g=g, pst=pst):
                    ins = None
                    for a in range(4):
                        ch = g * 4 + a
                        ins = e.transpose(out=pst[:, a, :], in_=zT3[:, cc, ch * 128:(ch + 1) * 128], identity=identb)
                    return ins
                s.add("pe", trz, reads=["zT", "const0"], writes=["ps%d" % bank])
                dst = ZH3[:, g * 4:(g + 1) * 4, 256 + cc * 128:256 + (cc + 1) * 128]
                s.add("dve", lambda e, dst=dst, pst=pst: e.tensor_copy(dst, pst), reads=["ps%d" % bank], writes=["ZHz"])
        for l in range(3):
            s.add("dve", lambda e, l=l: e.tensor_tensor(out=bsc[0:64, l:l + 1], in0=fvec[0:64, 0:1], in1=fvec[0:64, l + 1:l + 2], op=ALU.mult),
                  reads=["fvec"], writes=["bsc"])
        s.add("dve", lambda e: e.tensor_scalar(bsc[0:64, 0:3], bsc[0:64, 0:3], 1.0 / (2.0 * math.pi), 16.5, op0=ALU.mult, op1=ALU.add),
              reads=["bsc"], writes=["bsc"])
        s.add("dve", lambda e: e.tensor_scalar(bsc[0:64, 3:4], fvec[0:64, 0:1], 1.0 / (2.0 * math.pi), None, op0=ALU.mult),
              reads=["fvec", "bsc"], writes=["bsc"])
        frp = bsc[0:64, 3:4]
        ki = A.h32.bitcast(mybir.dt.int32)[:, A.alloc(4 * 512) // 4:][:, 0:512]
        kf = A.f32(512)

        def sin_layer(ps_ap, l, dst, rn, wn):
            s.add("dve", lambda e: e.tensor_scalar(uu[0:64, :], ps_ap, frp, bsc[0:64, l:l + 1], op0=ALU.mult, op1=ALU.add),
                  reads=rn + ["bsc"], writes=["uu"])
            s.add("dve", lambda e: e.tensor_copy(ki[0:64, :], uu[0:64, :]), reads=["uu"], writes=["ki"])
            s.add("dve", lambda e: e.tensor_copy(kf[0:64, :], ki[0:64, :]), reads=["ki"], writes=["kf"])
            s.add("dve", lambda e: e.tensor_tensor(out=uu[0:64, :], in0=uu[0:64, :], in1=kf[0:64, :], op=ALU.subtract),
                  reads=["uu", "kf"], writes=["uu"])
            s.add("dve", lambda e: e.scalar_tensor_tensor(out=uu[0:64, :], in0=uu[0:64, :], scalar=0.0, in1=uu[0:64, :], op0=ALU.is_lt, op1=ALU.add),
                  reads=["uu"], writes=["uu"])
            s.add("act", lambda e: e.activation(out=dst, in_=uu[0:64, :], func=AF.Sin, bias=negpi[0:64, :], scale=2.0 * math.pi),
                  reads=["uu", "negpi"], writes=wn)

        decay_v = decay_d.rearrange("(c p) n -> c p n", p=128)
        pstep_d = dec[0].ap[0][0]
        for j in range(8):
            zs = j % 2
            s.add("sp", lambda e, j=j, zs=zs: e.dma_start(out=zemb[zs][0:33, :], in_=zemb_d[:, 512 * j:512 * (j + 1)]),
                  writes=["zemb%d" % zs], dma="zemb%d" % zs)
            s.add("pe", lambda e, zs=zs: e.matmul(PS[2][0:64, :], fw1[0:33, :], zemb[zs][0:33, :], start=True, stop=True),
                  reads=["fw1", "zemb%d" % zs], writes=["ps2"])
            sin_layer(PS[2][0:64, :], 0, hA[0:64, :], ["ps2"], ["hA"])
            if dbg == "p2f" and j == 0:
                s.add("dve", lambda e: e.tensor_copy(dt_[0:64, 2048:2560], hA[0:64, :]), reads=["hA"], writes=["dbgtmp"])
                s.add("dve", lambda e: e.tensor_copy(dt_[64:128, 2048:2560], PS[2][0:64, :]), reads=["hA", "ps2"], writes=["dbgtmp"])
            s.add("pe", lambda e: e.matmul(PS[3][0:64, :], fw2[0:64, :], hA[0:64, :], start=True, stop=True),
                  reads=["fw2", "hA"], writes=["ps3"])
            sin_layer(PS[3][0:64, :], 1, hB[0:64, :], ["ps3"], ["hB"])
            if dbg == "p2f" and j == 0:
                s.add("dve", lambda e: e.tensor_copy(dt_[0:64, 2560:3072], hB[0:64, :]), reads=["hB"], writes=["dbgtmp"])
            s.add("pe", lambda e: e.matmul(PS[2][0:64, :], fw3[0:64, :], hB[0:64, :], start=True, stop=True),
                  reads=["fw3", "hB"], writes=["ps2"])
            if dbg == "p2f" and j == 0:
                s.add("dve", lambda e: e.tensor_copy(dt_[64:128, 2560:3072], PS[2][0:64, :]), reads=["ps2"], writes=["dbgtmp"])
            sin_layer(PS[2][0:64, :], 2, hA[0:64, :], ["ps2"], ["hA"])
            if dbg == "p2f" and j == 0:
                s.add("dve", lambda e: e.tensor_copy(dt_[64:128, 3072:3584], uu[0:64, :]), reads=["uu"], writes=["dbgtmp"])
                s.add("dve", lambda e: e.tensor_copy(dt_[0:64, 3072:3584], hA[0:64, :]), reads=["hA"], writes=["dbgtmp"])
            for i in range(4):
                ch = 4 * j + i
                ds_ = ch % 2
                s.add("sp", lambda e, ch=ch, ds_=ds_: e.dma_start(out=dec[ds_], in_=decay_v[ch]),
                      writes=["dec%d" % ds_], dma="dec%d" % ds_)
                s.add("pe", lambda e, i=i: e.matmul(PS[3][:, :], hA[0:64, i * 128:(i + 1) * 128], fw4[0:64, :], start=True, stop=True),
                      reads=["hA", "fw4"], writes=["ps3"])
                dbc = _ap(dec[ds_], 0, [[pstep_d, 128], [0, 2], [1, 256]])
                s.add("dve", lambda e, dbc=dbc: e.tensor_tensor(out=val.rearrange("p (a c) -> p a c", a=2),
                                                                 in0=PS[3][:, :].rearrange("p (a c) -> p a c", a=2), in1=dbc, op=ALU.mult),
                      reads=["ps3", "dec%d" % ds_], writes=["val"])
                if ch == 0:
                    s.add("dve", lambda e: e.memset(val[0:1, 256:512], 0.0), reads=["val"], writes=["val"])
                s.add("dve", lambda e, ch=ch: e.tensor_tensor(out=ZH3[:, ch, 0:256], in0=val[:, 0:256], in1=val[:, 256:512], op=ALU.add),
                      reads=["val"], writes=["ZHp"])
                s.add("dve", lambda e, ch=ch: e.tensor_tensor(out=ZH3[:, ch, 512:768], in0=val[:, 0:256], in1=val[:, 256:512], op=ALU.subtract),
                      reads=["val"], writes=["ZHm"])
                s.add("act", lambda e: e.activation(out=val, in_=val, func=AF.Abs), reads=["val", "ZHp", "ZHm"], writes=["val"])
                s.add("dve", lambda e: e.tensor_tensor(out=absv, in0=val[:, 0:256], in1=val[:, 256:512], op=ALU.add),
                      reads=["val"], writes=["absv"])
                for c2 in range(2):
                    s.add("pe", lambda e, c2=c2, ch=ch: e.matmul(PS[4 + c2][:, 0:1], absv[:, c2 * 128:(c2 + 1) * 128], onescol,
                                                                  start=(ch == 0), stop=(ch == 31)),
                          reads=["absv", "onescol"], writes=["ps%d" % (4 + c2)])
        for c2 in range(2):
            s.add("dve", lambda e, c2=c2: e.reciprocal(scn[:, c2:c2 + 1], PS[4 + c2][:, 0:1]), reads=["ps%d" % (4 + c2)], writes=["scn"])
        s.add("dve", lambda e: e.tensor_scalar(scn, scn, 2.0 / 8192.0, None, op0=ALU.mult), reads=["scn"], writes=["scn"])

        ZHALL = ["ZHz", "ZHp", "ZHm"]
        for i in range(32):
            sl = i % 2
            dn = "dbuf%d" % sl
            s.add("sp", lambda e, i=i, sl=sl: e.dma_start(out=dbuf[sl], in_=dftF_d[i]), writes=[dn], dma=dn)
            mm_group(PS[0][:, :], lambda k, sl=sl: dbuf4[sl][:, 0, k, :], lambda k: ZH3[:, k, 0:512], 32, [dn] + ZHALL, "ps0")
            mm_group(PS[1][:, :], lambda k, sl=sl: dbuf4[sl][:, 1, k, :], lambda k: ZH3[:, k, 256:768], 32, [dn] + ZHALL, "ps1")
            s.add("act", lambda e: e.activation(out=sP, in_=PS[0][:, 0:256], func=AF.Copy), reads=["ps0"], writes=["sP"])
            s.add("act", lambda e: e.activation(out=sQ, in_=PS[1][:, 256:512], func=AF.Copy), reads=["ps1"], writes=["sQ"])
            s.add("dve", lambda e: e.tensor_tensor(out=tt[0], in0=PS[0][:, 256:512], in1=sP, op=ALU.mult), reads=["ps0", "sP"], writes=["tt0"])
            s.add("dve", lambda e: e.tensor_tensor(out=tt[1], in0=PS[1][:, 0:256], in1=sQ, op=ALU.mult), reads=["ps1", "sQ"], writes=["tt1"])
            s.add("dve", lambda e: e.tensor_tensor(out=tt[2], in0=PS[0][:, 256:512], in1=sQ, op=ALU.mult), reads=["ps0", "sQ"], writes=["tt2"])
            s.add("dve", lambda e: e.tensor_tensor(out=tt[3], in0=PS[1][:, 0:256], in1=sP, op=ALU.mult), reads=["ps1", "sP"], writes=["tt3"])
            s.add("pool", lambda e, i=i: e.tensor_tensor(out=R3[:, i, 0:256], in0=tt[0], in1=tt[1], op=ALU.subtract), reads=["tt0", "tt1"], writes=["R"])
            s.add("pool", lambda e, i=i: e.tensor_tensor(out=R3[:, i, 256:512], in0=tt[2], in1=tt[3], op=ALU.add), reads=["tt2", "tt3"], writes=["R"])
        if dbg == "p2f":
            s.barrier()
            s.add("dve", lambda e: e.tensor_copy(dt_[:, 0:1024].rearrange("p (c n) -> p c n", c=4), ZH3[:, 0:4, 0:256]), reads=["ZHp"], writes=["dbgtmp"])
            s.add("dve", lambda e: e.tensor_copy(dt_[:, 1024:2048].rearrange("p (c n) -> p c n", c=4), ZH3[:, 0:4, 512:768]), reads=["ZHm"], writes=["dbgtmp"])
            s.add("dve", lambda e: e.tensor_copy(dt_[:, 3584:4096], R3[:, 5, :]), reads=["R"], writes=["dbgtmp"])
            s.add("dve", lambda e: e.tensor_copy(dt_[:, 3072:3584], R3[:, 0, :]), reads=["R"], writes=["dbgtmp"])
            s.add("dve", lambda e: e.tensor_copy(dt_[:, 4096:4098], scn), reads=["scn"], writes=["dbgtmp"])
            s.add("sp", lambda e: e.dma_start(out=dbg_d[:, 0:4112], in_=dt_), reads=["dbgtmp"], writes=["dbgout"], dma="dbg")
            s.barrier()
        if upto("p4b"):
            xchg_finish(3)
        ysbs = [ysb, A.f32(256)]
        ysb3s = [y.rearrange("p (a c) -> p a c", a=2) for y in ysbs]

        def inv_issue(tch):
            sl = tch % 2
            dn = "dbuf%d" % sl
            ib = tch % 2
            s.add("sp", lambda e, tch=tch, sl=sl: e.dma_start(out=dbuf[sl], in_=dftI_d[tch]), writes=[dn], dma=dn)

            def inv(e, sl=sl, ib=ib):
                ins = None
                for k in range(32):
                    ins = e.matmul(PS[ib][:, 0:256], dbuf4[sl][:, 0, k, :], R3[:, k, 0:256], start=(k == 0), stop=False)
                    ins = e.matmul(PS[ib][:, 0:256], dbuf4[sl][:, 1, k, :], R3[:, k, 256:512], start=False, stop=(k == 31))
                return ins
            s.add("pe", inv, reads=[dn, "R"], writes=["ps%d" % ib])
            s.add("act", lambda e, ib=ib: e.activation(out=ysbs[ib], in_=PS[ib][:, 0:256], func=AF.Copy), reads=["ps%d" % ib], writes=["ysb%d" % ib])

        def inv_transposes(tch):
            ib = tch % 2
            a = tch % 4
            tb0 = 2 + 2 * ((tch // 4) % 2)
            for c2 in range(2):
                s.add("pe", lambda e, c2=c2, a=a, ib=ib, tb0=tb0: e.transpose(out=PS[tb0 + c2][:, a * 128:(a + 1) * 128], in_=ysb3s[ib][:, c2, :], identity=identf),
                      reads=["ysb%d" % ib, "const1"], writes=["ps%d" % (tb0 + c2)])

        ubs = [ub, A.f32(512)]

        def epilogue(tg):
            tb0 = 2 + 2 * (tg % 2)
            for c2 in range(2):
                tsl = slice(512 * tg, 512 * (tg + 1))
                yb = (tg * 2 + c2) % 2
                ubx = ubs[c2]
                un = "ub%d" % c2
                s.add("pool", lambda e, c2=c2, tsl=tsl, ubx=ubx: e.tensor_scalar(ubx, zT3[:, c2, tsl], hyb[:, c2:c2 + 1], None, op0=ALU.mult),
                      reads=["zT", "const9"], writes=[un])
                s.add("dve", lambda e, c2=c2, ubx=ubx, tb0=tb0: e.scalar_tensor_tensor(out=ubx, in0=PS[tb0 + c2][:, :], scalar=scn[:, c2:c2 + 1], in1=ubx,
                                                                                  op0=ALU.mult, op1=ALU.add),
                      reads=["ps%d" % (tb0 + c2), "scn", un], writes=[un])
                s.add("dve", lambda e, c2=c2, tsl=tsl, yb=yb, ubx=ubx: e.tensor_tensor(out=yhs[yb], in0=ubx, in1=x0T3[:, c2, tsl], op=ALU.mult),
                      reads=[un, "x0T"], writes=["yhs%d" % yb])
                s.add("sp", lambda e, c2=c2, tsl=tsl, yb=yb: e.dma_start(out=yloc_d[c2 * 128:(c2 + 1) * 128, tsl], in_=yhs[yb]),
                      reads=["yhs%d" % yb], writes=["yloc"], dma="yhs%d" % yb)
        if upto("p2"):
            inv_issue(0)
            for tch in range(32):
                if tch + 1 < 32:
                    inv_issue(tch + 1)
                inv_transposes(tch)
                if tch % 4 == 3:
                    epilogue(tch // 4)
        s.barrier()
        if dbg == "p2":
            s.add("sp", lambda e: e.dma_start(out=dbuf[0][:, 0:4096], in_=yloc_d[0:128, :]), reads=["yloc"], writes=["dbgld2"], dma="dbgld")
            tmpf = dbuf[1].bitcast(F32)
            s.add("dve", lambda e: e.tensor_copy(tmpf, dbuf[0][:, 0:4096]), reads=["dbgld2"], writes=["dbgtmp"])
            s.add("sp", lambda e: e.dma_start(out=dbg_d[:, 0:4096], in_=tmpf), reads=["dbgtmp"], writes=["dbgout"], dma="dbg")
            s.barrier()
    A.off = base_mark

    if upto("p4b"):
        xchg_start(0)
        G = A.bf16(32 * 1024)
        G3 = G.rearrange("p (k t) -> p k t", k=32)
        mT3 = G3[:, 0:16, :]
        mT = G[:, 0:16 * 1024]
        mt_end = A.off - 16 * 1024 * 2
        hTo = A.bf16(16 * 1024)
        hTo3 = hTo.rearrange("p (k t) -> p k t", k=16)
        YT = A.bf16(16 * 1024)
        YT3 = YT.rearrange("p (k t) -> p k t", k=16)
        p4_mark = A.off
        h2T_off = 175 * 1024
        h2T = A.h16[:, h2T_off // 2:h2T_off // 2 + 16 * 1024]
        h2T3 = h2T.rearrange("p (k t) -> p k t", k=16)
        xt2 = [A.f32(D) for _ in range(2)]
        xs2 = [A.bf16(D) for _ in range(2)]
        junk2 = A.bf16(D)
        gbc2 = A.f32(D)
        WBLK = 256
        wg = [A.bf16(2 * 16 * WBLK) for _ in range(2)]
        wg4 = [w.rearrange("p (a k n) -> p a k n", a=2, k=16) for w in wg]
        s.add("sp", lambda e: e.dma_start(out=gbc2, in_=g1_d.partition_broadcast(128)), writes=["gbc2"], dma="gbc")
        xo_v = xo_d.rearrange("(n p) d -> n p d", p=128)
        for t in range(8):
            sl = t % 2
            s.add("sp", lambda e, sl=sl, t=t: e.dma_start(out=xt2[sl], in_=xo_v[t]), writes=["oxt%d" % sl], dma="oxt%d" % sl)
            rms_scale(xt2[sl], "oxt%d" % sl, xs2[sl], "oxs%d" % sl, gbc2, "gbc2", 32 + t, junk2)
            transposes_to(xs2[sl], "oxs%d" % sl, hTo3, t * 128, "ohT")
        wgate_v = wgate_d.rearrange("(k p) n -> p k n", p=128)
        wbh_v = wbh_d.rearrange("(k p) n -> p k n", p=128)
        wba_v = wba_d.rearrange("(k p) n -> p k n", p=128)
        NJB = D // WBLK

        def load_gblock(jb):
            sl = jb % 2
            wn = "wg%d" % sl
            c0 = jb * WBLK
            s.add("pool", lambda e, sl=sl, c0=c0: e.dma_start(out=wg4[sl][:, 0, :, :], in_=wgate_v[:, :, c0:c0 + WBLK]), writes=[wn], dma=wn)
            s.add("pool", lambda e, sl=sl, c0=c0: e.dma_start(out=wg4[sl][:, 1, :, :], in_=wgate_v[:, :, D + c0:D + c0 + WBLK]), writes=[wn], dma=wn)
        load_gblock(0)
        load_gblock(1)
        xchg_finish(0)
        xchg_start(1)
        cntA = 0
        for jb in range(NJB):
            sl = jb % 2
            wn = "wg%d" % sl
            for sub in range(WBLK // 128):
                jc = jb * (WBLK // 128) + sub
                so = sub * 128
                for tc_ in range(2):
                    tsl = slice(512 * tc_, 512 * (tc_ + 1))
                    par = cntA % 3
                    cntA += 1
                    b0, b1 = 2 + 2 * par, 3 + 2 * par
                    mm_group(PS[b0][:, :], lambda k, sl=sl, so=so: wg4[sl][:, 0, k, so:so + 128], lambda k, tsl=tsl: hTo3[:, k, tsl], 16, [wn, "ohT_a", "ohT_d"], "ps%d" % b0)
                    mm_group(PS[b1][:, :], lambda k, sl=sl, so=so: wg4[sl][:, 1, k, so:so + 128], lambda k, tsl=tsl: hTo3[:, k, tsl], 16, [wn, "ohT_a", "ohT_d"], "ps%d" % b1)
                    s.add("act", lambda e, jc=jc, b0=b0, tsl=tsl: e.activation(out=G3[:, jc, tsl], in_=PS[b0][:, :], func=AF.Sigmoid, bias=bgate[:, jc:jc + 1], scale=1.0),
                          reads=["ps%d" % b0, "const10"], writes=["gh%d" % jc])
                    s.add("act", lambda e, jc=jc, b1=b1, tsl=tsl: e.activation(out=G3[:, 16 + jc, tsl], in_=PS[b1][:, :], func=AF.Sigmoid, bias=bgate[:, 16 + jc:17 + jc], scale=1.0),
                          reads=["ps%d" % b1, "const10"], writes=["ga%d" % jc])
            if jb + 2 < NJB:
                load_gblock(jb + 2)
        xchg_finish(1)
        for g in range(4):
            kind, c2 = g // 2, g % 2
            base = kind * 8 + c2
            dst = _ap(YT, base * 1024, [[YT.ap[0][0], 128], [2 * 1024, 4], [1, 1024]])
            s.add("sp", lambda e, g=g, dst=dst: e.dma_start(out=dst, in_=ystage_d[g].rearrange("p (q t) -> p q t", q=4)),
                  reads=["ystage%d" % g], writes=["YT"], dma="YT")
        s.barrier()
        A.off = p4_mark
        wb = [A.bf16(2 * 8 * WBLK) for _ in range(2)]
        wb4 = [w.rearrange("p (a k n) -> p a k n", a=2, k=8) for w in wb]
        mtmp = [[A.f32(512) for _ in range(2)] for _ in range(2)]

        def load_bblock(jb):
            sl = jb % 2
            wn = "wb%d" % sl
            c0 = jb * WBLK
            s.add("pool", lambda e, sl=sl, c0=c0: e.dma_start(out=wb4[sl][:, 0, :, :], in_=wbh_v[:, :, c0:c0 + WBLK]), writes=[wn], dma=wn)
            s.add("pool", lambda e, sl=sl, c0=c0: e.dma_start(out=wb4[sl][:, 1, :, :], in_=wba_v[:, :, c0:c0 + WBLK]), writes=[wn], dma=wn)
        load_bblock(0)
        cntB = 0
        for jb in range(NJB):
            sl = jb % 2
            wn = "wb%d" % sl
            if jb + 1 < NJB:
                load_bblock(jb + 1)
            for sub in range(WBLK // 128):
                jc = jb * (WBLK // 128) + sub
                so = sub * 128
                for tc_ in range(2):
                    tsl = slice(512 * tc_, 512 * (tc_ + 1))
                    par = cntB % 2
                    cntB += 1
                    b2, b3 = (2, 3) if par == 0 else (4, 5)
                    m1_, m2_ = mtmp[par]
                    pn = "_%d" % par
                    mm_group(PS[b2][:, :], lambda k, sl=sl, so=so: wb4[sl][:, 0, k, so:so + 128], lambda k, tsl=tsl: YT3[:, k, tsl], 8, [wn, "YT"], "ps%d" % b2)
                    mm_group(PS[b3][:, :], lambda k, sl=sl, so=so: wb4[sl][:, 1, k, so:so + 128], lambda k, tsl=tsl: YT3[:, 8 + k, tsl], 8, [wn, "YT"], "ps%d" % b3)
                    s.add("dve", lambda e, b2=b2, m1_=m1_, jc=jc, tsl=tsl: e.tensor_tensor(out=m1_, in0=PS[b2][:, :], in1=G3[:, jc, tsl], op=ALU.mult),
                          reads=["ps%d" % b2, "gh%d" % jc], writes=["m1" + pn])
                    s.add("dve", lambda e, b3=b3, m2_=m2_, jc=jc, tsl=tsl: e.tensor_tensor(out=m2_, in0=PS[b3][:, :], in1=G3[:, 16 + jc, tsl], op=ALU.mult),
                          reads=["ps%d" % b3, "ga%d" % jc], writes=["m2" + pn])
                    s.add("pool", lambda e, jc=jc, tsl=tsl, m1_=m1_, m2_=m2_: e.tensor_tensor(out=mT3[:, jc, tsl], in0=m1_, in1=m2_, op=ALU.add),
                          reads=["m1" + pn, "m2" + pn], writes=["gh%d" % jc])
        s.barrier()
        if dbg == "p4b":
            A.off = p4_mark
            dump(mT[:, 0:8192], 8192, "mT")
            s.barrier()
    if upto("p4"):
        A.off = mt_end
        x1 = A.f32(8 * D)
        x13 = x1.rearrange("p (t d) -> p t d", t=8)
        wo = [A.bf16(16 * 512) for _ in range(2)]
        wo3 = [w.rearrange("p (k n) -> p k n", k=16) for w in wo]
        xs3 = [A.bf16(D) for _ in range(2)]
        junk3 = A.bf16(D)
        gbc3 = A.f32(D)
        assert A.off <= h2T_off
        s.add("sp", lambda e: e.dma_start(out=gbc3, in_=g2_d.partition_broadcast(128)), writes=["gbc3"], dma="gbc")
        for t in range(8):
            s.add("sp", lambda e, t=t: e.dma_start(out=x13[:, t, :], in_=xo_v[t]), writes=["x1_%d" % t], dma="x1ld")
        wout_v = wout_d.rearrange("(k p) n -> p k n", p=128)
        for db in range(4):
            sl = db % 2
            wn = "wo%d" % sl
            s.add("pool", lambda e, sl=sl, db=db: e.dma_start(out=wo3[sl], in_=wout_v[:, :, 512 * db:512 * (db + 1)]), writes=[wn], dma=wn)
            for t in range(8):
                bank = 2 + (t % 2)
                mm_group(PS[bank][:, :], lambda k, t=t: mT3[:, k, 128 * t:128 * (t + 1)], lambda k, sl=sl: wo3[sl][:, k, :], 16, [wn, "mT"], "ps%d" % bank)
                dsl = slice(512 * db, 512 * (db + 1))
                s.add("dve", lambda e, t=t, dsl=dsl, bank=bank: e.tensor_tensor(out=x13[:, t, dsl], in0=PS[bank][:, :], in1=x13[:, t, dsl], op=ALU.add),
                      reads=["ps%d" % bank, "x1_%d" % t], writes=["x1_%d" % t])
        x1_v = x1_d.rearrange("(n p) d -> n p d", p=128)
        for t in range(8):
            s.add("sp", lambda e, t=t: e.dma_start(out=x1_v[t], in_=x13[:, t, :]), reads=["x1_%d" % t], writes=["x1d"], dma="x1st")
            sl = t % 2
            rms_scale(x13[:, t, :], "x1_%d" % t, xs3[sl], "fxs%d" % sl, gbc3, "gbc3", 48 + t, junk3)
            transposes_to(xs3[sl], "fxs%d" % sl, h2T3, t * 128, "fhT")
        s.barrier()
        if dbg == "p4":
            A.off = mt_end + 8 * D * 4
            dump(x1[:, 0:8192], 8192, "x1_3")
            s.barrier()
    if upto("p5"):
        A.off = base_mark
        aT = A.bf16(NFF * 1024)
        aT3 = aT.rearrange("p (f t) -> p f t", f=NFF)
        p5_mark = A.off
        wgu = [A.bf16(2 * 16 * 256) for _ in range(2)]
        wgu4 = [w.rearrange("p (a k n) -> p a k n", a=2, k=16) for w in wgu]
        sgl = [A.f32(512) for _ in range(2)]
        assert A.off <= h2T_off
        wfg_v = wfg_d.rearrange("(k p) n -> p k n", p=128)
        wfu_v = wfu_d.rearrange("(k p) n -> p k n", p=128)
        for fb in range(DFF // 256):
            sl = fb % 2
            wn = "wgu%d" % sl
            c0 = fb * 256
            s.add("pool", lambda e, sl=sl, c0=c0: e.dma_start(out=wgu4[sl][:, 0, :, :], in_=wfg_v[:, :, c0:c0 + 256]), writes=[wn], dma=wn)
            s.add("pool", lambda e, sl=sl, c0=c0: e.dma_start(out=wgu4[sl][:, 1, :, :], in_=wfu_v[:, :, c0:c0 + 256]), writes=[wn], dma=wn)
            for sub in range(2):
                fc = fb * 2 + sub
                so = sub * 128
                for tc_ in range(2):
                    tsl = slice(512 * tc_, 512 * (tc_ + 1))
                    u = (fc * 2 + tc_) % 2
                    bg, bu = 2 + 2 * u, 3 + 2 * u
                    mm_group(PS[bg][:, :], lambda k, sl=sl, so=so: wgu4[sl][:, 0, k, so:so + 128], lambda k, tsl=tsl: h2T3[:, k, tsl], 16, [wn, "fhT_a", "fhT_d"], "ps%d" % bg)
                    mm_group(PS[bu][:, :], lambda k, sl=sl, so=so: wgu4[sl][:, 1, k, so:so + 128], lambda k, tsl=tsl: h2T3[:, k, tsl], 16, [wn, "fhT_a", "fhT_d"], "ps%d" % bu)
                    s.add("act", lambda e, u=u, bg=bg: e.activation(out=sgl[u], in_=PS[bg][:, :], func=AF.Silu), reads=["ps%d" % bg], writes=["sgl%d" % u])
                    s.add("dve", lambda e, u=u, bu=bu, fc=fc, tsl=tsl: e.tensor_tensor(out=aT3[:, fc, tsl], in0=PS[bu][:, :], in1=sgl[u], op=ALU.mult),
                          reads=["ps%d" % bu, "sgl%d" % u], writes=["aT"])
        s.barrier()
        A.off = p5_mark
        wd = [A.bf16(NFF * 512) for _ in range(2)]
        wd3 = [w.rearrange("p (f n) -> p f n", f=NFF) for w in wd]
        xr = [A.f32(512) for _ in range(2)]
        ot = [A.f32(512) for _ in range(2)]
        wfd_v = wfd_d.rearrange("(f p) n -> p f n", p=128)
        out_v = out_d.rearrange("(n p) d -> n p d", p=128)
        cnt = 0
        for db in range(4):
            sl = db % 2
            wn = "wd%d" % sl
            for f0 in (0, 22):
                s.add("pool", lambda e, sl=sl, f0=f0, db=db: e.dma_start(out=wd3[sl][:, f0:f0 + 22, :], in_=wfd_v[:, f0:f0 + 22, 512 * db:512 * (db + 1)]), writes=[wn], dma=wn)
            for t in range(8):
                bank = 2 + (t % 2)
                u = cnt % 2
                cnt += 1
                dsl = slice(512 * db, 512 * (db + 1))
                s.add("sp", lambda e, t=t, dsl=dsl, u=u: e.dma_start(out=xr[u], in_=x1_v[t][:, dsl]), reads=["x1d"], writes=["xr%d" % u], dma="xr%d" % u)
                mm_group(PS[bank][:, :], lambda k, t=t: aT3[:, k, 128 * t:128 * (t + 1)], lambda k, sl=sl: wd3[sl][:, k, :], NFF, [wn, "aT"], "ps%d" % bank)
                s.add("dve", lambda e, u=u, bank=bank: e.tensor_tensor(out=ot[u], in0=PS[bank][:, :], in1=xr[u], op=ALU.add),
                      reads=["ps%d" % bank, "xr%d" % u], writes=["ot%d" % u])
                s.add("sp", lambda e, t=t, dsl=dsl, u=u: e.dma_start(out=out_v[t][:, dsl], in_=ot[u]), reads=["ot%d" % u], writes=["outd"], dma="ot%d" % u)
    s.barrier()

    dma_keys = list(s.dma_cnt.keys())
    eng_sems = {e: nc.alloc_semaphore("sem_" + e) for e in ENGS}
    dma_sems = {k: nc.alloc_semaphore("dsem_%d" % i) for i, k in enumerate(dma_keys)}
    with nc.Block() as block:
        s.emit_all(block, eng_sems, dma_sems)
    return nc


_CONST_CACHE = {}


def _constants():
    if _CONST_CACHE:
        return _CONST_CACHE
    bf = ml_dtypes.bfloat16
    c = {}
    c["identb"] = np.eye(128, dtype=np.float32).astype(bf)
    c["identf"] = np.eye(128, dtype=np.float32)
    c["onesdiv"] = np.full((128, 128), 1.0 / 128.0, dtype=np.float32).astype(bf)
    perm = np.zeros((128, 128), dtype=np.float32)
    for d in range(128):
        partner = d + 32 if (d % 64) < 32 else d - 32
        perm[partner, d] = 1.0
    c["perm"] = perm.astype(bf)
    inv = (10000.0 ** (-np.arange(0, 64, 2, dtype=np.float32) / 64.0)).astype(np.float32)
    pos = np.arange(64, dtype=np.float32)
    ropec = np.zeros((128, 64), dtype=np.float32)
    ropes = np.zeros((128, 64), dtype=np.float32)
    for d in range(128):
        ang = (pos * inv[d % 32]).astype(np.float32)
        ropec[d] = np.cos(ang)
        sgn = -1.0 if (d % 64) < 32 else 1.0
        ropes[d] = sgn * np.sin(ang)
    c["ropec"] = ropec
    c["ropes"] = ropes
    L = S
    posf = np.arange(L, dtype=np.float32)
    t = (posf / np.float32(L - 1)).astype(np.float32)
    fb = np.linspace(1e-4, 15, 16, dtype=np.float32)
    ang = ((np.float32(2.0 * math.pi) * posf / np.float32(L))[:, None] * fb[None, :]).astype(np.float32)
    zemb = np.concatenate([t[:, None], np.cos(ang), -np.sin(ang)], axis=-1).astype(np.float32)
    c["zemb"] = np.ascontiguousarray(zemb.T)
    max_decay = math.log(1e-2) / 0.3
    min_decay = math.log(1e-2) / 1.5
    deltas = np.abs(np.linspace(min_decay, max_decay, 1024, dtype=np.float32))
    c["decay_full"] = np.exp(-t[:, None] * deltas[None, :]).astype(np.float32)
    m = np.arange(S, dtype=np.int64)[:, None]
    f = np.arange(S, dtype=np.int64)[None, :]
    ph = (m * (2 * f + 1)) % (2 * 8192)
    th = ph.astype(np.float64) * (2.0 * math.pi / (2 * 8192))
    C = np.cos(th).astype(np.float32).astype(bf)
    Sn = np.sin(th).astype(np.float32).astype(bf)
    def fwd_layout(M):
        return M.reshape(32, 128, 32, 128).transpose(2, 1, 0, 3)
    def inv_layout(M):
        return M.reshape(32, 128, 32, 128).transpose(0, 3, 2, 1)
    c["dftF"] = np.ascontiguousarray(np.stack([fwd_layout(C), fwd_layout(Sn)], axis=2)).reshape(32, 128, 2 * 32 * 128)
    c["dftI"] = np.ascontiguousarray(np.stack([inv_layout(C), inv_layout(Sn)], axis=2)).reshape(32, 128, 2 * 32 * 128)
    _CONST_CACHE.update(c)
    return _CONST_CACHE


def _chunkcols(v):
    v = np.asarray(v, dtype=np.float32)
    return np.ascontiguousarray(v.reshape(-1, 128).T)


def _core_inputs(inp, b, r):
    c = _constants()
    f32 = np.float32
    w_in = inp["w_in"][0]
    hsl = lambda base: slice(base + 256 * r, base + 256 * (r + 1))
    kvh = r // 2
    cols = np.concatenate([
        np.arange(0 + 256 * r, 0 + 256 * (r + 1)),
        np.arange(1024 + 256 * r, 1024 + 256 * (r + 1)),
        np.arange(2048 + 256 * r, 2048 + 256 * (r + 1)),
        np.arange(3072 + 256 * r, 3072 + 256 * (r + 1)),
        np.arange(4096 + 128 * kvh, 4096 + 128 * (kvh + 1)),
        np.arange(4352 + 128 * kvh, 4352 + 128 * (kvh + 1)),
    ])
    m = {}
    m["x"] = np.ascontiguousarray(inp["x"][b])
    m["xo"] = np.ascontiguousarray(inp["x"][b, 1024 * r:1024 * (r + 1)])
    m["wmix"] = np.ascontiguousarray(w_in[:, cols])
    m["wgate"] = np.ascontiguousarray(w_in[:, 4608:8704])
    m["bgate"] = _chunkcols(inp["b_gate"][0])
    m["g1"] = np.ascontiguousarray(inp["mix_norm_g"][0])
    cw = inp["hy_conv_w"][0]
    cb = inp["hy_conv_b"][0]
    convw = np.zeros((128, 18), dtype=f32)
    convb = np.zeros((128, 6), dtype=f32)
    for cc in range(6):
        base = (cc // 2) * 1024 + 256 * r + (cc % 2) * 128
        for j in range(3):
            convw[:, cc * 3 + j] = cw[j, base:base + 128]
        convb[:, cc] = cb[base:base + 128]
    m["convw"] = convw
    m["convb"] = convb
    m["fw1"] = np.ascontiguousarray(inp["flt_w1"][0])
    m["fw2"] = np.ascontiguousarray(inp["flt_w2"][0])
    m["fw3"] = np.ascontiguousarray(inp["flt_w3"][0])
    w4 = inp["flt_w4"][0]
    m["fw4"] = np.ascontiguousarray(np.concatenate([w4[:, hsl(0)], w4[:, hsl(1024)]], axis=1))
    m["fvec"] = np.ascontiguousarray(np.stack([inp["flt_freq"][0], inp["flt_b1"][0], inp["flt_b2"][0], inp["flt_b3"][0]], axis=1))
    m["hybias"] = _chunkcols(inp["hy_bias"][0][hsl(0)])
    m["gqk"] = np.ascontiguousarray(np.stack([inp["q_norm_g"][0], inp["k_norm_g"][0]], axis=1))
    m["wbh"] = inp["w_br_hyena"][0]
    m["wba"] = inp["w_br_attn"][0]
    m["wout"] = inp["w_out"][0]
    m["g2"] = np.ascontiguousarray(inp["ffn_norm_g"][0])
    m["wfg"] = inp["w_ffn_gate"][0]
    m["wfu"] = inp["w_ffn_up"][0]
    m["wfd"] = inp["w_ffn_down"][0]
    for k in ("identb", "identf", "onesdiv", "perm", "ropec", "ropes", "zemb", "dftF", "dftI"):
        m[k] = c[k]
    m["decay"] = np.ascontiguousarray(c["decay_full"][:, 256 * r:256 * (r + 1)])
    out = {}
    for k, v in m.items():
        v = np.asarray(v)
        if k in _SHAPES and tuple(v.shape) != _SHAPES[k]:
            v = np.zeros(_SHAPES[k], dtype=v.dtype)
        out[k] = np.ascontiguousarray(v)
    return out


_NC_CACHE = {}


def kernel(**inputs):
    inp = {k: np.asarray(v) for k, v in inputs.items()}
    key = _DEBUG["stop"]
    if key not in _NC_CACHE:
        _NC_CACHE[key] = build_program()
    nc = _NC_CACHE[key]
    in_maps = [_core_inputs(inp, c // 4, c % 4) for c in range(8)]
    res = run_bass_kernel_spmd(nc, in_maps, core_ids=list(range(8)))
    if key is not None:
        return [r["dbg"] for r in res.results]
    out = np.zeros((2, S, D), dtype=np.float32)
    for c in range(8):
        b, r = c // 4, c % 4
        out[b, 1024 * r:1024 * (r + 1)] = res.results[c]["out"]
    return out
```

```python
import math
import numpy as np
import ml_dtypes
import concourse.bass as bass
import concourse.mybir as mybir
from concourse.bass_utils import run_bass_kernel_spmd

F32 = mybir.dt.float32
BF16 = mybir.dt.bfloat16
AF = mybir.ActivationFunctionType
ALU = mybir.AluOpType

D = 2048
S = 4096
DFF = 5632
EPS = 1e-6
NFF = DFF // 128
ENGS = ["pe", "act", "dve", "pool", "sp"]
_DEBUG = {"stop": None}
_RID = {}
_SHAPES = {}


class _Op:
    __slots__ = ("eng", "emit", "deps", "dma_deps", "is_dma", "semkey", "signaled", "sig_idx", "inc")


class Sched:
    def __init__(self):
        self.ops = {e: [] for e in ENGS}
        self.lastw = {}
        self.readers = {}
        self.dma_cnt = {}

    def add(self, eng, emit, reads=(), writes=(), dma=None, inc=16):
        op = _Op()
        op.inc = inc
        op.eng = eng
        op.emit = emit
        op.is_dma = dma is not None
        op.semkey = dma
        op.signaled = False
        op.sig_idx = 0
        deps = {}
        same_raw = set()
        for r in reads:
            w = self.lastw.get(r)
            if w is not None:
                deps[id(w)] = w
                if w.eng == eng and eng != "pe" and not w.is_dma:
                    same_raw.add(id(w))
        for wn in writes:
            w = self.lastw.get(wn)
            if w is not None:
                deps[id(w)] = w
            for rd in self.readers.get(wn, ()):
                deps[id(rd)] = rd
        for r in reads:
            self.readers.setdefault(r, []).append(op)
        for wn in writes:
            self.lastw[wn] = op
            self.readers[wn] = []
        op.deps = []
        op.dma_deps = {}
        for d in deps.values():
            if d is op:
                continue
            if d.is_dma:
                op.dma_deps[d.semkey] = self.dma_cnt[d.semkey]
            elif d.eng != eng or op.is_dma or id(d) in same_raw or eng != "pe":
                op.deps.append(d)
                d.signaled = True
        if op.is_dma:
            self.dma_cnt[dma] = self.dma_cnt.get(dma, 0) + inc
        self.ops[eng].append(op)
        return op

    def barrier(self):
        lasts = {}
        for e in ENGS:
            for op in reversed(self.ops[e]):
                if not op.is_dma and op.emit is not None:
                    lasts[e] = op
                    break
        for e in ENGS:
            op = _Op()
            op.inc = 0
            op.eng = e
            op.emit = None
            op.is_dma = False
            op.semkey = None
            op.signaled = False
            op.sig_idx = 0
            op.deps = []
            for e2, l in lasts.items():
                if e2 != e:
                    op.deps.append(l)
                    l.signaled = True
            op.dma_deps = dict(self.dma_cnt)
            self.ops[e].append(op)

    def emit_all(self, block, eng_sems, dma_sems):
        for e in ENGS:
            c = 0
            for op in self.ops[e]:
                if op.signaled and not op.is_dma:
                    c += 1
                    op.sig_idx = c

        def make(engname):
            def fn(e):
                waited = {}
                if engname == "sp":
                    _RID["v"] = e.snap(e.partition_id() % 4, min_val=0, max_val=3)
                for op in self.ops[engname]:
                    for d in op.deps:
                        key = ("e", d.eng)
                        if waited.get(key, 0) < d.sig_idx:
                            e.wait_ge(eng_sems[d.eng], d.sig_idx)
                            waited[key] = d.sig_idx
                    for k, v in op.dma_deps.items():
                        key = ("d", k)
                        if waited.get(key, 0) < v:
                            e.wait_ge(dma_sems[k], v)
                            waited[key] = v
                    if op.emit is None:
                        continue
                    ins = op.emit(e)
                    if op.is_dma:
                        ins.then_inc(dma_sems[op.semkey], op.inc)
                    elif op.signaled:
                        ins.then_inc(eng_sems[engname], 1)
            return fn

        block.tensor(make("pe"))
        block.scalar(make("act"))
        block.vector(make("dve"))
        block.gpsimd(make("pool"))
        block.sync(make("sp"))


class Arena:
    def __init__(self, nc, nbytes):
        self.h32 = nc.alloc_sbuf_tensor("arena", [128, nbytes // 4], F32)
        self.h16 = self.h32.bitcast(BF16)
        self.nbytes = nbytes
        self.off = 0

    def alloc(self, nbytes):
        nbytes = (nbytes + 63) // 64 * 64
        o = self.off
        self.off += nbytes
        assert self.off <= self.nbytes, ("SBUF arena overflow", self.off, self.nbytes)
        return o

    def f32(self, n):
        o = self.alloc(4 * n)
        return self.h32[:, o // 4:o // 4 + n]

    def bf16(self, n):
        o = self.alloc(2 * n)
        return self.h16[:, o // 2:o // 2 + n]


def _ap(t, extra_off, dims):
    return bass.AP(t.tensor, t.offset + extra_off, dims)


def build_program():
    nc = bass.Bass("TRN2", target_bir_lowering=False)
    s = Sched()
    dbg = _DEBUG["stop"]
    ORDER = ["p1", "p3", "p2a", "p2f", "p2", "p4b", "p4", "p5"]

    def upto(name):
        return dbg is None or ORDER.index(name) <= ORDER.index(dbg)

    def din(name, shape, dt=F32):
        need = {"wgate": "p4b", "wbh": "p4b", "wba": "p4b", "wout": "p4", "wfg": "p5", "wfu": "p5", "wfd": "p5",
                "dftF": "p2f", "dftI": "p2"}
        if name in need and not upto(need[name]):
            shape = [32, 128, 128] if len(shape) == 3 else [128, 128]
        _SHAPES[name] = tuple(shape)
        return nc.dram_tensor(name, list(shape), dt, kind="ExternalInput").ap()

    x_d = din("x", [S, D])
    xo_d = din("xo", [1024, D])
    wmix_d = din("wmix", [D, 1280])
    wgate_d = din("wgate", [D, 4096])
    bgate_d = din("bgate", [128, 32])
    g1_d = din("g1", [D])
    convw_d = din("convw", [128, 18])
    convb_d = din("convb", [128, 6])
    fw1_d = din("fw1", [33, 64])
    fw2_d = din("fw2", [64, 64])
    fw3_d = din("fw3", [64, 64])
    fw4_d = din("fw4", [64, 512])
    fvec_d = din("fvec", [64, 4])
    hyb_d = din("hybias", [128, 2])
    gqk_d = din("gqk", [128, 2])
    wbh_d = din("wbh", [1024, D])
    wba_d = din("wba", [1024, D])
    wout_d = din("wout", [D, D])
    g2_d = din("g2", [D])
    wfg_d = din("wfg", [D, DFF])
    wfu_d = din("wfu", [D, DFF])
    wfd_d = din("wfd", [DFF, D])
    identb_d = din("identb", [128, 128], BF16)
    identf_d = din("identf", [128, 128])
    onesdiv_d = din("onesdiv", [128, 128], BF16)
    perm_d = din("perm", [128, 128], BF16)
    ropec_d = din("ropec", [128, 64])
    ropes_d = din("ropes", [128, 64])
    zemb_d = din("zemb", [33, S])
    decay_d = din("decay", [S, 256])
    dftF_d = din("dftF", [32, 128, 2 * 32 * 128], BF16)
    dftI_d = din("dftI", [32, 128, 2 * 32 * 128], BF16)

    out_d = nc.dram_tensor("out", [1024, D], F32, kind="ExternalOutput").ap()
    yloc_d = nc.dram_tensor("yloc", [512, S], BF16, kind="Internal").ap()
    cloc_d = nc.dram_tensor("cloc", [128, S], BF16, kind="Internal").ap()
    cgat_d = nc.dram_tensor("cgat", [4 * 128, S], BF16, kind="Internal").ap()
    ystage_d = nc.dram_tensor("ystage", [4, 128, 4 * 1024], BF16, kind="Internal").ap()
    x1_d = nc.dram_tensor("x1s", [1024, D], F32, kind="Internal").ap()
    dbg_d = None
    if dbg is not None:
        dbg_d = nc.dram_tensor("dbg", [128, 8192], F32, kind="ExternalOutput").ap()

    A = Arena(nc, 207 * 1024)
    PS = [nc.alloc_psum_tensor("psb%d" % i, [128, 512], F32) for i in range(8)]
    PSB = [p.bitcast(BF16) for p in PS]

    identb = A.bf16(128)
    identf = A.f32(128)
    onesdiv = A.bf16(128)
    perm = A.bf16(128)
    ropec = A.f32(64)
    ropes = A.f32(64)
    gqk = A.f32(2)
    convw = A.f32(18)
    convb = A.f32(6)
    hyb = A.f32(2)
    bgate = A.f32(32)
    fvec = A.f32(4)
    stats = A.f32(64)
    negpi = A.f32(1)
    onescol = A.f32(1)
    epst = A.f32(1)
    small_loads = [(identb, identb_d), (identf, identf_d), (onesdiv, onesdiv_d), (perm, perm_d),
                   (ropec, ropec_d), (ropes, ropes_d), (gqk, gqk_d), (convw, convw_d), (convb, convb_d),
                   (hyb, hyb_d), (bgate, bgate_d)]
    for i, (dst, src) in enumerate(small_loads):
        s.add("sp", lambda e, dst=dst, src=src: e.dma_start(out=dst, in_=src), writes=["const%d" % i], dma="const")
    s.add("sp", lambda e: e.dma_start(out=fvec[0:64, :], in_=fvec_d), writes=["fvec"], dma="const")
    s.add("dve", lambda e: e.memset(negpi, -math.pi), writes=["negpi"])
    s.add("dve", lambda e: e.memset(onescol, 1.0), writes=["onescol"])
    s.add("dve", lambda e: e.memset(epst, EPS), writes=["epst"])
    base_mark = A.off

    def dump(ap_sb, ncols, name):
        tmp = A.f32(ncols)
        s.add("dve", lambda e: e.tensor_copy(tmp, ap_sb), reads=[name], writes=["dbgtmp"])
        s.add("sp", lambda e: e.dma_start(out=dbg_d[:, 0:ncols], in_=tmp), reads=["dbgtmp"], writes=["dbgout"], dma="dbg")

    def mm_group(bank_ap, lhs_fn, rhs_fn, nk, reads, bankname):
        def f(e):
            ins = None
            for k in range(nk):
                ins = e.matmul(bank_ap, lhs_fn(k), rhs_fn(k), start=(k == 0), stop=(k == nk - 1))
            return ins
        s.add("pe", f, reads=reads, writes=[bankname])

    cgv = cgat_d.rearrange("(q p) t -> p q t", q=4)

    def xchg_start(g):
        s.add("sp", lambda e: e.dma_start(out=cloc_d, in_=yloc_d[g * 128:(g + 1) * 128, :]),
              reads=["yloc"], writes=["cloc"], dma="cloc")
        s.add("pool", lambda e: e.collective_compute("AllGather", ALU.bypass, replica_groups=[[0, 1, 2, 3], [4, 5, 6, 7]],
                                                     ins=[cloc_d], outs=[cgat_d]),
              reads=["cloc"], writes=["cgat"], dma="cc", inc=1)

    def xchg_finish(g):
        s.add("sp", lambda e: e.dma_start(out=ystage_d[g].rearrange("p (q t) -> p q t", q=4),
                                          in_=cgv[:, :, bass.ds(_RID["v"] * 1024, 1024)]),
              reads=["cgat"], writes=["ystage%d" % g], dma="ystage")

    def transposes_to(src_bf, srcname, hdst3, col0, hname):
        for g4 in range(4):
            bank = g4 % 2
            pst = PSB[bank][:, 0:512].rearrange("p (a t) -> p a t", a=4)

            def tr(e, g4=g4, pst=pst):
                ins = None
                for a in range(4):
                    dk = g4 * 4 + a
                    ins = e.transpose(out=pst[:, a, :], in_=src_bf[:, dk * 128:(dk + 1) * 128], identity=identb)
                return ins
            s.add("pe", tr, reads=[srcname, "const0"], writes=["ps%d" % bank])
            dst = hdst3[:, g4 * 4:(g4 + 1) * 4, col0:col0 + 128]
            if g4 % 2 == 0:
                s.add("act", lambda e, dst=dst, pst=pst: e.activation(out=dst, in_=pst, func=AF.Copy),
                      reads=["ps%d" % bank], writes=[hname + "_a"])
            else:
                s.add("dve", lambda e, dst=dst, pst=pst: e.tensor_copy(dst, pst),
                      reads=["ps%d" % bank], writes=[hname + "_d"])

    def rms_scale(src_f32, srcname, dst_bf, dstname, gtile, gname, statcol, junkbuf):
        sc = stats[:, statcol:statcol + 1]
        sn = "stat%d" % statcol
        s.add("act", lambda e: e.activation(out=junkbuf, in_=src_f32, func=AF.Square, accum_out=sc),
              reads=[srcname], writes=["junk", sn])
        s.add("act", lambda e: e.activation(out=sc, in_=sc, func=AF.Sqrt, bias=epst, scale=1.0 / D), reads=[sn, "epst"], writes=[sn])
        s.add("dve", lambda e: e.reciprocal(sc, sc), reads=[sn], writes=[sn])
        s.add("dve", lambda e: e.scalar_tensor_tensor(out=dst_bf, in0=src_f32, scalar=sc, in1=gtile, op0=ALU.mult, op1=ALU.mult),
              reads=[srcname, sn, gname], writes=[dstname])

    RAW = A.bf16(6 * 4098)
    RAW3 = RAW.rearrange("p (c t) -> p c t", c=6)
    raw_end = A.off
    QT = A.bf16(2 * S)
    KT = A.bf16(S)
    V = A.bf16(32 * 128)
    QT3 = QT.rearrange("p (h t) -> p h t", h=2)
    V3 = V.rearrange("p (c d) -> p c d", c=32)
    p1_mark = A.off

    wm = A.bf16(16 * 1280)
    wm3 = wm.rearrange("p (k n) -> p k n", k=16)
    xt = [A.f32(D) for _ in range(2)]
    xs = [A.bf16(D) for _ in range(2)]
    junk = A.bf16(D)
    hT = [A.bf16(16 * 512) for _ in range(2)]
    hT3 = [h.rearrange("p (k t) -> p k t", k=16) for h in hT]
    gbc = A.f32(D)
    sqs = [A.bf16(512) for _ in range(1)]
    rstdqs = [A.f32(512)] * 3
    qns = [A.bf16(512) for _ in range(3)]
    rawq = [A.f32(512) for _ in range(3)]
    t1 = A.f32(512)
    t2 = A.f32(512)
    print("P1 arena end", A.off, "of", A.nbytes)

    wmix_v = wmix_d.rearrange("(k p) n -> p k n", p=128)
    for k0 in (0, 8):
        s.add("pool", lambda e, k0=k0: e.dma_start(out=wm3[:, k0:k0 + 8, :], in_=wmix_v[:, k0:k0 + 8, :]), writes=["wm"], dma="wm")
    s.add("sp", lambda e: e.dma_start(out=gbc, in_=g1_d.partition_broadcast(128)), writes=["gbc"], dma="gbc")
    s.add("dve", lambda e: e.memset(RAW3[:, :, 0:1], 0.0), writes=["rawpad0"])
    s.add("dve", lambda e: e.memset(RAW3[:, :, 4097:4098], 0.0), writes=["rawpad1"])
    x_v = x_d.rearrange("(n p) d -> n p d", p=128)
    pstep_c = ropec.ap[0][0]
    pstep_s = ropes.ap[0][0]

    evac_flip = 0
    for j in range(8):
        hs = j % 2
        hname = "hT%d" % hs
        for i in range(4):
            tile = 4 * j + i
            sl = i % 2
            s.add("sp", lambda e, sl=sl, tile=tile: e.dma_start(out=xt[sl], in_=x_v[tile]), writes=["xt%d" % sl], dma="xt%d" % sl)
            rms_scale(xt[sl], "xt%d" % sl, xs[sl], "xs%d" % sl, gbc, "gbc", tile, junk)
            transposes_to(xs[sl], "xs%d" % sl, hT3[hs], i * 128, hname)
        for u in range(3):
            bank = 2 + (u % 2)
            col = 768 + u * 128
            mm_group(PS[bank][:, :], lambda k, col=col: wm3[:, k, col:col + 128],
                     lambda k, hs=hs: hT3[hs][:, k, :], 16, [hname + "_a", hname + "_d", "wm"], "ps%d" % bank)
            s.add("act", lambda e, bank=bank, u=u: e.activation(out=rawq[u], in_=PS[bank][:, :], func=AF.Copy),
                  reads=["ps%d" % bank], writes=["rawq%d" % u])

        def hy_proj(cc):
            bank = 2 + (cc % 2)
            mm_group(PS[bank][:, :], lambda k, cc=cc: wm3[:, k, cc * 128:(cc + 1) * 128],
                     lambda k, hs=hs: hT3[hs][:, k, :], 16, [hname + "_a", hname + "_d", "wm"], "ps%d" % bank)
            dst = RAW3[:, cc, 1 + 512 * j:1 + 512 * (j + 1)]
            if cc % 2 == 0:
                s.add("act", lambda e, dst=dst, bank=bank: e.activation(out=dst, in_=PS[bank][:, :], func=AF.Copy),
                      reads=["ps%d" % bank], writes=["raw_a"])
            else:
                s.add("dve", lambda e, dst=dst, bank=bank: e.tensor_copy(dst, PS[bank][:, :]),
                      reads=["ps%d" % bank], writes=["raw_d"])

        def qk_square(u):
            s.add("act", lambda e, u=u: e.activation(out=sqs[0], in_=rawq[u], func=AF.Square),
                  reads=["rawq%d" % u], writes=["sq"])

        def qk_norm(u):
            s.add("pe", lambda e, u=u: e.matmul(PS[4][:, :], onesdiv, sqs[0], start=True, stop=True),
                  reads=["sq", "const2"], writes=["ps4"])
            s.add("act", lambda e, u=u: e.activation(out=rstdqs[u], in_=PS[4][:, :], func=AF.Sqrt, bias=epst, scale=1.0),
                  reads=["ps4", "epst"], writes=["rstdq"])
            s.add("dve", lambda e, u=u: e.reciprocal(rstdqs[u], rstdqs[u]), reads=["rstdq"], writes=["rstdq"])
            gcol = gqk[:, 0:1] if u < 2 else gqk[:, 1:2]
            s.add("dve", lambda e, u=u, gcol=gcol: e.scalar_tensor_tensor(
                out=qns[u], in0=rawq[u], scalar=gcol, in1=rstdqs[u], op0=ALU.mult, op1=ALU.mult),
                reads=["rawq%d" % u, "rstdq", "const6"], writes=["qn%d" % u])

        def qk_rope(u):
            qn_ = qns[u]
            s.add("pe", lambda e, qn_=qn_: e.matmul(PS[5][:, :], perm, qn_, start=True, stop=True),
                  reads=["qn%d" % u, "const3"], writes=["ps5"])
            for half in range(2):
                p0 = half * 64
                if half == 0:
                    cap = _ap(ropec, p0 * pstep_c + 8 * j, [[pstep_c, 64], [1, 8], [0, 64]])
                    sap = _ap(ropes, p0 * pstep_s + 8 * j, [[pstep_s, 64], [1, 8], [0, 64]])
                else:
                    cap = _ap(ropec, p0 * pstep_c, [[pstep_c, 64], [0, 8], [1, 64]])
                    sap = _ap(ropes, p0 * pstep_s, [[pstep_s, 64], [0, 8], [1, 64]])
                qv = qn_[p0:p0 + 64, :].rearrange("p (a b) -> p a b", a=8)
                t1v = t1[p0:p0 + 64, :].rearrange("p (a b) -> p a b", a=8)
                t2v = t2[p0:p0 + 64, :].rearrange("p (a b) -> p a b", a=8)
                pv = PS[5][p0:p0 + 64, :].rearrange("p (a b) -> p a b", a=8)
                s.add("pool", lambda e, t1v=t1v, qv=qv, cap=cap: e.tensor_tensor(out=t1v, in0=qv, in1=cap, op=ALU.mult),
                      reads=["qn%d" % u, "const4"], writes=["t1_%d" % half])
                s.add("dve", lambda e, t2v=t2v, pv=pv, sap=sap: e.tensor_tensor(out=t2v, in0=pv, in1=sap, op=ALU.mult),
                      reads=["ps5", "const5"], writes=["t2_%d" % half])
            dstq = QT3[:, u, 512 * j:512 * (j + 1)] if u < 2 else KT[:, 512 * j:512 * (j + 1)]
            s.add("pool", lambda e, dstq=dstq: e.tensor_tensor(out=dstq, in0=t1, in1=t2, op=ALU.add),
                  reads=["t1_0", "t1_1", "t2_0", "t2_1"], writes=["qk"])
        for cc in range(3):
            qk_square(cc)
            hy_proj(cc)
            qk_norm(cc)
        for cc in range(3, 6):
            hy_proj(cc)
            qk_rope(cc - 3)

        def vmm(e, hs=hs):
            ins = None
            for i in range(4):
                for k in range(16):
                    ins = e.matmul(PS[6][:, i * 128:(i + 1) * 128], hT3[hs][:, k, i * 128:(i + 1) * 128],
                                   wm3[:, k, 1152:1280], start=(k == 0), stop=(k == 15))
            return ins
        s.add("pe", vmm, reads=[hname + "_a", hname + "_d", "wm"], writes=["ps6"])
        s.add("act", lambda e, j=j: e.activation(out=V3[:, 4 * j:4 * j + 4, :],
                                                 in_=PS[6][:, :].rearrange("p (a d) -> p a d", a=4), func=AF.Copy),
              reads=["ps6"], writes=["v"])
    s.barrier()
    A.off = p1_mark
    if dbg == "p1":
        dump(QT[:, 0:8192], 8192, "qk")
        s.barrier()

    if upto("p3"):
        PT = [A.bf16(512) for _ in range(3)]
        rec = A.f32(512)
        yab = [A.bf16(512) for _ in range(2)]
        ones128 = A.bf16(128)
        s.add("dve", lambda e: e.memset(ones128, 1.0), writes=["ones128"])
        scale = 128.0 ** -0.5
        unit = 0
        for h in range(2):
            for qc in range(8):
                ob = 4 + 2 * (unit % 2)
                def st_mm(sc_):
                    sb = sc_ % 2
                    s.add("pe", lambda e, sb=sb, sc_=sc_, h=h, qc=qc: e.matmul(
                        PS[sb][:, :], KT[:, 128 * sc_:128 * (sc_ + 1)], QT3[:, h, 512 * qc:512 * (qc + 1)], start=True, stop=True),
                        reads=["qk"], writes=["ps%d" % sb])
                    pt = sc_ % 3
                    s.add("act", lambda e, sb=sb, pt=pt: e.activation(out=PT[pt], in_=PS[sb][:, :], func=AF.Exp, scale=scale),
                          reads=["ps%d" % sb], writes=["PT%d" % pt])

                def pv_mm(sc_):
                    pt = sc_ % 3
                    s.add("pe", lambda e, pt=pt, sc_=sc_, ob=ob: e.matmul(PS[ob][:, :], V3[:, sc_, :], PT[pt], start=(sc_ == 0), stop=(sc_ == 31)),
                          reads=["PT%d" % pt, "v"], writes=["ps%d" % ob])
                    s.add("pe", lambda e, pt=pt, sc_=sc_, ob=ob: e.matmul(PS[ob + 1][:, :], ones128, PT[pt], start=(sc_ == 0), stop=(sc_ == 31)),
                          reads=["PT%d" % pt, "ones128"], writes=["ps%d" % (ob + 1)])
                st_mm(0)
                for sc_ in range(32):
                    if sc_ + 1 < 32:
                        st_mm(sc_ + 1)
                    pv_mm(sc_)
                ys = unit % 2
                s.add("dve", lambda e, ob=ob: e.reciprocal(rec, PS[ob + 1][:, :]), reads=["ps%d" % (ob + 1)], writes=["rec"])
                s.add("dve", lambda e, ob=ob, ys=ys: e.tensor_tensor(out=yab[ys], in0=PS[ob][:, :], in1=rec, op=ALU.mult),
                      reads=["ps%d" % ob, "rec"], writes=["yab%d" % ys])
                s.add("sp", lambda e, h=h, qc=qc, ys=ys: e.dma_start(out=yloc_d[256 + h * 128:256 + (h + 1) * 128, 512 * qc:512 * (qc + 1)], in_=yab[ys]),
                      reads=["yab%d" % ys], writes=["yloc"], dma="yab%d" % ys)
                unit += 1
        s.barrier()
        if upto("p4b"):
            xchg_start(2)
        if dbg == "p3":
            s.add("sp", lambda e: e.dma_start(out=xt[0].bitcast(BF16), in_=yloc_d[256:384, :]), reads=["yloc"], writes=["dbgld"], dma="dbgld")
            dump(xt[0].bitcast(BF16)[:, 0:4096], 4096, "dbgld")
            s.barrier()
    A.off = raw_end

    x0T = A.bf16(2 * S)
    zT = A.bf16(2 * S)
    x0T3 = x0T.rearrange("p (c t) -> p c t", c=2)
    zT3 = zT.rearrange("p (c t) -> p c t", c=2)
    p2_mark = A.off
    if upto("p2a"):
        tbuf = [A.f32(S) for _ in range(4)]

        def conv_chain(cc, tb, tbn, eng, dst, dstn):
            w0 = convw[:, cc * 3 + 0:cc * 3 + 1]
            w1 = convw[:, cc * 3 + 1:cc * 3 + 2]
            w2 = convw[:, cc * 3 + 2:cc * 3 + 3]
            bb = convb[:, cc:cc + 1]
            s.add(eng, lambda e: e.tensor_scalar(tb, RAW3[:, cc, 1:4097], w1, bb, op0=ALU.mult, op1=ALU.add),
                  reads=["raw_a", "raw_d", "const7", "const8"], writes=[tbn])
            s.add(eng, lambda e: e.scalar_tensor_tensor(out=tb, in0=RAW3[:, cc, 0:4096], scalar=w0, in1=tb, op0=ALU.mult, op1=ALU.add),
                  reads=["raw_a", "raw_d", "rawpad0", tbn, "const7"], writes=[tbn])
            s.add(eng, lambda e: e.scalar_tensor_tensor(out=dst, in0=RAW3[:, cc, 2:4098], scalar=w2, in1=tb, op0=ALU.mult, op1=ALU.add),
                  reads=["raw_a", "raw_d", "rawpad1", tbn, "const7"], writes=[dstn])
        conv_chain(2, tbuf[0], "tb0", "dve", tbuf[0], "tb0")
        conv_chain(3, tbuf[1], "tb1", "dve", tbuf[1], "tb1")
        conv_chain(4, tbuf[2], "tb2", "dve", tbuf[2], "tb2")
        conv_chain(5, tbuf[3], "tb3", "dve", tbuf[3], "tb3")
        for c2 in range(2):
            s.add("dve", lambda e, c2=c2: e.tensor_tensor(out=zT3[:, c2, :], in0=tbuf[2 + c2], in1=tbuf[c2], op=ALU.mult),
                  reads=["tb%d" % c2, "tb%d" % (2 + c2)], writes=["zT"])
        conv_chain(0, tbuf[0], "tb0", "dve", x0T3[:, 0, :], "x0T")
        conv_chain(1, tbuf[1], "tb1", "dve", x0T3[:, 1, :], "x0T")
        s.barrier()
        if dbg == "p2a":
            A.off = p2_mark
            dump(zT[:, 0:8192], 8192, "zT")
            s.barrier()
    A.off = p2_mark

    if upto("p2f"):
        ZH = RAW[:, 0:32 * 768]
        ZH3 = ZH.rearrange("p (c n) -> p c n", c=32)
        R = A.bf16(32 * 512)
        R3 = R.rearrange("p (c n) -> p c n", c=32)
        dbuf = [A.bf16(2 * 32 * 128) for _ in range(2)]
        dbuf4 = [b.rearrange("p (a c n) -> p a c n", a=2, c=32) for b in dbuf]
        fw1 = A.f32(64)
        fw2 = A.f32(64)
        fw3 = A.f32(64)
        fw4 = A.f32(512)
        zemb = [A.f32(512) for _ in range(2)]
        hA = A.f32(512)
        hB = A.f32(512)
        uu = A.f32(512)
        val = A.f32(512)
        absv = A.f32(256)
        dec = [A.f32(256) for _ in range(2)]
        bsc = A.f32(4)
        scn = A.f32(2)
        sP = A.f32(256)
        sQ = A.f32(256)
        tt = [A.f32(256) for _ in range(4)]
        ysb = A.f32(256)
        ysb3 = ysb.rearrange("p (a c) -> p a c", a=2)
        ub = A.f32(512)
        yhs = [A.bf16(512) for _ in range(2)]
        dt_ = A.f32(4112) if dbg == "p2f" else None

        if upto("p4b"):
            xchg_finish(2)
            xchg_start(3)
        s.add("sp", lambda e: e.dma_start(out=fw1[0:33, :], in_=fw1_d), writes=["fw1"], dma="fw")
        s.add("sp", lambda e: e.dma_start(out=fw2[0:64, :], in_=fw2_d), writes=["fw2"], dma="fw")
        s.add("sp", lambda e: e.dma_start(out=fw3[0:64, :], in_=fw3_d), writes=["fw3"], dma="fw")
        s.add("sp", lambda e: e.dma_start(out=fw4[0:64, :], in_=fw4_d), writes=["fw4"], dma="fw")
        for cc in range(2):
            for g in range(8):
                bank = g % 2
                pst = PSB[bank][:, 0:512].rearrange("p (a t) -> p a t", a=4)

                def trz(e, cc=cc, g=g, pst=pst):
                    ins = None
                    for a in range(4):
                        ch = g * 4 + a
                        ins = e.transpose(out=pst[:, a, :], in_=zT3[:, cc, ch * 128:(ch + 1) * 128], identity=identb)
                    return ins
                s.add("pe", trz, reads=["zT", "const0"], writes=["ps%d" % bank])
                dst = ZH3[:, g * 4:(g + 1) * 4, 256 + cc * 128:256 + (cc + 1) * 128]
                s.add("dve", lambda e, dst=dst, pst=pst: e.tensor_copy(dst, pst), reads=["ps%d" % bank], writes=["ZHz"])
        for l in range(3):
            s.add("dve", lambda e, l=l: e.tensor_tensor(out=bsc[0:64, l:l + 1], in0=fvec[0:64, 0:1], in1=fvec[0:64, l + 1:l + 2], op=ALU.mult),
                  reads=["fvec"], writes=["bsc"])
        s.add("dve", lambda e: e.tensor_scalar(bsc[0:64, 0:3], bsc[0:64, 0:3], 1.0 / (2.0 * math.pi), 16.5, op0=ALU.mult, op1=ALU.add),
              reads=["bsc"], writes=["bsc"])
        s.add("dve", lambda e: e.tensor_scalar(bsc[0:64, 3:4], fvec[0:64, 0:1], 1.0 / (2.0 * math.pi), None, op0=ALU.mult),
              reads=["fvec", "bsc"], writes=["bsc"])
        frp = bsc[0:64, 3:4]
        ki = A.h32.bitcast(mybir.dt.int32)[:, A.alloc(4 * 512) // 4:][:, 0:512]
        kf = A.f32(512)

        def sin_layer(ps_ap, l, dst, rn, wn):
            s.add("dve", lambda e: e.tensor_scalar(uu[0:64, :], ps_ap, frp, bsc[0:64, l:l + 1], op0=ALU.mult, op1=ALU.add),
                  reads=rn + ["bsc"], writes=["uu"])
            s.add("dve", lambda e: e.tensor_copy(ki[0:64, :], uu[0:64, :]), reads=["uu"], writes=["ki"])
            s.add("dve", lambda e: e.tensor_copy(kf[0:64, :], ki[0:64, :]), reads=["ki"], writes=["kf"])
            s.add("dve", lambda e: e.tensor_tensor(out=uu[0:64, :], in0=uu[0:64, :], in1=kf[0:64, :], op=ALU.subtract),
                  reads=["uu", "kf"], writes=["uu"])
            s.add("dve", lambda e: e.scalar_tensor_tensor(out=uu[0:64, :], in0=uu[0:64, :], scalar=0.0, in1=uu[0:64, :], op0=ALU.is_lt, op1=ALU.add),
                  reads=["uu"], writes=["uu"])
            s.add("act", lambda e: e.activation(out=dst, in_=uu[0:64, :], func=AF.Sin, bias=negpi[0:64, :], scale=2.0 * math.pi),
                  reads=["uu", "negpi"], writes=wn)

        decay_v = decay_d.rearrange("(c p) n -> c p n", p=128)
        pstep_d = dec[0].ap[0][0]
        for j in range(8):
            zs = j % 2
            s.add("sp", lambda e, j=j, zs=zs: e.dma_start(out=zemb[zs][0:33, :], in_=zemb_d[:, 512 * j:512 * (j + 1)]),
                  writes=["zemb%d" % zs], dma="zemb%d" % zs)
            s.add("pe", lambda e, zs=zs: e.matmul(PS[2][0:64, :], fw1[0:33, :], zemb[zs][0:33, :], start=True, stop=True),
                  reads=["fw1", "zemb%d" % zs], writes=["ps2"])
            sin_layer(PS[2][0:64, :], 0, hA[0:64, :], ["ps2"], ["hA"])
            if dbg == "p2f" and j == 0:
                s.add("dve", lambda e: e.tensor_copy(dt_[0:64, 2048:2560], hA[0:64, :]), reads=["hA"], writes=["dbgtmp"])
                s.add("dve", lambda e: e.tensor_copy(dt_[64:128, 2048:2560], PS[2][0:64, :]), reads=["hA", "ps2"], writes=["dbgtmp"])
            s.add("pe", lambda e: e.matmul(PS[3][0:64, :], fw2[0:64, :], hA[0:64, :], start=True, stop=True),
                  reads=["fw2", "hA"], writes=["ps3"])
            sin_layer(PS[3][0:64, :], 1, hB[0:64, :], ["ps3"], ["hB"])
            if dbg == "p2f" and j == 0:
                s.add("dve", lambda e: e.tensor_copy(dt_[0:64, 2560:3072], hB[0:64, :]), reads=["hB"], writes=["dbgtmp"])
            s.add("pe", lambda e: e.matmul(PS[2][0:64, :], fw3[0:64, :], hB[0:64, :], start=True, stop=True),
                  reads=["fw3", "hB"], writes=["ps2"])
            if dbg == "p2f" and j == 0:
                s.add("dve", lambda e: e.tensor_copy(dt_[64:128, 2560:3072], PS[2][0:64, :]), reads=["ps2"], writes=["dbgtmp"])
            sin_layer(PS[2][0:64, :], 2, hA[0:64, :], ["ps2"], ["hA"])
            if dbg == "p2f" and j == 0:
                s.add("dve", lambda e: e.tensor_copy(dt_[64:128, 3072:3584], uu[0:64, :]), reads=["uu"], writes=["dbgtmp"])
                s.add("dve", lambda e: e.tensor_copy(dt_[0:64, 3072:3584], hA[0:64, :]), reads=["hA"], writes=["dbgtmp"])
            for i in range(4):
                ch = 4 * j + i
                ds_ = ch % 2
                s.add("sp", lambda e, ch=ch, ds_=ds_: e.dma_start(out=dec[ds_], in_=decay_v[ch]),
                      writes=["dec%d" % ds_], dma="dec%d" % ds_)
                s.add("pe", lambda e, i=i: e.matmul(PS[3][:, :], hA[0:64, i * 128:(i + 1) * 128], fw4[0:64, :], start=True, stop=True),
                      reads=["hA", "fw4"], writes=["ps3"])
                dbc = _ap(dec[ds_], 0, [[pstep_d, 128], [0, 2], [1, 256]])
                s.add("dve", lambda e, dbc=dbc: e.tensor_tensor(out=val.rearrange("p (a c) -> p a c", a=2),
                                                                 in0=PS[3][:, :].rearrange("p (a c) -> p a c", a=2), in1=dbc, op=ALU.mult),
                      reads=["ps3", "dec%d" % ds_], writes=["val"])
                if ch == 0:
                    s.add("dve", lambda e: e.memset(val[0:1, 256:512], 0.0), reads=["val"], writes=["val"])
                s.add("dve", lambda e, ch=ch: e.tensor_tensor(out=ZH3[:, ch, 0:256], in0=val[:, 0:256], in1=val[:, 256:512], op=ALU.add),
                      reads=["val"], writes=["ZHp"])
                s.add("dve", lambda e, ch=ch: e.tensor_tensor(out=ZH3[:, ch, 512:768], in0=val[:, 0:256], in1=val[:, 256:512], op=ALU.subtract),
                      reads=["val"], writes=["ZHm"])
                s.add("dve", lambda e: e.scalar_tensor_tensor(out=val, in0=val, scalar=-1.0, in1=val, op0=ALU.mult, op1=ALU.max),
                      reads=["val", "ZHp", "ZHm"], writes=["val"])
                s.add("dve", lambda e: e.tensor_tensor(out=absv, in0=val[:, 0:256], in1=val[:, 256:512], op=ALU.add),
                      reads=["val"], writes=["absv"])
                for c2 in range(2):
                    s.add("pe", lambda e, c2=c2, ch=ch: e.matmul(PS[4 + c2][:, 0:1], absv[:, c2 * 128:(c2 + 1) * 128], onescol,
                                                                  start=(ch == 0), stop=(ch == 31)),
                          reads=["absv", "onescol"], writes=["ps%d" % (4 + c2)])
        for c2 in range(2):
            s.add("dve", lambda e, c2=c2: e.reciprocal(scn[:, c2:c2 + 1], PS[4 + c2][:, 0:1]), reads=["ps%d" % (4 + c2)], writes=["scn"])
        s.add("dve", lambda e: e.tensor_scalar(scn, scn, 2.0 / 8192.0, None, op0=ALU.mult), reads=["scn"], writes=["scn"])

        ZHALL = ["ZHz", "ZHp", "ZHm"]
        for i in range(32):
            sl = i % 2
            dn = "dbuf%d" % sl
            s.add("sp", lambda e, i=i, sl=sl: e.dma_start(out=dbuf[sl], in_=dftF_d[i]), writes=[dn], dma=dn)
            mm_group(PS[0][:, :], lambda k, sl=sl: dbuf4[sl][:, 0, k, :], lambda k: ZH3[:, k, 0:512], 32, [dn] + ZHALL, "ps0")
            mm_group(PS[1][:, :], lambda k, sl=sl: dbuf4[sl][:, 1, k, :], lambda k: ZH3[:, k, 256:768], 32, [dn] + ZHALL, "ps1")
            s.add("act", lambda e: e.activation(out=sP, in_=PS[0][:, 0:256], func=AF.Copy), reads=["ps0"], writes=["sP"])
            s.add("act", lambda e: e.activation(out=sQ, in_=PS[1][:, 256:512], func=AF.Copy), reads=["ps1"], writes=["sQ"])
            s.add("dve", lambda e: e.tensor_tensor(out=tt[0], in0=PS[0][:, 256:512], in1=sP, op=ALU.mult), reads=["ps0", "sP"], writes=["tt0"])
            s.add("dve", lambda e: e.tensor_tensor(out=tt[1], in0=PS[1][:, 0:256], in1=sQ, op=ALU.mult), reads=["ps1", "sQ"], writes=["tt1"])
            s.add("dve", lambda e: e.tensor_tensor(out=tt[2], in0=PS[0][:, 256:512], in1=sQ, op=ALU.mult), reads=["ps0", "sQ"], writes=["tt2"])
            s.add("dve", lambda e: e.tensor_tensor(out=tt[3], in0=PS[1][:, 0:256], in1=sP, op=ALU.mult), reads=["ps1", "sP"], writes=["tt3"])
            s.add("pool", lambda e, i=i: e.tensor_tensor(out=R3[:, i, 0:256], in0=tt[0], in1=tt[1], op=ALU.subtract), reads=["tt0", "tt1"], writes=["R"])
            s.add("pool", lambda e, i=i: e.tensor_tensor(out=R3[:, i, 256:512], in0=tt[2], in1=tt[3], op=ALU.add), reads=["tt2", "tt3"], writes=["R"])
        if dbg == "p2f":
            s.barrier()
            s.add("dve", lambda e: e.tensor_copy(dt_[:, 0:1024].rearrange("p (c n) -> p c n", c=4), ZH3[:, 0:4, 0:256]), reads=["ZHp"], writes=["dbgtmp"])
            s.add("dve", lambda e: e.tensor_copy(dt_[:, 1024:2048].rearrange("p (c n) -> p c n", c=4), ZH3[:, 0:4, 512:768]), reads=["ZHm"], writes=["dbgtmp"])
            s.add("dve", lambda e: e.tensor_copy(dt_[:, 3584:4096], R3[:, 5, :]), reads=["R"], writes=["dbgtmp"])
            s.add("dve", lambda e: e.tensor_copy(dt_[:, 3072:3584], R3[:, 0, :]), reads=["R"], writes=["dbgtmp"])
            s.add("dve", lambda e: e.tensor_copy(dt_[:, 4096:4098], scn), reads=["scn"], writes=["dbgtmp"])
            s.add("sp", lambda e: e.dma_start(out=dbg_d[:, 0:4112], in_=dt_), reads=["dbgtmp"], writes=["dbgout"], dma="dbg")
            s.barrier()
        if upto("p4b"):
            xchg_finish(3)
        ysbs = [ysb, A.f32(256)]
        ysb3s = [y.rearrange("p (a c) -> p a c", a=2) for y in ysbs]

        def inv_issue(tch):
            sl = tch % 2
            dn = "dbuf%d" % sl
            ib = tch % 2
            s.add("sp", lambda e, tch=tch, sl=sl: e.dma_start(out=dbuf[sl], in_=dftI_d[tch]), writes=[dn], dma=dn)

            def inv(e, sl=sl, ib=ib):
                ins = None
                for k in range(32):
                    ins = e.matmul(PS[ib][:, 0:256], dbuf4[sl][:, 0, k, :], R3[:, k, 0:256], start=(k == 0), stop=False)
                    ins = e.matmul(PS[ib][:, 0:256], dbuf4[sl][:, 1, k, :], R3[:, k, 256:512], start=False, stop=(k == 31))
                return ins
            s.add("pe", inv, reads=[dn, "R"], writes=["ps%d" % ib])
            s.add("act", lambda e, ib=ib: e.activation(out=ysbs[ib], in_=PS[ib][:, 0:256], func=AF.Copy), reads=["ps%d" % ib], writes=["ysb%d" % ib])

        def inv_transposes(tch):
            ib = tch % 2
            a = tch % 4
            tb0 = 2 + 2 * ((tch // 4) % 2)
            for c2 in range(2):
                s.add("pe", lambda e, c2=c2, a=a, ib=ib, tb0=tb0: e.transpose(out=PS[tb0 + c2][:, a * 128:(a + 1) * 128], in_=ysb3s[ib][:, c2, :], identity=identf),
                      reads=["ysb%d" % ib, "const1"], writes=["ps%d" % (tb0 + c2)])

        ubs = [ub, A.f32(512)]

        def epilogue(tg):
            tb0 = 2 + 2 * (tg % 2)
            for c2 in range(2):
                tsl = slice(512 * tg, 512 * (tg + 1))
                yb = (tg * 2 + c2) % 2
                ubx = ubs[c2]
                un = "ub%d" % c2
                s.add("pool", lambda e, c2=c2, tsl=tsl, ubx=ubx: e.tensor_scalar(ubx, zT3[:, c2, tsl], hyb[:, c2:c2 + 1], None, op0=ALU.mult),
                      reads=["zT", "const9"], writes=[un])
                s.add("dve", lambda e, c2=c2, ubx=ubx, tb0=tb0: e.scalar_tensor_tensor(out=ubx, in0=PS[tb0 + c2][:, :], scalar=scn[:, c2:c2 + 1], in1=ubx,
                                                                                  op0=ALU.mult, op1=ALU.add),
                      reads=["ps%d" % (tb0 + c2), "scn", un], writes=[un])
                s.add("dve", lambda e, c2=c2, tsl=tsl, yb=yb, ubx=ubx: e.tensor_tensor(out=yhs[yb], in0=ubx, in1=x0T3[:, c2, tsl], op=ALU.mult),
                      reads=[un, "x0T"], writes=["yhs%d" % yb])
                s.add("sp", lambda e, c2=c2, tsl=tsl, yb=yb: e.dma_start(out=yloc_d[c2 * 128:(c2 + 1) * 128, tsl], in_=yhs[yb]),
                      reads=["yhs%d" % yb], writes=["yloc"], dma="yhs%d" % yb)
        if upto("p2"):
            inv_issue(0)
            for tch in range(32):
                if tch + 1 < 32:
                    inv_issue(tch + 1)
                inv_transposes(tch)
                if tch % 4 == 3:
                    epilogue(tch // 4)
        s.barrier()
        if dbg == "p2":
            s.add("sp", lambda e: e.dma_start(out=dbuf[0][:, 0:4096], in_=yloc_d[0:128, :]), reads=["yloc"], writes=["dbgld2"], dma="dbgld")
            tmpf = dbuf[1].bitcast(F32)
            s.add("dve", lambda e: e.tensor_copy(tmpf, dbuf[0][:, 0:4096]), reads=["dbgld2"], writes=["dbgtmp"])
            s.add("sp", lambda e: e.dma_start(out=dbg_d[:, 0:4096], in_=tmpf), reads=["dbgtmp"], writes=["dbgout"], dma="dbg")
            s.barrier()
    A.off = base_mark

    if upto("p4b"):
        xchg_start(0)
        G = A.bf16(32 * 1024)
        G3 = G.rearrange("p (k t) -> p k t", k=32)
        mT3 = G3[:, 0:16, :]
        mT = G[:, 0:16 * 1024]
        mt_end = A.off - 16 * 1024 * 2
        hTo = A.bf16(16 * 1024)
        hTo3 = hTo.rearrange("p (k t) -> p k t", k=16)
        YT = A.bf16(16 * 1024)
        YT3 = YT.rearrange("p (k t) -> p k t", k=16)
        p4_mark = A.off
        h2T_off = 175 * 1024
        h2T = A.h16[:, h2T_off // 2:h2T_off // 2 + 16 * 1024]
        h2T3 = h2T.rearrange("p (k t) -> p k t", k=16)
        xt2 = [A.f32(D) for _ in range(2)]
        xs2 = [A.bf16(D) for _ in range(2)]
        junk2 = A.bf16(D)
        gbc2 = A.f32(D)
        WBLK = 256
        wg = [A.bf16(2 * 16 * WBLK) for _ in range(2)]
        wg4 = [w.rearrange("p (a k n) -> p a k n", a=2, k=16) for w in wg]
        s.add("sp", lambda e: e.dma_start(out=gbc2, in_=g1_d.partition_broadcast(128)), writes=["gbc2"], dma="gbc")
        xo_v = xo_d.rearrange("(n p) d -> n p d", p=128)
        for t in range(8):
            sl = t % 2
            s.add("sp", lambda e, sl=sl, t=t: e.dma_start(out=xt2[sl], in_=xo_v[t]), writes=["oxt%d" % sl], dma="oxt%d" % sl)
            rms_scale(xt2[sl], "oxt%d" % sl, xs2[sl], "oxs%d" % sl, gbc2, "gbc2", 32 + t, junk2)
            transposes_to(xs2[sl], "oxs%d" % sl, hTo3, t * 128, "ohT")
        wgate_v = wgate_d.rearrange("(k p) n -> p k n", p=128)
        wbh_v = wbh_d.rearrange("(k p) n -> p k n", p=128)
        wba_v = wba_d.rearrange("(k p) n -> p k n", p=128)
        NJB = D // WBLK

        def load_gblock(jb):
            sl = jb % 2
            wn = "wg%d" % sl
            c0 = jb * WBLK
            s.add("pool", lambda e, sl=sl, c0=c0: e.dma_start(out=wg4[sl][:, 0, :, :], in_=wgate_v[:, :, c0:c0 + WBLK]), writes=[wn], dma=wn)
            s.add("pool", lambda e, sl=sl, c0=c0: e.dma_start(out=wg4[sl][:, 1, :, :], in_=wgate_v[:, :, D + c0:D + c0 + WBLK]), writes=[wn], dma=wn)
        load_gblock(0)
        load_gblock(1)
        xchg_finish(0)
        xchg_start(1)
        cntA = 0
        for jb in range(NJB):
            sl = jb % 2
            wn = "wg%d" % sl
            for sub in range(WBLK // 128):
                jc = jb * (WBLK // 128) + sub
                so = sub * 128
                for tc_ in range(2):
                    tsl = slice(512 * tc_, 512 * (tc_ + 1))
                    par = cntA % 3
                    cntA += 1
                    b0, b1 = 2 + 2 * par, 3 + 2 * par
                    mm_group(PS[b0][:, :], lambda k, sl=sl, so=so: wg4[sl][:, 0, k, so:so + 128], lambda k, tsl=tsl: hTo3[:, k, tsl], 16, [wn, "ohT_a", "ohT_d"], "ps%d" % b0)
                    mm_group(PS[b1][:, :], lambda k, sl=sl, so=so: wg4[sl][:, 1, k, so:so + 128], lambda k, tsl=tsl: hTo3[:, k, tsl], 16, [wn, "ohT_a", "ohT_d"], "ps%d" % b1)
                    s.add("act", lambda e, jc=jc, b0=b0, tsl=tsl: e.activation(out=G3[:, jc, tsl], in_=PS[b0][:, :], func=AF.Sigmoid, bias=bgate[:, jc:jc + 1], scale=1.0),
                          reads=["ps%d" % b0, "const10"], writes=["gh%d" % jc])
                    s.add("act", lambda e, jc=jc, b1=b1, tsl=tsl: e.activation(out=G3[:, 16 + jc, tsl], in_=PS[b1][:, :], func=AF.Sigmoid, bias=bgate[:, 16 + jc:17 + jc], scale=1.0),
                          reads=["ps%d" % b1, "const10"], writes=["ga%d" % jc])
            if jb + 2 < NJB:
                load_gblock(jb + 2)
        xchg_finish(1)
        for g in range(4):
            kind, c2 = g // 2, g % 2
            base = kind * 8 + c2
            dst = _ap(YT, base * 1024, [[YT.ap[0][0], 128], [2 * 1024, 4], [1, 1024]])
            s.add("sp", lambda e, g=g, dst=dst: e.dma_start(out=dst, in_=ystage_d[g].rearrange("p (q t) -> p q t", q=4)),
                  reads=["ystage%d" % g], writes=["YT"], dma="YT")
        s.barrier()
        A.off = p4_mark
        wb = [A.bf16(2 * 8 * WBLK) for _ in range(2)]
        wb4 = [w.rearrange("p (a k n) -> p a k n", a=2, k=8) for w in wb]
        mtmp = [[A.f32(512) for _ in range(2)] for _ in range(2)]

        def load_bblock(jb):
            sl = jb % 2
            wn = "wb%d" % sl
            c0 = jb * WBLK
            s.add("pool", lambda e, sl=sl, c0=c0: e.dma_start(out=wb4[sl][:, 0, :, :], in_=wbh_v[:, :, c0:c0 + WBLK]), writes=[wn], dma=wn)
            s.add("pool", lambda e, sl=sl, c0=c0: e.dma_start(out=wb4[sl][:, 1, :, :], in_=wba_v[:, :, c0:c0 + WBLK]), writes=[wn], dma=wn)
        load_bblock(0)
        cntB = 0
        for jb in range(NJB):
            sl = jb % 2
            wn = "wb%d" % sl
            if jb + 1 < NJB:
                load_bblock(jb + 1)
            for sub in range(WBLK // 128):
                jc = jb * (WBLK // 128) + sub
                so = sub * 128
                for tc_ in range(2):
                    tsl = slice(512 * tc_, 512 * (tc_ + 1))
                    par = cntB % 2
                    cntB += 1
                    b2, b3 = (2, 3) if par == 0 else (4, 5)
                    m1_, m2_ = mtmp[par]
                    pn = "_%d" % par
                    mm_group(PS[b2][:, :], lambda k, sl=sl, so=so: wb4[sl][:, 0, k, so:so + 128], lambda k, tsl=tsl: YT3[:, k, tsl], 8, [wn, "YT"], "ps%d" % b2)
                    mm_group(PS[b3][:, :], lambda k, sl=sl, so=so: wb4[sl][:, 1, k, so:so + 128], lambda k, tsl=tsl: YT3[:, 8 + k, tsl], 8, [wn, "YT"], "ps%d" % b3)
                    s.add("dve", lambda e, b2=b2, m1_=m1_, jc=jc, tsl=tsl: e.tensor_tensor(out=m1_, in0=PS[b2][:, :], in1=G3[:, jc, tsl], op=ALU.mult),
                          reads=["ps%d" % b2, "gh%d" % jc], writes=["m1" + pn])
                    s.add("dve", lambda e, b3=b3, m2_=m2_, jc=jc, tsl=tsl: e.tensor_tensor(out=m2_, in0=PS[b3][:, :], in1=G3[:, 16 + jc, tsl], op=ALU.mult),
                          reads=["ps%d" % b3, "ga%d" % jc], writes=["m2" + pn])
                    s.add("pool", lambda e, jc=jc, tsl=tsl, m1_=m1_, m2_=m2_: e.tensor_tensor(out=mT3[:, jc, tsl], in0=m1_, in1=m2_, op=ALU.add),
                          reads=["m1" + pn, "m2" + pn], writes=["gh%d" % jc])
        s.barrier()
        if dbg == "p4b":
            A.off = p4_mark
            dump(mT[:, 0:8192], 8192, "mT")
            s.barrier()
    if upto("p4"):
        A.off = mt_end
        x1 = A.f32(8 * D)
        x13 = x1.rearrange("p (t d) -> p t d", t=8)
        wo = [A.bf16(16 * 512) for _ in range(2)]
        wo3 = [w.rearrange("p (k n) -> p k n", k=16) for w in wo]
        xs3 = [A.bf16(D) for _ in range(2)]
        junk3 = A.bf16(D)
        gbc3 = A.f32(D)
        assert A.off <= h2T_off
        s.add("sp", lambda e: e.dma_start(out=gbc3, in_=g2_d.partition_broadcast(128)), writes=["gbc3"], dma="gbc")
        for t in range(8):
            s.add("sp", lambda e, t=t: e.dma_start(out=x13[:, t, :], in_=xo_v[t]), writes=["x1_%d" % t], dma="x1ld")
        wout_v = wout_d.rearrange("(k p) n -> p k n", p=128)
        for db in range(4):
            sl = db % 2
            wn = "wo%d" % sl
            s.add("pool", lambda e, sl=sl, db=db: e.dma_start(out=wo3[sl], in_=wout_v[:, :, 512 * db:512 * (db + 1)]), writes=[wn], dma=wn)
            for t in range(8):
                bank = 2 + (t % 2)
                mm_group(PS[bank][:, :], lambda k, t=t: mT3[:, k, 128 * t:128 * (t + 1)], lambda k, sl=sl: wo3[sl][:, k, :], 16, [wn, "mT"], "ps%d" % bank)
                dsl = slice(512 * db, 512 * (db + 1))
                s.add("dve", lambda e, t=t, dsl=dsl, bank=bank: e.tensor_tensor(out=x13[:, t, dsl], in0=PS[bank][:, :], in1=x13[:, t, dsl], op=ALU.add),
                      reads=["ps%d" % bank, "x1_%d" % t], writes=["x1_%d" % t])
        x1_v = x1_d.rearrange("(n p) d -> n p d", p=128)
        for t in range(8):
            s.add("sp", lambda e, t=t: e.dma_start(out=x1_v[t], in_=x13[:, t, :]), reads=["x1_%d" % t], writes=["x1d"], dma="x1st")
            sl = t % 2
            rms_scale(x13[:, t, :], "x1_%d" % t, xs3[sl], "fxs%d" % sl, gbc3, "gbc3", 48 + t, junk3)
            transposes_to(xs3[sl], "fxs%d" % sl, h2T3, t * 128, "fhT")
        s.barrier()
        if dbg == "p4":
            A.off = mt_end + 8 * D * 4
            dump(x1[:, 0:8192], 8192, "x1_3")
            s.barrier()
    if upto("p5"):
        A.off = base_mark
        aT = A.bf16(NFF * 1024)
        aT3 = aT.rearrange("p (f t) -> p f t", f=NFF)
        p5_mark = A.off
        wgu = [A.bf16(2 * 16 * 256) for _ in range(2)]
        wgu4 = [w.rearrange("p (a k n) -> p a k n", a=2, k=16) for w in wgu]
        sgl = [A.f32(512) for _ in range(2)]
        assert A.off <= h2T_off
        wfg_v = wfg_d.rearrange("(k p) n -> p k n", p=128)
        wfu_v = wfu_d.rearrange("(k p) n -> p k n", p=128)
        for fb in range(DFF // 256):
            sl = fb % 2
            wn = "wgu%d" % sl
            c0 = fb * 256
            s.add("pool", lambda e, sl=sl, c0=c0: e.dma_start(out=wgu4[sl][:, 0, :, :], in_=wfg_v[:, :, c0:c0 + 256]), writes=[wn], dma=wn)
            s.add("pool", lambda e, sl=sl, c0=c0: e.dma_start(out=wgu4[sl][:, 1, :, :], in_=wfu_v[:, :, c0:c0 + 256]), writes=[wn], dma=wn)
            for sub in range(2):
                fc = fb * 2 + sub
                so = sub * 128
                for tc_ in range(2):
                    tsl = slice(512 * tc_, 512 * (tc_ + 1))
                    u = (fc * 2 + tc_) % 2
                    bg, bu = 2 + 2 * u, 3 + 2 * u
                    mm_group(PS[bg][:, :], lambda k, sl=sl, so=so: wgu4[sl][:, 0, k, so:so + 128], lambda k, tsl=tsl: h2T3[:, k, tsl], 16, [wn, "fhT_a", "fhT_d"], "ps%d" % bg)
                    mm_group(PS[bu][:, :], lambda k, sl=sl, so=so: wgu4[sl][:, 1, k, so:so + 128], lambda k, tsl=tsl: h2T3[:, k, tsl], 16, [wn, "fhT_a", "fhT_d"], "ps%d" % bu)
                    s.add("act", lambda e, u=u, bg=bg: e.activation(out=sgl[u], in_=PS[bg][:, :], func=AF.Silu), reads=["ps%d" % bg], writes=["sgl%d" % u])
                    s.add("dve", lambda e, u=u, bu=bu, fc=fc, tsl=tsl: e.tensor_tensor(out=aT3[:, fc, tsl], in0=PS[bu][:, :], in1=sgl[u], op=ALU.mult),
                          reads=["ps%d" % bu, "sgl%d" % u], writes=["aT"])
        s.barrier()
        A.off = p5_mark
        wd = [A.bf16(NFF * 512) for _ in range(2)]
        wd3 = [w.rearrange("p (f n) -> p f n", f=NFF) for w in wd]
        xr = [A.f32(512) for _ in range(2)]
        ot = [A.f32(512) for _ in range(2)]
        wfd_v = wfd_d.rearrange("(f p) n -> p f n", p=128)
        out_v = out_d.rearrange("(n p) d -> n p d", p=128)
        cnt = 0
        for db in range(4):
            sl = db % 2
            wn = "wd%d" % sl
            for f0 in (0, 22):
                s.add("pool", lambda e, sl=sl, f0=f0, db=db: e.dma_start(out=wd3[sl][:, f0:f0 + 22, :], in_=wfd_v[:, f0:f0 + 22, 512 * db:512 * (db + 1)]), writes=[wn], dma=wn)
            for t in range(8):
                bank = 2 + (t % 2)
                u = cnt % 2
                cnt += 1
                dsl = slice(512 * db, 512 * (db + 1))
                s.add("sp", lambda e, t=t, dsl=dsl, u=u: e.dma_start(out=xr[u], in_=x1_v[t][:, dsl]), reads=["x1d"], writes=["xr%d" % u], dma="xr%d" % u)
                mm_group(PS[bank][:, :], lambda k, t=t: aT3[:, k, 128 * t:128 * (t + 1)], lambda k, sl=sl: wd3[sl][:, k, :], NFF, [wn, "aT"], "ps%d" % bank)
                s.add("dve", lambda e, u=u, bank=bank: e.tensor_tensor(out=ot[u], in0=PS[bank][:, :], in1=xr[u], op=ALU.add),
                      reads=["ps%d" % bank, "xr%d" % u], writes=["ot%d" % u])
                s.add("sp", lambda e, t=t, dsl=dsl, u=u: e.dma_start(out=out_v[t][:, dsl], in_=ot[u]), reads=["ot%d" % u], writes=["outd"], dma="ot%d" % u)
    s.barrier()

    dma_keys = list(s.dma_cnt.keys())
    eng_sems = {e: nc.alloc_semaphore("sem_" + e) for e in ENGS}
    dma_sems = {k: nc.alloc_semaphore("dsem_%d" % i) for i, k in enumerate(dma_keys)}
    with nc.Block() as block:
        s.emit_all(block, eng_sems, dma_sems)
    return nc


_CONST_CACHE = {}


def _constants():
    if _CONST_CACHE:
        return _CONST_CACHE
    bf = ml_dtypes.bfloat16
    c = {}
    c["identb"] = np.eye(128, dtype=np.float32).astype(bf)
    c["identf"] = np.eye(128, dtype=np.float32)
    c["onesdiv"] = np.full((128, 128), 1.0 / 128.0, dtype=np.float32).astype(bf)
    perm = np.zeros((128, 128), dtype=np.float32)
    for d in range(128):
        partner = d + 32 if (d % 64) < 32 else d - 32
        perm[partner, d] = 1.0
    c["perm"] = perm.astype(bf)
    inv = (10000.0 ** (-np.arange(0, 64, 2, dtype=np.float32) / 64.0)).astype(np.float32)
    pos = np.arange(64, dtype=np.float32)
    ropec = np.zeros((128, 64), dtype=np.float32)
    ropes = np.zeros((128, 64), dtype=np.float32)
    for d in range(128):
        ang = (pos * inv[d % 32]).astype(np.float32)
        ropec[d] = np.cos(ang)
        sgn = -1.0 if (d % 64) < 32 else 1.0
        ropes[d] = sgn * np.sin(ang)
    c["ropec"] = ropec
    c["ropes"] = ropes
    L = S
    posf = np.arange(L, dtype=np.float32)
    t = (posf / np.float32(L - 1)).astype(np.float32)
    fb = np.linspace(1e-4, 15, 16, dtype=np.float32)
    ang = ((np.float32(2.0 * math.pi) * posf / np.float32(L))[:, None] * fb[None, :]).astype(np.float32)
    zemb = np.concatenate([t[:, None], np.cos(ang), -np.sin(ang)], axis=-1).astype(np.float32)
    c["zemb"] = np.ascontiguousarray(zemb.T)
    max_decay = math.log(1e-2) / 0.3
    min_decay = math.log(1e-2) / 1.5
    deltas = np.abs(np.linspace(min_decay, max_decay, 1024, dtype=np.float32))
    c["decay_full"] = np.exp(-t[:, None] * deltas[None, :]).astype(np.float32)
    m = np.arange(S, dtype=np.int64)[:, None]
    f = np.arange(S, dtype=np.int64)[None, :]
    ph = (m * (2 * f + 1)) % (2 * 8192)
    th = ph.astype(np.float64) * (2.0 * math.pi / (2 * 8192))
    C = np.cos(th).astype(np.float32).astype(bf)
    Sn = np.sin(th).astype(np.float32).astype(bf)
    def fwd_layout(M):
        return M.reshape(32, 128, 32, 128).transpose(2, 1, 0, 3)
    def inv_layout(M):
        return M.reshape(32, 128, 32, 128).transpose(0, 3, 2, 1)
    c["dftF"] = np.ascontiguousarray(np.stack([fwd_layout(C), fwd_layout(Sn)], axis=2)).reshape(32, 128, 2 * 32 * 128)
    c["dftI"] = np.ascontiguousarray(np.stack([inv_layout(C), inv_layout(Sn)], axis=2)).reshape(32, 128, 2 * 32 * 128)
    _CONST_CACHE.update(c)
    return _CONST_CACHE


def _chunkcols(v):
    v = np.asarray(v, dtype=np.float32)
    return np.ascontiguousarray(v.reshape(-1, 128).T)


def _core_inputs(inp, b, r):
    c = _constants()
    f32 = np.float32
    w_in = inp["w_in"][0]
    hsl = lambda base: slice(base + 256 * r, base + 256 * (r + 1))
    kvh = r // 2
    cols = np.concatenate([
        np.arange(0 + 256 * r, 0 + 256 * (r + 1)),
        np.arange(1024 + 256 * r, 1024 + 256 * (r + 1)),
        np.arange(2048 + 256 * r, 2048 + 256 * (r + 1)),
        np.arange(3072 + 256 * r, 3072 + 256 * (r + 1)),
        np.arange(4096 + 128 * kvh, 4096 + 128 * (kvh + 1)),
        np.arange(4352 + 128 * kvh, 4352 + 128 * (kvh + 1)),
    ])
    m = {}
    m["x"] = np.ascontiguousarray(inp["x"][b])
    m["xo"] = np.ascontiguousarray(inp["x"][b, 1024 * r:1024 * (r + 1)])
    m["wmix"] = np.ascontiguousarray(w_in[:, cols])
    m["wgate"] = np.ascontiguousarray(w_in[:, 4608:8704])
    m["bgate"] = _chunkcols(inp["b_gate"][0])
    m["g1"] = np.ascontiguousarray(inp["mix_norm_g"][0])
    cw = inp["hy_conv_w"][0]
    cb = inp["hy_conv_b"][0]
    convw = np.zeros((128, 18), dtype=f32)
    convb = np.zeros((128, 6), dtype=f32)
    for cc in range(6):
        base = (cc // 2) * 1024 + 256 * r + (cc % 2) * 128
        for j in range(3):
            convw[:, cc * 3 + j] = cw[j, base:base + 128]
        convb[:, cc] = cb[base:base + 128]
    m["convw"] = convw
    m["convb"] = convb
    m["fw1"] = np.ascontiguousarray(inp["flt_w1"][0])
    m["fw2"] = np.ascontiguousarray(inp["flt_w2"][0])
    m["fw3"] = np.ascontiguousarray(inp["flt_w3"][0])
    w4 = inp["flt_w4"][0]
    m["fw4"] = np.ascontiguousarray(np.concatenate([w4[:, hsl(0)], w4[:, hsl(1024)]], axis=1))
    m["fvec"] = np.ascontiguousarray(np.stack([inp["flt_freq"][0], inp["flt_b1"][0], inp["flt_b2"][0], inp["flt_b3"][0]], axis=1))
    m["hybias"] = _chunkcols(inp["hy_bias"][0][hsl(0)])
    m["gqk"] = np.ascontiguousarray(np.stack([inp["q_norm_g"][0], inp["k_norm_g"][0]], axis=1))
    m["wbh"] = inp["w_br_hyena"][0]
    m["wba"] = inp["w_br_attn"][0]
    m["wout"] = inp["w_out"][0]
    m["g2"] = np.ascontiguousarray(inp["ffn_norm_g"][0])
    m["wfg"] = inp["w_ffn_gate"][0]
    m["wfu"] = inp["w_ffn_up"][0]
    m["wfd"] = inp["w_ffn_down"][0]
    for k in ("identb", "identf", "onesdiv", "perm", "ropec", "ropes", "zemb", "dftF", "dftI"):
        m[k] = c[k]
    m["decay"] = np.ascontiguousarray(c["decay_full"][:, 256 * r:256 * (r + 1)])
    out = {}
    for k, v in m.items():
        v = np.asarray(v)
        if k in _SHAPES and tuple(v.shape) != _SHAPES[k]:
            v = np.zeros(_SHAPES[k], dtype=v.dtype)
        out[k] = np.ascontiguousarray(v)
    return out


_NC_CACHE = {}


def kernel(**inputs):
    inp = {k: np.asarray(v) for k, v in inputs.items()}
    key = _DEBUG["stop"]
    if key not in _NC_CACHE:
        _NC_CACHE[key] = build_program()
    nc = _NC_CACHE[key]
    in_maps = [_core_inputs(inp, c // 4, c % 4) for c in range(8)]
    res = run_bass_kernel_spmd(nc, in_maps, core_ids=list(range(8)))
    if key is not None:
        return [r["dbg"] for r in res.results]
    out = np.zeros((2, S, D), dtype=np.float32)
    for c in range(8):
        b, r = c // 4, c % 4
        out[b, 1024 * r:1024 * (r + 1)] = res.results[c]["out"]
    return out
```
